# Optimizing a Trainium2 kernel written in Bass

```python
import jax, jax.numpy as jnp
from jax import lax
import numpy as np

D_MODEL = 2048
BATCH = 4
SEQ = 4096
DEPTH = 2

GRID_W = 64
CTX_LEN = 256
HEAD_DIM = 128
F_GROUPS = 4
F_W = F_GROUPS * HEAD_DIM
NA_HEADS = 8
NA_W = NA_HEADS * HEAD_DIM
NA_KH = 8
NA_KW = 16
C_GROUPS = 4
C_W = C_GROUPS * HEAD_DIM
CHUNK = 128
D_FF = 5632
CONV_W = 3
EPS = 1e-6
NEG_INF = -1e30

OFF_F = 0
OFF_Q = OFF_F + F_W
OFF_K = OFF_Q + NA_W
OFF_V = OFF_K + NA_W
OFF_C = OFF_V + NA_W
OFF_GATE = OFF_C + 2 * C_W
IN_W = OFF_GATE + 3 * D_MODEL

kernel_name = "hybrid_fourier_natten_gmlp_dit_block"


def rms_norm(x, w):
    x32 = x.astype(jnp.float32)
    y = x32 * lax.rsqrt(jnp.mean(x32 * x32, axis=-1, keepdims=True) + EPS)
    return (y * w.astype(jnp.float32)).astype(x.dtype)


def modulate(h, shift, scale):
    return h * (1 + scale) + shift


def split_proj(p):
    return (p[..., OFF_F:OFF_Q], p[..., OFF_Q:OFF_K], p[..., OFF_K:OFF_V],
            p[..., OFF_V:OFF_C], p[..., OFF_C:OFF_GATE], p[..., OFF_GATE:])


def fourier_mix(xf):
    b, n, _ = xf.shape
    xg = xf.astype(jnp.float32).reshape(b, n, F_GROUPS, HEAD_DIM)
    y = jnp.fft.fftn(xg, axes=(1, 3), norm="ortho").real
    return y.reshape(b, n, F_W).astype(xf.dtype)


def spatial_gating(z, norm_w, ws, bs):
    b, n, _ = z.shape
    z = jax.nn.gelu(z, approximate=False)
    u, v = jnp.split(z, 2, axis=-1)
    v32 = v.astype(jnp.float32).reshape(b, n, C_GROUPS, HEAD_DIM)
    v32 = (v32 - v32.mean(-1, keepdims=True)) * lax.rsqrt(v32.var(-1, keepdims=True) + EPS)
    vn = (v32.reshape(b, n, C_W) * norm_w.astype(jnp.float32)).astype(z.dtype)
    vc = vn.reshape(b, n // CHUNK, CHUNK, C_GROUPS, HEAD_DIM)
    sp = jnp.einsum('gij,bcjgd->bcigd', ws, vc) + bs.T[:, :, None]
    return u * sp.reshape(b, n, C_W)


def depthwise_conv(x, w, bias):
    n = x.shape[1]
    pad = CONV_W // 2
    xp = jnp.pad(x, ((0, 0), (pad, pad), (0, 0)))
    y = bias
    for tap in range(CONV_W):
        y = y + xp[:, tap:tap + n] * w[tap]
    return y


def conv_ffn(h, w_up, conv_w, conv_b, w_down):
    a, gv = jnp.split(h @ w_up, 2, axis=-1)
    a = depthwise_conv(a, conv_w, conv_b)
    return (jax.nn.silu(a) * gv) @ w_down


def context_attention(q, k, v):
    b, l, _ = q.shape
    qh = q.reshape(b, l, NA_HEADS, HEAD_DIM)
    kh = k.reshape(b, l, NA_HEADS, HEAD_DIM)
    vh = v.reshape(b, l, NA_HEADS, HEAD_DIM)
    s = jnp.einsum('bqhd,bkhd->bhqk', qh, kh, preferred_element_type=jnp.float32) * (HEAD_DIM ** -0.5)
    p = jax.nn.softmax(s, axis=-1).astype(v.dtype)
    return jnp.einsum('bhqk,bkhd->bqhd', p, vh).reshape(b, l, NA_W)


def neighborhood_attention(q, k, v, k_ctx, v_ctx, rpb):
    b, s, _ = q.shape
    rows = s // GRID_W
    kh = min(NA_KH, rows)
    scale = HEAD_DIM ** -0.5
    qg = q.reshape(b, rows, GRID_W, NA_HEADS, HEAD_DIM)
    kg = k.reshape(b, rows, GRID_W, NA_HEADS, HEAD_DIM)
    vg = v.reshape(b, rows, GRID_W, NA_HEADS, HEAD_DIM)
    kc = k_ctx.reshape(b, -1, NA_HEADS, HEAD_DIM)
    vc = v_ctx.reshape(b, -1, NA_HEADS, HEAD_DIM)
    cols = np.arange(GRID_W)
    c_start = np.clip(cols - NA_KW // 2, 0, GRID_W - NA_KW)
    col_in = (cols[None, :] >= c_start[:, None]) & (cols[None, :] < c_start[:, None] + NA_KW)
    dc_idx = np.clip(cols[None, :] - cols[:, None] + NA_KW - 1, 0, 2 * NA_KW - 2)

    def row_block(r):
        r_start = jnp.clip(r - kh // 2, 0, rows - kh)
        k_blk = lax.dynamic_slice_in_dim(kg, r_start, kh, axis=1)
        v_blk = lax.dynamic_slice_in_dim(vg, r_start, kh, axis=1)
        q_row = lax.dynamic_index_in_dim(qg, r, axis=1, keepdims=False)
        s_win = jnp.einsum('bqhd,bkwhd->bhqkw', q_row, k_blk,
                           preferred_element_type=jnp.float32) * scale
        dr_idx = r_start + jnp.arange(kh) - r + NA_KH - 1
        bias = rpb[:, dr_idx[None, :, None], dc_idx[:, None, :]]
        s_win = jnp.where(col_in[:, None, :], s_win + bias.astype(jnp.float32), NEG_INF)
        s_ctx = jnp.einsum('bqhd,blhd->bhql', q_row, kc,
                           preferred_element_type=jnp.float32) * scale
        scores = jnp.concatenate([s_win.reshape(b, NA_HEADS, GRID_W, kh * GRID_W), s_ctx], axis=-1)
        p = jax.nn.softmax(scores, axis=-1).astype(v.dtype)
        p_win = p[..., :kh * GRID_W].reshape(b, NA_HEADS, GRID_W, kh, GRID_W)
        p_ctx = p[..., kh * GRID_W:]
        return (jnp.einsum('bhqkw,bkwhd->bqhd', p_win, v_blk)
                + jnp.einsum('bhql,blhd->bqhd', p_ctx, vc))

    out = lax.map(row_block, jnp.arange(rows))
    return jnp.moveaxis(out, 0, 1).reshape(b, s, NA_W)


def gated_merge(f, att, z, gates, norm_w, ws, bs, w_f_out, w_na_out, w_c_out, w_o):
    a_f = fourier_mix(f) @ w_f_out
    a_na = att @ w_na_out
    a_c = spatial_gating(z, norm_w, ws, bs) @ w_c_out
    g_f, g_na, g_c = jnp.split(jax.nn.sigmoid(gates), 3, axis=-1)
    return (g_f * a_f + g_na * a_na + g_c * a_c) @ w_o


def setup_inputs(seed: int = 0) -> dict:
    key = jax.random.key(seed)
    ks = jax.random.split(key, 24)
    f32 = jnp.float32
    nrm = lambda k, shape, s: jax.random.normal(k, shape, f32) * s
    d = D_MODEL
    return {
        "x": nrm(ks[0], (BATCH, SEQ, d), 1.0),
        "c": nrm(ks[1], (BATCH, d), 1.0),
        "ctx": nrm(ks[2], (BATCH, CTX_LEN, d), 1.0),
        "c_ctx": nrm(ks[3], (d,), 1.0),
        "ada_w": nrm(ks[4], (DEPTH, d, 6 * d), 0.5 * d ** -0.5),
        "ada_b": nrm(ks[5], (DEPTH, 6 * d), 0.01),
        "norm1_w": 1.0 + nrm(ks[6], (DEPTH, d), 0.02),
        "norm2_w": 1.0 + nrm(ks[7], (DEPTH, d), 0.02),
        "w_in": nrm(ks[8], (DEPTH, d, IN_W), d ** -0.5),
        "na_rpb": nrm(ks[9], (DEPTH, NA_HEADS, 2 * NA_KH - 1, 2 * NA_KW - 1), 0.1),
        "gmlp_norm_w": 1.0 + nrm(ks[10], (DEPTH, C_W), 0.02),
        "gmlp_ws": nrm(ks[11], (DEPTH, C_GROUPS, CHUNK, CHUNK), CHUNK ** -0.5),
        "gmlp_bs": 1.0 + nrm(ks[12], (DEPTH, C_GROUPS, CHUNK), 0.02),
        "w_f_out": nrm(ks[13], (DEPTH, F_W, d), F_W ** -0.5),
        "w_na_out": nrm(ks[14], (DEPTH, NA_W, d), NA_W ** -0.5),
        "w_c_out": nrm(ks[15], (DEPTH, C_W, d), C_W ** -0.5),
        "w_o": nrm(ks[16], (DEPTH, d, d), d ** -0.5),
        "ffn_up": nrm(ks[17], (DEPTH, d, 2 * D_FF), d ** -0.5),
        "ffn_conv_w": nrm(ks[18], (DEPTH, CONV_W, D_FF), CONV_W ** -0.5),
        "ffn_conv_b": nrm(ks[19], (DEPTH, D_FF), 0.01),
        "ffn_down": nrm(ks[20], (DEPTH, D_FF, d), D_FF ** -0.5),
        "final_norm_w": 1.0 + nrm(ks[21], (d,), 0.02),
    }


def reference(x, c, ctx, c_ctx, ada_w, ada_b, norm1_w, norm2_w, w_in, na_rpb, gmlp_norm_w,
              gmlp_ws, gmlp_bs, w_f_out, w_na_out, w_c_out, w_o, ffn_up, ffn_conv_w, ffn_conv_b,
              ffn_down, final_norm_w):
    xc = ctx
    silu_c = jax.nn.silu(c)
    silu_cc = jax.nn.silu(c_ctx)
    for l in range(DEPTH):
        last = l == DEPTH - 1
        mod_lat = (silu_c @ ada_w[l] + ada_b[l])[:, None, :]
        mod_ctx = silu_cc @ ada_w[l] + ada_b[l]
        sh1, sc1, g1, sh2, sc2, g2 = jnp.split(mod_lat, 6, axis=-1)
        csh1, csc1, cg1, csh2, csc2, cg2 = jnp.split(mod_ctx, 6, axis=-1)
        mix_w = (gmlp_norm_w[l], gmlp_ws[l], gmlp_bs[l], w_f_out[l], w_na_out[l], w_c_out[l], w_o[l])
        ffn_w = (ffn_up[l], ffn_conv_w[l], ffn_conv_b[l], ffn_down[l])

        hc = modulate(rms_norm(xc, norm1_w[l]), csh1, csc1)
        if last:
            k_ctx, v_ctx = jnp.split(hc @ w_in[l][:, OFF_K:OFF_C], 2, axis=-1)
        else:
            f_c, q_c, k_ctx, v_ctx, z_c, gate_c = split_proj(hc @ w_in[l])
            att_c = context_attention(q_c, k_ctx, v_ctx)
            xc = xc + cg1 * gated_merge(f_c, att_c, z_c, gate_c, *mix_w)
            hc2 = modulate(rms_norm(xc, norm2_w[l]), csh2, csc2)
            xc = xc + cg2 * conv_ffn(hc2, *ffn_w)

        h = modulate(rms_norm(x, norm1_w[l]), sh1, sc1)
        f, q, k, v, z, gate = split_proj(h @ w_in[l])
        att = neighborhood_attention(q, k, v, k_ctx, v_ctx, na_rpb[l])
        x = x + g1 * gated_merge(f, att, z, gate, *mix_w)
        h2 = modulate(rms_norm(x, norm2_w[l]), sh2, sc2)
        x = x + g2 * conv_ffn(h2, *ffn_w)
    return rms_norm(x, final_norm_w)
```

```python
import numpy as np
import ml_dtypes
from contextlib import ExitStack
import concourse.bass as bass
import concourse.mybir as mybir
from concourse.bass_utils import run_bass_kernel_spmd

F32 = mybir.dt.float32
BF16 = mybir.dt.bfloat16
AF = mybir.ActivationFunctionType
ALU = mybir.AluOpType
AX = mybir.AxisListType

D = 2048
KC = 16
SEQ = 4096
NCTX = 256
TT = SEQ + NCTX
IN_W = 10752
DFF = 5632
JF = 44
EPS = 1e-6
DEPTH = 2
NEG = -30000.0


class Res:
    __slots__ = ("name", "writers", "readers")

    def __init__(self, name):
        self.name = name
        self.writers = []
        self.readers = []


class Op:
    __slots__ = ("eng", "fn", "deps", "is_dma", "slot", "needed", "count")


class Sched:
    ENG = ("pe", "act", "dve", "pool", "sp")
    ATTR = {"pe": "tensor", "act": "scalar", "dve": "vector", "pool": "gpsimd", "sp": "sync"}

    def __init__(self, nc, stack):
        self.nc = nc
        self.stack = stack
        self.seg = {e: [] for e in self.ENG}
        self.segall = []
        self.resources = []
        self.esem = {e: stack.enter_context(nc.semaphore("s_" + e)) for e in self.ENG}
        self.ecount = {e: 0 for e in self.ENG}
        self.waited = {e: {} for e in self.ENG}
        self.slot_of = {}
        self.free_slots = []
        self.all_slots = []
        self.pending = {e: [] for e in self.ENG}
        self.last = {e: None for e in self.ENG}
        self.n_ops = 0

    def res(self, name="r"):
        r = Res(name)
        self.resources.append(r)
        return r

    def op(self, eng, fn, reads=(), writes=(), pw=(), dma=None):
        o = Op()
        o.eng = eng; o.fn = fn; o.is_dma = dma is not None; o.slot = dma
        o.needed = False; o.count = None
        deps = list(self.pending[eng])
        self.pending[eng] = []
        for r in reads:
            deps.extend(r.writers)
        for w in writes:
            deps.extend(w.writers)
            deps.extend(w.readers)
        for w in pw:
            deps.extend(w.readers)
            if w.writers:
                deps.append(w.writers[0])
        o.deps = deps
        for r in reads:
            r.readers.append(o)
        for w in writes:
            w.writers = [o]
            w.readers = []
        for w in pw:
            w.writers.append(o)
        self.segall.append(o)
        self.seg[eng].append(o)
        if not o.is_dma:
            self.last[eng] = o
        self.n_ops += 1
        return o

    def _slot(self, key):
        s = self.slot_of.get(id(key))
        if s is None:
            if self.free_slots:
                s = self.free_slots.pop()
            else:
                s = [self.stack.enter_context(self.nc.semaphore("d%d" % len(self.all_slots))), 0, None]
                self.all_slots.append(s)
            self.slot_of[id(key)] = s
        return s

    def barrier(self):
        self.flush(barrier=True)

    def flush(self, barrier=False, final=False):
        nc = self.nc
        bar = []
        if barrier or final:
            for e in self.ENG:
                if self.last[e] is not None:
                    self.last[e].needed = True
        for o in self.segall:
            for d in o.deps:
                if d.is_dma:
                    continue
                if d.eng == o.eng and o.eng == "pe" and not o.is_dma:
                    continue
                d.needed = True
        for o in self.segall:
            if o.count is not None:
                continue
            if o.is_dma:
                s = self._slot(o.slot)
                s[1] += 16
                s[2] = o
                o.count = (s[0], s[1])
            elif o.needed:
                self.ecount[o.eng] += 1
                o.count = (self.esem[o.eng], self.ecount[o.eng])
        segs = self.seg
        extra = []
        if barrier or final:
            for e in self.ENG:
                if self.last[e] is not None and self.last[e].count is not None:
                    extra.append(self.last[e].count)
            for s in self.all_slots:
                if s[1] > 0:
                    extra.append((s[0], s[1]))

        def emit(ename):
            def body(eng):
                waited = self.waited[ename]
                for o in segs[ename]:
                    for d in o.deps:
                        c = d.count
                        if c is None:
                            continue
                        if d.eng == ename and ename == "pe" and not d.is_dma and not o.is_dma:
                            continue
                        if waited.get(id(c[0]), 0) >= c[1]:
                            continue
                        waited[id(c[0])] = c[1]
                        eng.wait_ge(c[0], c[1])
                    ins = o.fn(eng)
                    if o.count is not None:
                        ins.then_inc(o.count[0], 16 if o.is_dma else 1)
                for sem, val in extra:
                    if waited.get(id(sem), 0) < val:
                        waited[id(sem)] = val
                        eng.wait_ge(sem, val)
            return body
        with nc.Block() as block:
            for e in self.ENG:
                getattr(block, self.ATTR[e])(emit(e))
        self.seg = {e: [] for e in self.ENG}
        self.segall = []
        if barrier or final:
            for r in self.resources:
                r.writers = []
                r.readers = []
            self.resources = []
            self.slot_of = {}
            self.free_slots = list(self.all_slots)
            self.last = {e: None for e in self.ENG}


class B:
    def __init__(self, dbg=False):
        self.dbg = dbg
        self.nc = bass.Bass("TRN2", target_bir_lowering=False)
        self.st = ExitStack()
        self.S = None
        self.ins = {}
        self.psum = None
        self.psi = 0

    def inp(self, name, shape, dt=F32):
        t = self.nc.dram_tensor(name, list(shape), dt, kind="ExternalInput").ap()
        self.ins[name] = t
        return t

    def scratch(self, name, shape, dt):
        kind = "ExternalOutput" if (self.dbg and name in self.dbg) else "Internal"
        return self.nc.dram_tensor(name, list(shape), dt, kind=kind).ap()

    def sb(self, stack, name, shape, dt):
        self.uid = getattr(self, "uid", 0) + 1
        return stack.enter_context(self.nc.sbuf_tensor("%s_%d" % (name, self.uid), list(shape), dt))

    def ps(self):
        i = self.psi
        self.psi = (self.psi + 1) % len(self.psum)
        return self.psum[i]

    def dma(self, q, out, in_, slot, reads=(), writes=(), pw=(), slow=False):
        if slow:
            return self.S.op(q, lambda e: e.dma_start(out=out, in_=in_, allow_slow_non_contiguous=True), reads=reads, writes=writes, pw=pw, dma=slot)
        return self.S.op(q, lambda e: e.dma_start(out=out, in_=in_), reads=reads, writes=writes, pw=pw, dma=slot)


def build(dbg=None, n_layers=DEPTH):
    b = B(dbg=dbg or ())
    b.n_layers = n_layers
    nc = b.nc
    st = b.st
    with st:
        _build(b)
    return b


def _build(b):
    nc, st = b.nc, b.st
    S = b.S = Sched(nc, st)
    inp = b.inp

    xT_in = inp("xT", [KC, 128, TT])
    cT = inp("cT", [128, KC, 2])
    ada_w = inp("ada_w", [DEPTH, D, 6 * D])
    ada_b = inp("ada_bT", [DEPTH, 128, 96, 2])
    nw1 = inp("nw1", [DEPTH, 128, KC, 2])
    nw2 = inp("nw2", [DEPTH, 128, KC, 2])
    fnw = inp("fnw", [128, KC])
    w_in = inp("w_in", [DEPTH, D, IN_W])
    rpbt = inp("rpbt", [DEPTH, 8, 128, 8, 512], BF16)
    maskt = inp("maskt", [128, 8, 8, 512], BF16)
    flags_in = inp("flags", [128, 2])
    gnw = inp("gnw", [DEPTH, 128, 4])
    wsT = inp("wsT", [DEPTH, 128, 4, 128])
    bsb = inp("bsb", [DEPTH, 128, 4, 128])
    w_f_out = inp("w_f_out", [DEPTH, 512, D])
    w_na_out = inp("w_na_out", [DEPTH, 1024, D])
    w_c_out = inp("w_c_out", [DEPTH, 512, D])
    w_o = inp("w_o", [DEPTH, D, D])
    ffn_up = inp("ffn_up", [DEPTH, D, 2 * DFF])
    cw = inp("cw", [DEPTH, 128, JF, 3])
    cb = inp("cb", [DEPTH, 128, JF])
    ffn_down = inp("ffn_down", [DEPTH, DFF, D])
    cdsd = inp("cdsd", [128, 256], BF16)
    cn = inp("cn", [SEQ, SEQ], BF16)
    sn = inp("sn", [SEQ, SEQ], BF16)
    cnc = inp("cnc", [NCTX, NCTX], BF16)
    snc = inp("snc", [NCTX, NCTX], BF16)
    ident_in = inp("ident", [128, 128], BF16)
    outT = nc.dram_tensor("outT", [KC, 128, 2048], F32, kind="ExternalOutput").ap()

    sc = b.scratch
    xT_mid = sc("xT_mid", [KC, 128, TT], F32)
    xT_nxt = sc("xT_nxt", [KC, 128, TT], F32)
    xT_fin = sc("xT_fin", [KC, 128, TT], F32)
    xcs = sc("xcs", [TT, 4, 256], BF16)
    qT = sc("qT", [8, 128, TT], BF16)
    kT = sc("kT", [8, 128, TT], BF16)
    Vd = sc("Vd", [TT, 1024], BF16)
    uT = sc("uT", [4, 128, TT], BF16)
    spT = sc("spT", [4, 128, TT], BF16)
    gT = sc("gT", [48, 128, TT], BF16)
    yT = sc("yT", [4, 128, TT], BF16)
    attT = sc("attT", [8, 128, TT], BF16)
    hmT = sc("hmT", [JF, 128, TT], BF16)

    sb = b.sb
    ident = sb(st, "ident_sb", [128, 128], BF16)
    ones_f = sb(st, "ones_f", [128, 128], F32)
    ones_b = sb(st, "ones_b", [128, 128], BF16)
    scT = sb(st, "scT", [128, KC, 2], F32)
    modT = sb(st, "modT", [128, 96, 2], F32)
    A1 = sb(st, "A1", [128, KC, 2], F32)
    A2 = sb(st, "A2", [128, KC, 2], F32)
    fnw_sb = sb(st, "fnw_sb", [128, KC], F32)
    ones16 = sb(st, "ones16", [128, KC], F32)
    zero16 = sb(st, "zero16", [128, KC], F32)
    flg = sb(st, "flg", [128, 2], F32)
    b.psum = [st.enter_context(nc.psum_tensor("psb%d" % i, [128, 512], F32)) for i in range(8)]
    r_psum = None

    def newres(n=1, name="r"):
        return [S.res(name) for _ in range(n)] if n > 1 else S.res(name)

    class Ring:
        def __init__(self, stack, name, shape, dt, n):
            self.t = [sb(stack, "%s%d" % (name, i), shape, dt) for i in range(n)]
            self.r = [None] * n
            self.i = 0
            self.n = n
            self.name = name

        def next(self):
            i = self.i
            self.i = (i + 1) % self.n
            if self.r[i] is None or self.r[i] not in S.resources:
                self.r[i] = S.res(self.name)
            return self.t[i], self.r[i]

    class PsRing:
        def __init__(self, banks):
            self.banks = banks
            self.r = [None] * len(banks)
            self.i = 0

        def next(self):
            i = self.i
            self.i = (i + 1) % len(self.banks)
            if self.r[i] is None or self.r[i] not in S.resources:
                self.r[i] = S.res("ps")
            return b.psum[self.banks[i]], self.r[i]

    PS = PsRing([0, 1, 2, 3])
    PSA = PsRing([4, 5, 6, 7])

    r_const = S.res("const")
    b.dma("sp", ident[:], ident_in, ident, writes=[r_const])
    S.op("dve", lambda e: e.memset(ones_f[:], 1.0), pw=[r_const])
    S.op("dve", lambda e: e.memset(ones_b[:], 1.0), pw=[r_const])
    S.op("dve", lambda e: e.memset(ones16[:], 1.0), pw=[r_const])
    S.op("dve", lambda e: e.memset(zero16[:], 0.0), pw=[r_const])
    b.dma("sp", scT[:], cT, scT, pw=[r_const])
    b.dma("sp", fnw_sb[:], fnw, fnw_sb, pw=[r_const])
    b.dma("sp", flg[:], flags_in, flg, pw=[r_const])
    S.barrier()
    r_c2 = S.res("c2")
    S.op("act", lambda e: e.activation(out=scT[:], in_=scT[:], func=AF.Silu), writes=[r_c2])
    S.barrier()

    def lat_tiles(lo, hi, step=512):
        return [(t, min(step, hi - t), 0) for t in range(lo, hi, step)]

    def norm_phase(pst, src, actA, r_act, tiles, Asc, Bsh, mul_only=False, dst=None):
        xr = Ring(pst, "n_x", [128, KC, 512], F32, 2)
        sq = Ring(pst, "n_sq", [128, 512], F32, 3)
        rs = Ring(pst, "n_rs", [128, 512], F32, 2)
        tm = Ring(pst, "n_tm", [128, 512], F32, 3)
        og = Ring(pst, "n_o", [128, KC, 512], F32, 1) if dst is not None else None
        for (c0, n, d0, sel) in tiles:
            xt, r_x = xr.next()
            b.dma("sp", xt[:, :, 0:n], src[:, :, c0:c0 + n].rearrange("k p t -> p k t"), xt, writes=[r_x], slow=(n == 1))
            pt, r_p = PS.next()
            for kc in range(KC):
                s_, r_s = sq.next()
                S.op("act", lambda e, s_=s_, xt=xt, kc=kc, n=n: e.activation(out=s_[:, 0:n], in_=xt[:, kc, 0:n], func=AF.Square),
                     reads=[r_x], writes=[r_s])
                S.op("pe", lambda e, pt=pt, s_=s_, kc=kc, n=n: e.matmul(pt[:, 0:n], lhsT=ones_f[:], rhs=s_[:, 0:n], start=(kc == 0), stop=(kc == KC - 1)),
                     reads=[r_s], writes=[r_p] if kc == 0 else (), pw=[r_p] if kc else ())
            rt, r_r = rs.next()
            S.op("dve", lambda e, rt=rt, pt=pt, n=n: e.tensor_scalar(out=rt[:, 0:n], in0=pt[:, 0:n], scalar1=1.0 / D, scalar2=EPS, op0=ALU.mult, op1=ALU.add),
                 reads=[r_p], writes=[r_r])
            S.op("act", lambda e, rt=rt, n=n: e.activation(out=rt[:, 0:n], in_=rt[:, 0:n], func=AF.Sqrt), reads=[r_r], writes=[r_r])
            S.op("dve", lambda e, rt=rt, n=n: e.reciprocal(out=rt[:, 0:n], in_=rt[:, 0:n]), reads=[r_r], writes=[r_r])
            if dst is not None:
                ot, r_o = og.next()
            for kc in range(KC):
                t_, r_t = tm.next()
                S.op("dve", lambda e, t_=t_, xt=xt, rt=rt, kc=kc, n=n: e.tensor_tensor(out=t_[:, 0:n], in0=xt[:, kc, 0:n], in1=rt[:, 0:n], op=ALU.mult),
                     reads=[r_x, r_r], writes=[r_t])
                if dst is None:
                    S.op("act", lambda e, t_=t_, kc=kc, n=n, d0=d0, sel=sel: e.activation(
                        out=actA[:, kc, d0:d0 + n], in_=t_[:, 0:n], func=AF.Identity,
                        bias=Bsh[:, kc, sel:sel + 1], scale=Asc[:, kc, sel:sel + 1]),
                        reads=[r_t], pw=[r_act])
                else:
                    S.op("act", lambda e, t_=t_, kc=kc, n=n, ot=ot: e.activation(
                        out=ot[:, kc, 0:n], in_=t_[:, 0:n], func=AF.Identity, scale=Asc[:, kc:kc + 1]),
                        reads=[r_t], pw=[r_o])
            if dst is not None:
                b.dma("sp", dst[:, :, d0:d0 + n].rearrange("k p t -> p k t"), ot[:, :, 0:n], ot, reads=[r_o])

    x_cur = xT_in
    for l in range(b.n_layers):
        last = (l == DEPTH - 1)
        with ExitStack() as pst:
            wr = Ring(pst, "ada_wb", [128, KC, 512], F32, 2)
            adab = sb(pst, "adab", [128, 96, 2], F32)
            n1 = sb(pst, "n1", [128, KC, 2], F32)
            n2 = sb(pst, "n2", [128, KC, 2], F32)
            r_ab = S.res("adab")
            b.dma("sp", adab[:], ada_b[l], adab, writes=[r_ab])
            b.dma("sp", n1[:], nw1[l], n1, pw=[r_ab])
            b.dma("sp", n2[:], nw2[l], n2, pw=[r_ab])
            pm, r_pm = PS.next()
            r_mod = S.res("mod")
            for blk in range(24):
                wt, r_w = wr.next()
                b.dma("sp", wt[:], ada_w[l][:, blk * 512:(blk + 1) * 512].rearrange("(k p) n -> p k n", p=128), wt, writes=[r_w])
                for fi in range(4):
                    fc = blk * 4 + fi
                    for kc in range(KC):
                        S.op("pe", lambda e, wt=wt, fi=fi, fc=fc, kc=kc: e.matmul(
                            pm[:, fc * 2:fc * 2 + 2], lhsT=wt[:, kc, fi * 128:(fi + 1) * 128], rhs=scT[:, kc, :],
                            start=(kc == 0), stop=(kc == KC - 1)), reads=[r_w], pw=[r_pm])
            S.op("dve", lambda e: e.tensor_tensor(out=modT[:].rearrange("p a b -> p (a b)"), in0=pm[:, 0:192],
                                                  in1=adab[:].rearrange("p a b -> p (a b)"), op=ALU.add),
                 reads=[r_pm, r_ab], writes=[r_mod])
            S.op("dve", lambda e: e.scalar_tensor_tensor(out=A1[:], in0=modT[:, 16:32, :], scalar=1.0, in1=n1[:], op0=ALU.add, op1=ALU.mult),
                 reads=[r_mod, r_ab], writes=[S.res()])
            S.op("dve", lambda e: e.scalar_tensor_tensor(out=A2[:], in0=modT[:, 64:80, :], scalar=1.0, in1=n2[:], op0=ALU.add, op1=ALU.mult),
                 reads=[r_mod, r_ab], writes=[S.res()])
            S.barrier()
        SH1 = modT[:, 0:16, :]
        G1 = modT[:, 32:48, :]
        SH2 = modT[:, 48:64, :]
        G2 = modT[:, 80:96, :]

        for grp in range(2):
            with ExitStack() as gst:
                actA = sb(gst, "actA", [128, KC, 2304], BF16)
                r_act = S.res("actA")
                if grp == 0:
                    tiles = [(t, 512, t, 0) for t in range(0, 2048, 512)] + [(SEQ, 256, 2048, 1)]
                else:
                    tiles = [(t, 512, t - 2048, 0) for t in range(2048, 4096, 512)]
                with ExitStack() as pst:
                    norm_phase(pst, x_cur, actA, r_act, tiles, A1, SH1)
                    S.barrier()
                r_act = S.res("actA")
                if grp == 0:
                    mt = [(t, 512, t, 0) for t in range(0, 2048, 512)] + [(2048, 256, SEQ, 1)]
                else:
                    mt = [(t, 512, t + 2048, 0) for t in range(0, 2048, 512)]
                with ExitStack() as pst:
                    wbr = Ring(pst, "p_wb", [128, KC, 256], BF16, 3)
                    stg = Ring(pst, "p_stg", [128, 512], BF16, 4)
                    fsb = Ring(pst, "p_fsb", [128, 512], BF16, 2)
                    xst = Ring(pst, "p_xst", [128, 2, 256], BF16, 3)
                    cd = sb(pst, "p_cd", [128, 256], BF16)
                    r_cd = S.res("cd")
                    b.dma("sp", cd[:], cdsd, cd, writes=[r_cd])
                    fcs = list(range(0, 20)) + list(range(28, 32)) + list(range(36, 84))
                    mt_halo = [(0, 128, 2048, 0), (1920, 128, 3968, 0)]
                    blocks = []
                    for fc in fcs:
                        if blocks and blocks[-1][0] // 2 == fc // 2 and len(blocks[-1]) < 2:
                            blocks[-1].append(fc)
                        else:
                            blocks.append([fc])

                    def load_blk(blk):
                        wt, r_w = wbr.next()
                        c0 = blk[0] * 128
                        nn = len(blk) * 128
                        b.dma("pool", wt[:, :, 0:nn], w_in[l][:, c0:c0 + nn].rearrange("(k p) n -> p k n", p=128), wt, writes=[r_w])
                        return wt, r_w
                    nxt = load_blk(blocks[0])
                    for bi, blk in enumerate(blocks):
                        wt, r_w = nxt
                        if bi + 1 < len(blocks):
                            nxt = load_blk(blocks[bi + 1])
                        for fi, fc in enumerate(blk):
                            fk = fc < 4 or 12 <= fc < 20
                            for (a0, n, g0, isctx) in (mt if (fk or not (last and grp == 1)) else mt_halo):
                                if last and isctx and not (12 <= fc < 20):
                                    continue
                                pt, r_p = PS.next()
                                for kc in range(KC):
                                    S.op("pe", lambda e, pt=pt, wt=wt, fi=fi, kc=kc, a0=a0, n=n: e.matmul(
                                        pt[:, 0:n], lhsT=wt[:, kc, fi * 128:(fi + 1) * 128], rhs=actA[:, kc, a0:a0 + n],
                                        start=(kc == 0), stop=(kc == KC - 1)),
                                        reads=[r_w, r_act], writes=[r_p] if kc == 0 else (), pw=[r_p] if kc else ())
                                if fc < 4:
                                    ft, r_f = fsb.next()
                                    S.op("act", lambda e, ft=ft, pt=pt, n=n: e.activation(out=ft[:, 0:n], in_=pt[:, 0:n], func=AF.Copy),
                                         reads=[r_p], writes=[r_f])
                                    nsub = n // 128
                                    for s0 in range(0, nsub, 2):
                                        k2 = min(2, nsub - s0)
                                        p2, r_p2 = PS.next()
                                        for a in range(k2):
                                            S.op("pe", lambda e, p2=p2, ft=ft, a=a, s0=s0: e.matmul(
                                                p2[:, a * 256:(a + 1) * 256], lhsT=ft[:, (s0 + a) * 128:(s0 + a + 1) * 128], rhs=cd[:], start=True, stop=True),
                                                reads=[r_f, r_cd], writes=[r_p2] if a == 0 else (), pw=[r_p2] if a else ())
                                        xs, r_xs = xst.next()
                                        S.op("dve", lambda e, xs=xs, p2=p2, k2=k2: e.tensor_copy(
                                            out=xs[:, 0:k2, :], in_=p2[:, 0:k2 * 256].rearrange("p (a c) -> p a c", a=k2)),
                                            reads=[r_p2], writes=[r_xs])
                                        t0 = g0 + s0 * 128
                                        b.dma("sp", xcs[t0:t0 + k2 * 128, fc, :].rearrange("(a p) c -> p a c", p=128), xs[:, 0:k2, :], xs, reads=[r_xs])
                                    continue
                                so, r_so = stg.next()
                                if fc < 12:
                                    S.op("act", lambda e, so=so, pt=pt, n=n: e.activation(out=so[:, 0:n], in_=pt[:, 0:n], func=AF.Copy, scale=128 ** -0.5),
                                         reads=[r_p], writes=[r_so])
                                    dstap = qT[fc - 4][:, g0:g0 + n]
                                elif fc < 20:
                                    S.op("dve", lambda e, so=so, pt=pt, n=n: e.tensor_copy(out=so[:, 0:n], in_=pt[:, 0:n]), reads=[r_p], writes=[r_so])
                                    dstap = kT[fc - 12][:, g0:g0 + n]
                                elif fc < 32:
                                    S.op("act", lambda e, so=so, pt=pt, n=n: e.activation(out=so[:, 0:n], in_=pt[:, 0:n], func=AF.Gelu), reads=[r_p], writes=[r_so])
                                    dstap = uT[fc - 28][:, g0:g0 + n]
                                else:
                                    S.op("act", lambda e, so=so, pt=pt, n=n: e.activation(out=so[:, 0:n], in_=pt[:, 0:n], func=AF.Sigmoid), reads=[r_p], writes=[r_so])
                                    dstap = gT[fc - 36][:, g0:g0 + n]
                                b.dma("sp", dstap, so[:, 0:n], so, reads=[r_so])
                    wvr = Ring(pst, "p_wv", [128, KC, 512], BF16, 2)
                    vst = Ring(pst, "p_vst", [128, 512], BF16, 3)
                    zg = Ring(pst, "p_zg", [128, 4, 128], F32, 2)
                    zc = Ring(pst, "p_zc", [128, 4, 128], F32, 2)
                    zq = Ring(pst, "p_zq", [128, 4, 128], F32, 2)
                    st4 = Ring(pst, "p_st4", [128, 8], F32, 4)
                    vh = Ring(pst, "p_vh", [128, 4, 128], BF16, 2)
                    spo = Ring(pst, "p_spo", [128, 4, 128], BF16, 2)
                    wst = sb(pst, "p_wst", [128, 4, 128], BF16)
                    bst = sb(pst, "p_bst", [128, 4, 128], F32)
                    gnt = sb(pst, "p_gnt", [128, 4], F32)
                    r_gc = S.res("gconst")
                    b.dma("pool", wst[:], wsT[l], wst, writes=[r_gc])
                    b.dma("sp", bst[:], bsb[l], bst, pw=[r_gc])
                    b.dma("sp", gnt[:], gnw[l], gnt, pw=[r_gc])
                    ntok = 2304 if grp == 0 else 2048
                    for sec, c0 in (("v0", 2560), ("v1", 3072), ("zv", 4096)):
                        wt, r_w = wvr.next()
                        b.dma("pool", wt[:], w_in[l][:, c0:c0 + 512].rearrange("(k p) n -> p k n", p=128), wt, writes=[r_w])
                        for tt in range(ntok // 128):
                            a0 = tt * 128
                            isctx = (grp == 0 and a0 >= 2048)
                            g0 = (SEQ + a0 - 2048) if isctx else (a0 + grp * 2048)
                            if sec == "zv" and last and (isctx or (grp == 1 and tt not in (0, 15))):
                                continue
                            pt, r_p = PS.next()
                            for kc in range(KC):
                                S.op("pe", lambda e, pt=pt, wt=wt, kc=kc, a0=a0: e.matmul(
                                    pt[:], lhsT=actA[:, kc, a0:a0 + 128], rhs=wt[:, kc, :], start=(kc == 0), stop=(kc == KC - 1)),
                                    reads=[r_w, r_act], writes=[r_p] if kc == 0 else (), pw=[r_p] if kc else ())
                            if sec != "zv":
                                so, r_so = vst.next()
                                S.op("dve" if tt % 2 else "act",
                                     (lambda e, so=so, pt=pt: e.tensor_copy(out=so[:], in_=pt[:])) if tt % 2 else
                                     (lambda e, so=so, pt=pt: e.activation(out=so[:], in_=pt[:], func=AF.Copy)),
                                     reads=[r_p], writes=[r_so])
                                vc0 = 0 if sec == "v0" else 512
                                b.dma("sp", Vd[g0:g0 + 128, vc0:vc0 + 512], so[:], so, reads=[r_so])
                                continue
                            z_, r_z = zg.next()
                            S.op("act", lambda e, z_=z_, pt=pt: e.activation(out=z_[:].rearrange("p g d -> p (g d)"), in_=pt[:], func=AF.Gelu),
                                 reads=[r_p], writes=[r_z])
                            s4, r_s4 = st4.next()
                            S.op("dve", lambda e, s4=s4, z_=z_: e.tensor_reduce(out=s4[:, 0:4], in_=z_[:], axis=AX.X, op=ALU.add),
                                 reads=[r_z], writes=[r_s4])
                            S.op("dve", lambda e, s4=s4: e.tensor_scalar(out=s4[:, 0:4], in0=s4[:, 0:4], scalar1=1.0 / 128, scalar2=0.0, op0=ALU.mult, op1=ALU.add),
                                 reads=[r_s4], writes=[r_s4])
                            c_, r_c = zc.next()
                            for g in range(4):
                                S.op("dve", lambda e, c_=c_, z_=z_, s4=s4, g=g: e.tensor_scalar(
                                    out=c_[:, g, :], in0=z_[:, g, :], scalar1=s4[:, g:g + 1], scalar2=0.0, op0=ALU.subtract, op1=ALU.add),
                                    reads=[r_z, r_s4], writes=[r_c] if g == 0 else (), pw=[r_c] if g else ())
                            q_, r_q = zq.next()
                            S.op("pool", lambda e, q_=q_, c_=c_: e.tensor_tensor(out=q_[:], in0=c_[:], in1=c_[:], op=ALU.mult), reads=[r_c], writes=[r_q])
                            S.op("dve", lambda e, s4=s4, q_=q_: e.tensor_reduce(out=s4[:, 4:8], in_=q_[:], axis=AX.X, op=ALU.add),
                                 reads=[r_q], writes=[r_s4])
                            S.op("dve", lambda e, s4=s4: e.tensor_scalar(out=s4[:, 4:8], in0=s4[:, 4:8], scalar1=1.0 / 128, scalar2=EPS, op0=ALU.mult, op1=ALU.add),
                                 reads=[r_s4], writes=[r_s4])
                            S.op("act", lambda e, s4=s4: e.activation(out=s4[:, 4:8], in_=s4[:, 4:8], func=AF.Sqrt), reads=[r_s4], writes=[r_s4])
                            S.op("dve", lambda e, s4=s4: e.reciprocal(out=s4[:, 4:8], in_=s4[:, 4:8]), reads=[r_s4], writes=[r_s4])
                            v_, r_v = vh.next()
                            for g in range(4):
                                S.op("dve", lambda e, v_=v_, c_=c_, s4=s4, g=g: e.tensor_scalar(
                                    out=v_[:, g, :], in0=c_[:, g, :], scalar1=s4[:, 4 + g:5 + g], scalar2=0.0, op0=ALU.mult, op1=ALU.add),
                                    reads=[r_c, r_s4], writes=[r_v] if g == 0 else (), pw=[r_v] if g else ())
                            p3, r_p3 = PS.next()
                            for g in range(4):
                                S.op("pe", lambda e, p3=p3, v_=v_, g=g: e.matmul(p3[:, g * 128:(g + 1) * 128], lhsT=v_[:, g, :], rhs=wst[:, g, :], start=True, stop=True),
                                     reads=[r_v, r_gc], writes=[r_p3] if g == 0 else (), pw=[r_p3] if g else ())
                            o_, r_o = spo.next()
                            for g in range(4):
                                S.op("dve", lambda e, o_=o_, p3=p3, g=g: e.scalar_tensor_tensor(
                                    out=o_[:, g, :], in0=p3[:, g * 128:(g + 1) * 128], scalar=gnt[:, g:g + 1], in1=bst[:, g, :], op0=ALU.mult, op1=ALU.add),
                                    reads=[r_p3, r_gc], writes=[r_o] if g == 0 else (), pw=[r_o] if g else ())
                            b.dma("sp", spT[:, :, g0:g0 + 128].rearrange("g p t -> p g t"), o_[:], o_, reads=[r_o])
                    S.barrier()

        with ExitStack() as pst:
            xa = sb(pst, "f_xa", [128, 32, 1024], BF16)
            r_xa = S.res("xa")
            for q4 in range(4):
                b.dma("sp", xa[:, q4 * 8:(q4 + 1) * 8, :],
                      xcs[q4 * 1024:(q4 + 1) * 1024].rearrange("(c p) g w -> p c (g w)", p=128), xa,
                      writes=[r_xa] if q4 == 0 else (), pw=[r_xa] if q4 else ())
            tcr = Ring(pst, "f_tc", [128, 8, 512], BF16, 3)
            tsr = Ring(pst, "f_ts", [128, 8, 512], BF16, 3)
            yst = Ring(pst, "f_y", [128, 512], BF16, 4)
            ftiles = [(t, 512) for t in range(0, SEQ if not last else 2048, 512)]
            if last:
                ftiles += [(2048, 128), (3968, 128)]
            for jt, (n0, nw) in enumerate(ftiles):
                banks = [(PSA if jt % 2 == 0 else PS).next() for _ in range(4)]
                for pc in range(4):
                    tc_, r_tc = tcr.next()
                    ts_, r_ts = tsr.next()
                    b.dma("sp", tc_[:, :, 0:nw], cn[pc * 1024:(pc + 1) * 1024, n0:n0 + nw].rearrange("(c p) n -> p c n", p=128), tc_, writes=[r_tc])
                    b.dma("sp", ts_[:, :, 0:nw], sn[pc * 1024:(pc + 1) * 1024, n0:n0 + nw].rearrange("(c p) n -> p c n", p=128), ts_, writes=[r_ts])
                    for g in range(4):
                        pt, r_p = banks[g]
                        for c8 in range(8):
                            ch = pc * 8 + c8
                            for cs, (tb, r_tb) in enumerate(((tc_, r_tc), (ts_, r_ts))):
                                first = (ch == 0 and cs == 0)
                                lastm = (ch == 31 and cs == 1)
                                S.op("pe", lambda e, pt=pt, ch=ch, g=g, cs=cs, tb=tb, c8=c8, first=first, lastm=lastm, nw=nw: e.matmul(
                                    pt[:, 0:nw], lhsT=xa[:, ch, g * 256 + cs * 128:g * 256 + cs * 128 + 128], rhs=tb[:, c8, 0:nw], start=first, stop=lastm),
                                    reads=[r_xa, r_tb], writes=[r_p] if first else (), pw=() if first else [r_p])
                for g in range(4):
                    pt, r_p = banks[g]
                    yo, r_y = yst.next()
                    S.op("act" if g % 2 else "dve",
                         (lambda e, yo=yo, pt=pt, nw=nw: e.activation(out=yo[:, 0:nw], in_=pt[:, 0:nw], func=AF.Copy)) if g % 2 else
                         (lambda e, yo=yo, pt=pt, nw=nw: e.tensor_copy(out=yo[:, 0:nw], in_=pt[:, 0:nw])), reads=[r_p], writes=[r_y])
                    b.dma("sp", yT[g][:, n0:n0 + nw], yo[:, 0:nw], yo, reads=[r_y])
            if not last:
                xc_ = sb(pst, "f_xc", [128, 2, 1024], BF16)
                tcc = sb(pst, "f_tcc", [128, 2, 256], BF16)
                tsc = sb(pst, "f_tsc", [128, 2, 256], BF16)
                r_xc = S.res("xc")
                b.dma("sp", xc_[:], xcs[SEQ:TT].rearrange("(c p) g w -> p c (g w)", p=128), xc_, writes=[r_xc])
                b.dma("sp", tcc[:], cnc.rearrange("(c p) n -> p c n", p=128), tcc, pw=[r_xc])
                b.dma("sp", tsc[:], snc.rearrange("(c p) n -> p c n", p=128), tsc, pw=[r_xc])
                for g in range(4):
                    pt, r_p = PS.next()
                    i = 0
                    for ch in range(2):
                        for cs, tb in enumerate((tcc, tsc)):
                            S.op("pe", lambda e, pt=pt, ch=ch, g=g, cs=cs, tb=tb, i=i: e.matmul(
                                pt[:, 0:256], lhsT=xc_[:, ch, g * 256 + cs * 128:g * 256 + cs * 128 + 128], rhs=tb[:, ch, :], start=(i == 0), stop=(i == 3)),
                                reads=[r_xc], writes=[r_p] if i == 0 else (), pw=() if i == 0 else [r_p])
                            i += 1
                    yo, r_y = yst.next()
                    S.op("dve", lambda e, yo=yo, pt=pt: e.tensor_copy(out=yo[:, 0:256], in_=pt[:, 0:256]), reads=[r_p], writes=[r_y])
                    b.dma("sp", yT[g][:, SEQ:TT], yo[:, 0:256], yo, reads=[r_y])
            S.barrier()

        with ExitStack() as pst:
            vall = sb(pst, "a_v", [128, 34, 1024], BF16)
            mkr = Ring(pst, "a_mk", [128, 8, 512], BF16, 3)
            r_v = S.res("vall")
            for q4 in range(4):
                b.dma("sp", vall[:, q4 * 8:(q4 + 1) * 8, :], Vd[q4 * 1024:(q4 + 1) * 1024].rearrange("(c p) d -> p c d", p=128), vall,
                      writes=[r_v] if q4 == 0 else (), pw=[r_v] if q4 else ())
            b.dma("sp", vall[:, 32:34, :], Vd[SEQ:TT].rearrange("(c p) d -> p c d", p=128), vall, pw=[r_v])
            qr_ = Ring(pst, "a_q", [128, TT], BF16, 2)
            kr_ = Ring(pst, "a_k", [128, TT], BF16, 2)
            rp_ = Ring(pst, "a_rp", [128, 8, 512], BF16, 2)
            pTr = Ring(pst, "a_p", [128, 512], BF16, 4)
            rdr = Ring(pst, "a_rd", [128, 512], F32, 2)
            aor = Ring(pst, "a_o", [128, 512], BF16, 3)

            rmr = Ring(pst, "a_rm", [128, 8, 512], BF16, 2)
            dsr = Ring(pst, "a_ds", [128, 512], F32, 2)

            def load_head(h):
                q_, r_q = qr_.next(); k_, r_k = kr_.next(); rp, r_rp = rp_.next()
                b.dma("sp", q_[:], qT[h], q_, writes=[r_q])
                b.dma("sp", k_[:], kT[h], k_, writes=[r_k])
                b.dma("sp", rp[:], rpbt[l, h], rp, writes=[r_rp])
                return (q_, r_q, k_, r_k, rp, r_rp)
            qblocks = [(qb * 512, 512, qb, 0) for qb in range(4 if last else 8)]
            if not last:
                qblocks.append((SEQ, 256, -1, 0))
            else:
                qblocks += [(2048, 128, 4, 0), (3968, 128, 7, 384)]
            items = [(h,) + blk for h in range(8) for blk in qblocks]
            heads = {0: load_head(0)}
            masks = {}
            rms = {}

            def load_mask(ii):
                if ii >= len(items) or items[ii][3] < 0:
                    return
                mk, r_mk = mkr.next()
                b.dma("sp", mk[:], maskt[:, items[ii][3]], mk, writes=[r_mk])
                masks[ii] = (mk, r_mk)

            def prep_rm(ii):
                if ii >= len(items) or items[ii][3] < 0:
                    return
                h_, q0_, nq_, qb_, qoff_ = items[ii]
                mk, r_mk = masks.pop(ii)
                rp, r_rp = heads[h_][4], heads[h_][5]
                rm, r_rm = rmr.next()
                S.op("pool", lambda e, rm=rm, rp=rp, mk=mk, nq_=nq_, qoff_=qoff_: e.tensor_tensor(
                    out=rm[:, :, 0:nq_], in0=rp[:, :, qoff_:qoff_ + nq_], in1=mk[:, :, qoff_:qoff_ + nq_], op=ALU.add),
                    reads=[r_rp, r_mk], writes=[r_rm])
                rms[ii] = (rm, r_rm)
            load_mask(0)
            load_mask(1)
            prep_rm(0)
            pending = []
            for ii, (h, q0, nq, qb, qoff) in enumerate(items):
                if ii % len(qblocks) == 0 and h + 1 < 8:
                    heads[h + 1] = load_head(h + 1)
                q_, r_q, k_, r_k, rp, r_rp = heads[h]
                load_mask(ii + 2)
                prep_rm(ii + 1)
                rm, r_rm = rms.pop(ii, (None, None))
                chunks = [(SEQ, 32, -1), (SEQ + 128, 33, -1)]
                if qb >= 0:
                    for j in range(8):
                        kc_idx = (4 * qb - 2 + j) % 32
                        chunks.append((kc_idx * 128, kc_idx, j))
                acc, r_acc = PSA.next()
                den, r_den = PSA.next()
                ds, r_ds = dsr.next()

                def s_mm(ci, k_=k_, q_=q_, rm=rm, r_rm=r_rm, q0=q0, nq=nq, r_q=r_q, r_k=r_k, chunks=chunks):
                    kcol, vidx, j = chunks[ci]
                    sp_, r_sp = PS.next()
                    S.op("pe", lambda e, sp_=sp_, kcol=kcol, j=j: e.matmul(sp_[:, 0:nq], lhsT=k_[:, kcol:kcol + 128], rhs=q_[:, q0:q0 + nq], start=True, stop=(j < 0)),
                         reads=[r_q, r_k], writes=[r_sp])
                    if j >= 0:
                        S.op("pe", lambda e, sp_=sp_, j=j: e.matmul(sp_[:, 0:nq], lhsT=ident[:], rhs=rm[:, j, 0:nq], start=False, stop=True), reads=[r_rm], pw=[r_sp])
                    return sp_, r_sp
                cur = s_mm(0)
                prev_p = None
                for ci in range(len(chunks)):
                    sp_, r_sp = cur
                    if ci + 1 < len(chunks):
                        cur = s_mm(ci + 1)
                    if ci == 0 and pending:
                        pending.pop()()
                    kcol, vidx, j = chunks[ci]
                    p_, r_pp = pTr.next()
                    S.op("act", lambda e, p_=p_, sp_=sp_, nq=nq: e.activation(out=p_[:, 0:nq], in_=sp_[:, 0:nq], func=AF.Exp), reads=[r_sp], writes=[r_pp])
                    first = (ci == 0); lastc = (ci == len(chunks) - 1)
                    S.op("pe", lambda e, p_=p_, vidx=vidx, first=first, lastc=lastc, acc=acc, h=h, nq=nq: e.matmul(
                        acc[:, 0:nq], lhsT=vall[:, vidx, h * 128:(h + 1) * 128], rhs=p_[:, 0:nq], start=first, stop=lastc),
                        reads=[r_pp, r_v], writes=[r_acc] if first else (), pw=() if first else [r_acc])
                    if ci == 0:
                        prev_p = (p_, r_pp)
                    elif ci == 1:
                        p0, r_p0 = prev_p
                        S.op("dve", lambda e, ds=ds, p0=p0, p_=p_, nq=nq: e.tensor_tensor(out=ds[:, 0:nq], in0=p0[:, 0:nq], in1=p_[:, 0:nq], op=ALU.add),
                             reads=[r_p0, r_pp], writes=[r_ds])
                    else:
                        S.op("dve", lambda e, ds=ds, p_=p_, nq=nq: e.tensor_tensor(out=ds[:, 0:nq], in0=ds[:, 0:nq], in1=p_[:, 0:nq], op=ALU.add),
                             reads=[r_pp, r_ds], writes=[r_ds])

                def fin(ds=ds, r_ds=r_ds, den=den, r_den=r_den, acc=acc, r_acc=r_acc, nq=nq, q0=q0, h=h):
                    S.op("pe", lambda e: e.matmul(den[:, 0:nq], lhsT=ones_f[:], rhs=ds[:, 0:nq], start=True, stop=True),
                         reads=[r_ds], writes=[r_den])
                    rd, r_rd = rdr.next()
                    S.op("dve", lambda e, rd=rd: e.reciprocal(out=rd[:, 0:nq], in_=den[:, 0:nq]), reads=[r_den], writes=[r_rd])
                    ao, r_ao = aor.next()
                    S.op("dve", lambda e, ao=ao, rd=rd: e.tensor_tensor(out=ao[:, 0:nq], in0=acc[:, 0:nq], in1=rd[:, 0:nq], op=ALU.mult),
                         reads=[r_acc, r_rd], writes=[r_ao])
                    b.dma("sp", attT[h][:, q0:q0 + nq], ao[:, 0:nq], ao, reads=[r_ao])
                pending.append(fin)
            while pending:
                pending.pop()()
            S.barrier()

        mgroups = [[(0, 384, 0), (384, 384, 0), (768, 384, 0)],
                   [(1152, 448, 0), (1600, 448, 0)],
                   [(2048, 384, 0), (2432, 384, 0), (2816, 384, 0)],
                   [(3200, 448, 0), (3648, 448, 0)]]
        if not last:
            mgroups[1].append((SEQ, 256, 1))
        else:
            mgroups = mgroups[:2] + [[(2048, 128, 0), (3968, 128, 0)]]
        for mg in mgroups:
            with ExitStack() as pst:
                cat = sb(pst, "m_cat", [128, KC, 1152], BF16)
                mT = sb(pst, "m_mT", [128, KC, 1152], BF16)
                ur = Ring(pst, "m_u", [128, 4, 512], BF16, 2)
                sr = Ring(pst, "m_s", [128, 4, 512], BF16, 2)
                r_cat = S.res("cat")
                loc = []
                a0 = 0
                for (g0, n, sel) in mg:
                    loc.append((a0, n, g0, sel))
                    b.dma("sp", cat[:, 0:4, a0:a0 + n], yT[:, :, g0:g0 + n].rearrange("g p t -> p g t"), cat, pw=[r_cat])
                    b.dma("sp", cat[:, 4:12, a0:a0 + n], attT[:, :, g0:g0 + n].rearrange("g p t -> p g t"), cat, pw=[r_cat])
                    u_, r_u = ur.next(); s_, r_s = sr.next()
                    b.dma("sp", u_[:, :, 0:n], uT[:, :, g0:g0 + n].rearrange("g p t -> p g t"), u_, writes=[r_u])
                    b.dma("sp", s_[:, :, 0:n], spT[:, :, g0:g0 + n].rearrange("g p t -> p g t"), s_, writes=[r_s])
                    S.op("pool", lambda e, u_=u_, s_=s_, a0=a0, n=n: e.tensor_tensor(out=cat[:, 12:16, a0:a0 + n], in0=u_[:, :, 0:n], in1=s_[:, :, 0:n], op=ALU.mult),
                         reads=[r_u, r_s], pw=[r_cat])
                    a0 += n
                wbr = Ring(pst, "m_wb", [128, KC, 256], BF16, 3)
                gtr = Ring(pst, "m_gt", [128, 3, 512], BF16, 3)
                t1r = Ring(pst, "m_t1", [128, 512], F32, 3)
                t2r = Ring(pst, "m_t2", [128, 512], F32, 3)
                xor_ = Ring(pst, "m_xo", [128, 512], F32, 3)
                xnr = Ring(pst, "m_xn", [128, 512], F32, 3)
                r_mT = S.res("mT")

                def load_w1(fb):
                    wt, r_w = wbr.next()
                    c0 = fb * 256
                    b.dma("pool", wt[:, 0:4, :], w_f_out[l][:, c0:c0 + 256].rearrange("(k p) n -> p k n", p=128), wt, writes=[r_w])
                    b.dma("pool", wt[:, 4:12, :], w_na_out[l][:, c0:c0 + 256].rearrange("(k p) n -> p k n", p=128), wt, pw=[r_w])
                    b.dma("pool", wt[:, 12:16, :], w_c_out[l][:, c0:c0 + 256].rearrange("(k p) n -> p k n", p=128), wt, pw=[r_w])
                    return wt, r_w
                nxt = load_w1(0)
                for fb in range(8):
                    wt, r_w = nxt
                    if fb + 1 < 8:
                        nxt = load_w1(fb + 1)
                    for fi in range(2):
                        fc = fb * 2 + fi
                        for (a0, n, g0, sel) in loc:
                            gt, r_g = gtr.next()
                            b.dma("sp", gt[:, :, 0:n], gT.rearrange("(s f) p t -> f p s t", s=3)[fc][:, :, g0:g0 + n], gt, writes=[r_g])
                            b.mu = getattr(b, "mu", 0) + 1
                            ring_ = PS if b.mu % 2 else PSA
                            pf, r_pf = ring_.next(); pn, r_pn = ring_.next(); pc_, r_pc = ring_.next()
                            for (pt, r_p, k0, k1) in ((pf, r_pf, 0, 4), (pn, r_pn, 4, 12), (pc_, r_pc, 12, 16)):
                                for kc in range(k0, k1):
                                    S.op("pe", lambda e, pt=pt, wt=wt, kc=kc, fi=fi, a0=a0, n=n, k0=k0, k1=k1: e.matmul(
                                        pt[:, 0:n], lhsT=wt[:, kc, fi * 128:(fi + 1) * 128], rhs=cat[:, kc, a0:a0 + n], start=(kc == k0), stop=(kc == k1 - 1)),
                                        reads=[r_w, r_cat], writes=[r_p] if kc == k0 else (), pw=() if kc == k0 else [r_p])
                            t1, r_t1 = t1r.next(); t2, r_t2 = t2r.next()
                            S.op("dve", lambda e, t1=t1, pf=pf, gt=gt, n=n: e.tensor_tensor(out=t1[:, 0:n], in0=pf[:, 0:n], in1=gt[:, 0, 0:n], op=ALU.mult),
                                 reads=[r_pf, r_g], writes=[r_t1])
                            S.op("dve", lambda e, t2=t2, pn=pn, gt=gt, n=n: e.tensor_tensor(out=t2[:, 0:n], in0=pn[:, 0:n], in1=gt[:, 1, 0:n], op=ALU.mult),
                                 reads=[r_pn, r_g], writes=[r_t2])
                            S.op("pool", lambda e, t1=t1, t2=t2, n=n: e.tensor_tensor(out=t1[:, 0:n], in0=t1[:, 0:n], in1=t2[:, 0:n], op=ALU.add),
                                 reads=[r_t2, r_t1], writes=[r_t1])
                            t3, r_t3 = t2r.next()
                            S.op("dve", lambda e, t3=t3, pc_=pc_, gt=gt, n=n: e.tensor_tensor(out=t3[:, 0:n], in0=pc_[:, 0:n], in1=gt[:, 2, 0:n], op=ALU.mult),
                                 reads=[r_pc, r_g], writes=[r_t3])
                            S.op("pool", lambda e, t1=t1, t3=t3, fc=fc, a0=a0, n=n: e.tensor_tensor(out=mT[:, fc, a0:a0 + n], in0=t1[:, 0:n], in1=t3[:, 0:n], op=ALU.add),
                                 reads=[r_t1, r_t3], pw=[r_mT])

                def load_w2(fb):
                    wt, r_w = wbr.next()
                    b.dma("pool", wt[:], w_o[l][:, fb * 256:(fb + 1) * 256].rearrange("(k p) n -> p k n", p=128), wt, writes=[r_w])
                    return wt, r_w
                nxt = load_w2(0)
                for fb in range(8):
                    wt, r_w = nxt
                    if fb + 1 < 8:
                        nxt = load_w2(fb + 1)
                    for fi in range(2):
                        fc = fb * 2 + fi
                        for (a0, n, g0, sel) in loc:
                            xo, r_xo = xor_.next()
                            b.dma("sp", xo[:, 0:n], x_cur[fc][:, g0:g0 + n], xo, writes=[r_xo])
                            pt, r_p = PS.next()
                            for kc in range(KC):
                                S.op("pe", lambda e, pt=pt, wt=wt, kc=kc, fi=fi, a0=a0, n=n: e.matmul(
                                    pt[:, 0:n], lhsT=wt[:, kc, fi * 128:(fi + 1) * 128], rhs=mT[:, kc, a0:a0 + n], start=(kc == 0), stop=(kc == KC - 1)),
                                    reads=[r_w, r_mT], writes=[r_p] if kc == 0 else (), pw=[r_p] if kc else ())
                            xn, r_xn = xnr.next()
                            S.op("dve", lambda e, xn=xn, pt=pt, xo=xo, fc=fc, sel=sel, n=n: e.scalar_tensor_tensor(
                                out=xn[:, 0:n], in0=pt[:, 0:n], scalar=G1[:, fc, sel:sel + 1], in1=xo[:, 0:n], op0=ALU.mult, op1=ALU.add),
                                reads=[r_p, r_xo], writes=[r_xn])
                            b.dma("sp", xT_mid[fc][:, g0:g0 + n], xn[:, 0:n], xn, reads=[r_xn])
                S.barrier()

        for grp in range(1 if last else 2):
            srcL, srcR = (SEQ - 1, 2048) if grp == 0 else (2047, 0)
            fL, fR = (0, 1) if grp == 0 else (1, 0)
            ntiles = [(srcL, 1, 0, 0)] + [(grp * 2048 + t, 512, 1 + t, 0) for t in range(0, 2048, 512)] + [(srcR, 1, 2049, 0)]
            if grp == 0 and not last:
                ntiles.append((SEQ, 256, 2050, 1))
            ncols = 2306 if (grp == 0 and not last) else 2050
            with ExitStack() as gst:
                actA = sb(gst, "actB", [128, KC, 2306], BF16)
                r_act = S.res("actB")
                with ExitStack() as pst:
                    norm_phase(pst, xT_mid, actA, r_act, ntiles, A2, SH2)
                    S.barrier()
                r_act = S.res("actB")
                ntl = (ncols + 511) // 512
                bnd = [(ncols * i) // ntl for i in range(ntl + 1)]
                mtiles = [(bnd[i], bnd[i + 1] - bnd[i]) for i in range(ntl)]
                with ExitStack() as pst:
                    war = Ring(pst, "u_wa", [128, KC, 128], BF16, 3)
                    wgr = Ring(pst, "u_wg", [128, KC, 128], BF16, 3)
                    arr = Ring(pst, "u_ar", [128, 2308], F32, 2)
                    grr = Ring(pst, "u_gr", [128, 2308], F32, 2)
                    yr = Ring(pst, "u_y", [128, 2308], F32, 2)
                    hr = Ring(pst, "u_h", [128, 2304], BF16, 2)
                    cwt = sb(pst, "u_cw", [128, JF, 3], F32)
                    cbt = sb(pst, "u_cb", [128, JF], F32)
                    r_cc = S.res("cc")
                    b.dma("sp", cwt[:], cw[l], cwt, writes=[r_cc])
                    b.dma("sp", cbt[:], cb[l], cbt, pw=[r_cc])
                    for _i in range(arr.n):
                        ar_t, r_art = arr.next()
                        S.op("pool", lambda e, ar_t=ar_t: e.memset(ar_t[:], 0.0), writes=[r_art])

                    def load_w(j):
                        wa, r_wa = war.next(); wg, r_wg = wgr.next()
                        b.dma("pool", wa[:], ffn_up[l][:, j * 128:(j + 1) * 128].rearrange("(k p) n -> p k n", p=128), wa, writes=[r_wa])
                        b.dma("pool", wg[:], ffn_up[l][:, DFF + j * 128:DFF + (j + 1) * 128].rearrange("(k p) n -> p k n", p=128), wg, writes=[r_wg])
                        return wa, r_wa, wg, r_wg
                    nxt = load_w(0)
                    for j in range(JF):
                        wa, r_wa, wg, r_wg = nxt
                        if j + 1 < JF:
                            nxt = load_w(j + 1)
                        ar, r_ar = arr.next()
                        gr, r_gr = grr.next()
                        for (t0, n) in mtiles:
                            for which, (wt, r_w, dst, r_d) in enumerate(((wa, r_wa, ar, r_ar), (wg, r_wg, gr, r_gr))):
                                pt, r_p = PS.next()
                                for kc in range(KC):
                                    S.op("pe", lambda e, pt=pt, wt=wt, kc=kc, t0=t0, n=n: e.matmul(
                                        pt[:, 0:n], lhsT=wt[:, kc, :], rhs=actA[:, kc, t0:t0 + n], start=(kc == 0), stop=(kc == KC - 1)),
                                        reads=[r_w, r_act], writes=[r_p] if kc == 0 else (), pw=[r_p] if kc else ())
                                segs = []
                                lo, hi = t0, t0 + n
                                m_hi = min(hi, 2050)
                                if lo < m_hi:
                                    segs.append((lo - t0, m_hi - lo, lo))
                                if hi > 2050:
                                    c_lo = max(lo, 2050)
                                    segs.append((c_lo - t0, hi - c_lo, c_lo + 1))
                                for (p0, sn_, d0) in segs:
                                    if which == 0:
                                        S.op("act", lambda e, pt=pt, dst=dst, p0=p0, sn_=sn_, d0=d0: e.activation(out=dst[:, d0:d0 + sn_], in_=pt[:, p0:p0 + sn_], func=AF.Copy),
                                             reads=[r_p], pw=[r_d])
                                    else:
                                        S.op("dve", lambda e, pt=pt, dst=dst, p0=p0, sn_=sn_, d0=d0: e.tensor_copy(out=dst[:, d0:d0 + sn_], in_=pt[:, p0:p0 + sn_]),
                                             reads=[r_p], pw=[r_d])
                        S.op("dve", lambda e, ar=ar, fL=fL: e.tensor_scalar(out=ar[:, 0:1], in0=ar[:, 0:1], scalar1=flg[:, fL:fL + 1], scalar2=0.0, op0=ALU.mult, op1=ALU.add),
                             reads=[r_ar], writes=[r_ar])
                        S.op("dve", lambda e, ar=ar, fR=fR: e.tensor_scalar(out=ar[:, 2049:2050], in0=ar[:, 2049:2050], scalar1=flg[:, fR:fR + 1], scalar2=0.0, op0=ALU.mult, op1=ALU.add),
                             reads=[r_ar], writes=[r_ar])
                        hi_c = 2307 if (grp == 0 and not last) else 2049
                        y_, r_y = yr.next()
                        W = hi_c - 1
                        S.op("dve", lambda e, y_=y_, ar=ar, j=j, W=W: e.tensor_scalar(out=y_[:, 1:1 + W], in0=ar[:, 1:1 + W], scalar1=cwt[:, j, 1:2], scalar2=cbt[:, j:j + 1], op0=ALU.mult, op1=ALU.add),
                             reads=[r_ar, r_cc], writes=[r_y])
                        S.op("dve", lambda e, y_=y_, ar=ar, j=j, W=W: e.scalar_tensor_tensor(out=y_[:, 1:1 + W], in0=ar[:, 0:W], scalar=cwt[:, j, 0:1], in1=y_[:, 1:1 + W], op0=ALU.mult, op1=ALU.add),
                             reads=[r_ar, r_y], writes=[r_y])
                        S.op("dve", lambda e, y_=y_, ar=ar, j=j, W=W: e.scalar_tensor_tensor(out=y_[:, 1:1 + W], in0=ar[:, 2:2 + W], scalar=cwt[:, j, 2:3], in1=y_[:, 1:1 + W], op0=ALU.mult, op1=ALU.add),
                             reads=[r_ar, r_y], writes=[r_y])
                        S.op("act", lambda e, y_=y_, W=W: e.activation(out=y_[:, 1:1 + W], in_=y_[:, 1:1 + W], func=AF.Silu), reads=[r_y], writes=[r_y])
                        h_, r_h = hr.next()
                        S.op("pool", lambda e, h_=h_, y_=y_, gr=gr: e.tensor_tensor(out=h_[:, 0:2048], in0=y_[:, 1:2049], in1=gr[:, 1:2049], op=ALU.mult),
                             reads=[r_y, r_gr], writes=[r_h])
                        b.dma("sp", hmT[j][:, grp * 2048:(grp + 1) * 2048], h_[:, 0:2048], h_, reads=[r_h])
                        if grp == 0 and not last:
                            S.op("dve", lambda e, h_=h_, y_=y_, gr=gr: e.tensor_tensor(out=h_[:, 2048:2304], in0=y_[:, 2051:2307], in1=gr[:, 2051:2307], op=ALU.mult),
                                 reads=[r_y, r_gr], pw=[r_h])
                            b.dma("sp", hmT[j][:, SEQ:TT], h_[:, 2048:2304], h_, reads=[r_h])
                    S.barrier()

        x_next = xT_nxt if not last else xT_fin
        dgroups = [[(g * 1024, 512, 0), (g * 1024 + 512, 512, 0)] for g in range(4)]
        if not last:
            dgroups.append([(SEQ, 256, 1)])
        else:
            dgroups = dgroups[:2]
        for dg in dgroups:
            with ExitStack() as pst:
                hm = sb(pst, "d_hm", [128, JF, 1024], BF16)
                r_hm = S.res("hm")
                loc = []
                a0 = 0
                for (g0, n, sel) in dg:
                    loc.append((a0, n, g0, sel))
                    for q4 in range(4):
                        b.dma("sp", hm[:, q4 * 11:(q4 + 1) * 11, a0:a0 + n], hmT[q4 * 11:(q4 + 1) * 11, :, g0:g0 + n].rearrange("j p t -> p j t"), hm, pw=[r_hm])
                    a0 += n
                wdr = Ring(pst, "d_w", [128, JF, 256], BF16, 2)
                xor_ = Ring(pst, "d_xo", [128, 512], F32, 3)
                xnr = Ring(pst, "d_xn", [128, 512], F32, 3)

                def load_wd(fb):
                    wt, r_w = wdr.next()
                    for q4 in range(4):
                        b.dma("pool", wt[:, q4 * 11:(q4 + 1) * 11, :],
                              ffn_down[l][q4 * 11 * 128:(q4 + 1) * 11 * 128, fb * 256:(fb + 1) * 256].rearrange("(k p) n -> p k n", p=128), wt,
                              writes=[r_w] if q4 == 0 else (), pw=[r_w] if q4 else ())
                    return wt, r_w
                nxt = load_wd(0)
                for fb in range(8):
                    wt, r_w = nxt
                    if fb + 1 < 8:
                        nxt = load_wd(fb + 1)
                    for fi in range(2):
                        fc = fb * 2 + fi
                        for (a0, n, g0, sel) in loc:
                            xo, r_xo = xor_.next()
                            b.dma("sp", xo[:, 0:n], xT_mid[fc][:, g0:g0 + n], xo, writes=[r_xo])
                            pt, r_p = PS.next()
                            for kc in range(JF):
                                S.op("pe", lambda e, pt=pt, wt=wt, kc=kc, fi=fi, a0=a0, n=n: e.matmul(
                                    pt[:, 0:n], lhsT=wt[:, kc, fi * 128:(fi + 1) * 128], rhs=hm[:, kc, a0:a0 + n], start=(kc == 0), stop=(kc == JF - 1)),
                                    reads=[r_w, r_hm], writes=[r_p] if kc == 0 else (), pw=[r_p] if kc else ())
                            xn, r_xn = xnr.next()
                            S.op("dve", lambda e, xn=xn, pt=pt, xo=xo, fc=fc, sel=sel, n=n: e.scalar_tensor_tensor(
                                out=xn[:, 0:n], in0=pt[:, 0:n], scalar=G2[:, fc, sel:sel + 1], in1=xo[:, 0:n], op0=ALU.mult, op1=ALU.add),
                                reads=[r_p, r_xo], writes=[r_xn])
                            b.dma("sp", x_next[fc][:, g0:g0 + n], xn[:, 0:n], xn, reads=[r_xn])
                S.barrier()
        x_cur = x_next

    with ExitStack() as pst:
        tiles = [(t, 512, t, 0) for t in range(0, 2048, 512)]
        norm_phase(pst, x_cur, None, None, tiles, fnw_sb, None, dst=outT)
        S.flush(final=True)


_CONST = {}


def _consts():
    if _CONST:
        return _CONST
    bf = ml_dtypes.bfloat16
    c = np.arange(128)
    ang = 2 * np.pi * np.outer(c, c) / 128.0
    _CONST["cdsd"] = np.concatenate([np.cos(ang), np.sin(ang)], 1).astype(np.float32) / np.sqrt(128.0)
    _CONST["cdsd"] = _CONST["cdsd"].astype(bf)
    n = np.arange(SEQ, dtype=np.int64)
    m = (np.outer(n, n) % SEQ).astype(np.float64) * (2 * np.pi / SEQ)
    _CONST["cn"] = (np.cos(m) / np.sqrt(SEQ)).astype(np.float32).astype(bf)
    _CONST["sn"] = (-np.sin(m) / np.sqrt(SEQ)).astype(np.float32).astype(bf)
    n = np.arange(NCTX, dtype=np.int64)
    m = (np.outer(n, n) % NCTX).astype(np.float64) * (2 * np.pi / NCTX)
    _CONST["cnc"] = (np.cos(m) / np.sqrt(NCTX)).astype(np.float32).astype(bf)
    _CONST["snc"] = (-np.sin(m) / np.sqrt(NCTX)).astype(np.float32).astype(bf)
    _CONST["ident"] = np.eye(128, dtype=np.float32).astype(bf)
    krr = np.arange(2)[:, None, None, None, None, None]
    kc = np.arange(64)[None, :, None, None, None, None]
    qb = np.arange(8)[None, None, :, None, None, None]
    j = np.arange(8)[None, None, None, :, None, None]
    qr = np.arange(8)[None, None, None, None, :, None]
    qc = np.arange(64)[None, None, None, None, None, :]
    R = 8 * qb + qr
    Kr = 8 * qb - 4 + 2 * j + krr
    rs = np.clip(R - 4, 0, 56)
    cs = np.clip(qc - 8, 0, 48)
    valid = (Kr >= 0) & (Kr <= 63) & (Kr >= rs) & (Kr <= rs + 7) & (kc >= cs) & (kc < cs + 16)
    mfull = np.where(valid, 0.0, NEG).astype(np.float32).reshape(128, 8, 8, 512).astype(bf)
    _CONST["maskt"] = [np.ascontiguousarray(np.roll(mfull, -4 * hf, axis=1)) for hf in range(2)]
    _CONST["cn_h"] = [_CONST["cn"], np.ascontiguousarray(np.roll(_CONST["cn"], (-2048, -2048), axis=(0, 1)))]
    _CONST["sn_h"] = [_CONST["sn"], np.ascontiguousarray(np.roll(_CONST["sn"], (-2048, -2048), axis=(0, 1)))]
    dr = np.clip(2 * j + krr - qr + 3, 0, 14)
    dc = np.clip(kc - qc + 15, 0, 30)
    dr, dc = np.broadcast_arrays(dr[:, :, 0], dc[:, :, 0])
    _CONST["dr"] = dr.reshape(128, 8, 512)
    _CONST["dc"] = dc.reshape(128, 8, 512)
    return _CONST


_NC_CACHE = {}


def _fm(v, rep=None):
    a = np.asarray(v, np.float32)
    a = a.reshape(a.shape[:-1] + (KC, 128)).swapaxes(-1, -2)
    if rep:
        a = np.repeat(a[..., None], rep, axis=-1)
    return np.ascontiguousarray(a)


def kernel(x, c, ctx, c_ctx, ada_w, ada_b, norm1_w, norm2_w, w_in, na_rpb, gmlp_norm_w, gmlp_ws, gmlp_bs,
           w_f_out, w_na_out, w_c_out, w_o, ffn_up, ffn_conv_w, ffn_conv_b, ffn_down, final_norm_w, _dbg=None, _nl=DEPTH):
    bf = ml_dtypes.bfloat16
    K = _consts()
    f32 = lambda a: np.ascontiguousarray(np.asarray(a, np.float32))
    x = f32(x); ctx = f32(ctx); c = f32(c); c_ctx = f32(c_ctx)
    na_rpb = f32(na_rpb)
    shared = {
        "ada_w": f32(ada_w),
        "ada_bT": np.ascontiguousarray(np.repeat(f32(ada_b).reshape(DEPTH, 96, 128).swapaxes(1, 2)[..., None], 2, axis=-1)),
        "nw1": _fm(norm1_w, 2), "nw2": _fm(norm2_w, 2), "fnw": _fm(final_norm_w),
        "w_in": f32(w_in),
        "rpbt": np.ascontiguousarray(na_rpb[:, :, K["dr"], K["dc"]]).astype(bf),
        "gnw": np.ascontiguousarray(f32(gmlp_norm_w).reshape(DEPTH, 4, 128).swapaxes(1, 2)),
        "wsT": np.ascontiguousarray(f32(gmlp_ws).transpose(0, 3, 1, 2)),
        "bsb": np.ascontiguousarray(np.broadcast_to(f32(gmlp_bs)[:, None, :, :], (DEPTH, 128, 4, 128))),
        "w_f_out": f32(w_f_out), "w_na_out": f32(w_na_out), "w_c_out": f32(w_c_out), "w_o": f32(w_o),
        "ffn_up": f32(ffn_up),
        "cw": np.ascontiguousarray(f32(ffn_conv_w).reshape(DEPTH, 3, JF, 128).transpose(0, 3, 2, 1)),
        "cb": np.ascontiguousarray(f32(ffn_conv_b).reshape(DEPTH, JF, 128).swapaxes(1, 2)),
        "ffn_down": f32(ffn_down),
        "cdsd": K["cdsd"], "cnc": K["cnc"], "snc": K["snc"], "ident": K["ident"],
    }
    key = (tuple(_dbg) if _dbg else (), _nl)
    if key not in _NC_CACHE:
        _NC_CACHE[key] = build(dbg=_dbg, n_layers=_nl)
    bb = _NC_CACHE[key]
    in_maps = []
    for core in range(8):
        bi, hf = core // 2, core % 2
        xl = np.roll(x[bi], -2048 * hf, axis=0)
        xt = np.concatenate([xl.T, ctx[bi].T], axis=1).reshape(KC, 128, TT)
        cT = np.stack([c[bi].reshape(KC, 128).T, c_ctx.reshape(KC, 128).T], axis=-1)
        m = dict(shared)
        m["xT"] = np.ascontiguousarray(xt)
        m["cT"] = np.ascontiguousarray(cT)
        m["cn"] = K["cn_h"][hf]; m["sn"] = K["sn_h"][hf]; m["maskt"] = K["maskt"][hf]
        fl = np.zeros((128, 2), np.float32); fl[:, 0] = hf; fl[:, 1] = 1 - hf
        m["flags"] = fl
        in_maps.append(m)
    res = run_bass_kernel_spmd(bb.nc, in_maps, core_ids=list(range(8)))
    out = np.empty((4, SEQ, D), np.float32)
    for core in range(8):
        bi, hf = core // 2, core % 2
        out[bi, hf * 2048:(hf + 1) * 2048] = res.results[core]["outT"].reshape(D, 2048).T
    if _dbg:
        return out, res
    return out
```

```python
import numpy as np
import ml_dtypes
from contextlib import ExitStack
import concourse.bass as bass
import concourse.mybir as mybir
from concourse.bass_utils import run_bass_kernel_spmd

F32 = mybir.dt.float32
BF16 = mybir.dt.bfloat16
AF = mybir.ActivationFunctionType
ALU = mybir.AluOpType
AX = mybir.AxisListType

D = 2048
KC = 16
SEQ = 4096
NCTX = 256
TT = SEQ + NCTX
IN_W = 10752
DFF = 5632
JF = 44
EPS = 1e-6
DEPTH = 2
NEG = -30000.0


class Res:
    __slots__ = ("name", "writers", "readers")

    def __init__(self, name):
        self.name = name
        self.writers = []
        self.readers = []


class Op:
    __slots__ = ("eng", "fn", "deps", "is_dma", "slot", "needed", "count")


class Sched:
    ENG = ("pe", "act", "dve", "pool", "sp")
    ATTR = {"pe": "tensor", "act": "scalar", "dve": "vector", "pool": "gpsimd", "sp": "sync"}

    def __init__(self, nc, stack):
        self.nc = nc
        self.stack = stack
        self.seg = {e: [] for e in self.ENG}
        self.segall = []
        self.resources = []
        self.esem = {e: stack.enter_context(nc.semaphore("s_" + e)) for e in self.ENG}
        self.ecount = {e: 0 for e in self.ENG}
        self.waited = {e: {} for e in self.ENG}
        self.slot_of = {}
        self.free_slots = []
        self.all_slots = []
        self.pending = {e: [] for e in self.ENG}
        self.last = {e: None for e in self.ENG}
        self.n_ops = 0

    def res(self, name="r"):
        r = Res(name)
        self.resources.append(r)
        return r

    def op(self, eng, fn, reads=(), writes=(), pw=(), dma=None):
        o = Op()
        o.eng = eng; o.fn = fn; o.is_dma = dma is not None; o.slot = dma
        o.needed = False; o.count = None
        deps = list(self.pending[eng])
        self.pending[eng] = []
        for r in reads:
            deps.extend(r.writers)
        for w in writes:
            deps.extend(w.writers)
            deps.extend(w.readers)
        for w in pw:
            deps.extend(w.readers)
            if w.writers:
                deps.append(w.writers[0])
        o.deps = deps
        for r in reads:
            r.readers.append(o)
        for w in writes:
            w.writers = [o]
            w.readers = []
        for w in pw:
            w.writers.append(o)
        self.segall.append(o)
        self.seg[eng].append(o)
        if not o.is_dma:
            self.last[eng] = o
        self.n_ops += 1
        return o

    def _slot(self, key):
        s = self.slot_of.get(id(key))
        if s is None:
            if self.free_slots:
                s = self.free_slots.pop()
            else:
                s = [self.stack.enter_context(self.nc.semaphore("d%d" % len(self.all_slots))), 0, None]
                self.all_slots.append(s)
            self.slot_of[id(key)] = s
        return s

    def barrier(self):
        self.flush(barrier=True)

    def flush(self, barrier=False, final=False):
        nc = self.nc
        bar = []
        if barrier or final:
            for e in self.ENG:
                if self.last[e] is not None:
                    self.last[e].needed = True
        for o in self.segall:
            for d in o.deps:
                if d.is_dma:
                    continue
                if d.eng == o.eng and o.eng == "pe" and not o.is_dma:
                    continue
                d.needed = True
        for o in self.segall:
            if o.count is not None:
                continue
            if o.is_dma:
                s = self._slot(o.slot)
                s[1] += 16
                s[2] = o
                o.count = (s[0], s[1])
            elif o.needed:
                self.ecount[o.eng] += 1
                o.count = (self.esem[o.eng], self.ecount[o.eng])
        segs = self.seg
        extra = []
        if barrier or final:
            for e in self.ENG:
                if self.last[e] is not None and self.last[e].count is not None:
                    extra.append(self.last[e].count)
            for s in self.all_slots:
                if s[1] > 0:
                    extra.append((s[0], s[1]))

        def emit(ename):
            def body(eng):
                waited = self.waited[ename]
                for o in segs[ename]:
                    for d in o.deps:
                        c = d.count
                        if c is None:
                            continue
                        if d.eng == ename and ename == "pe" and not d.is_dma and not o.is_dma:
                            continue
                        if waited.get(id(c[0]), 0) >= c[1]:
                            continue
                        waited[id(c[0])] = c[1]
                        eng.wait_ge(c[0], c[1])
                    ins = o.fn(eng)
                    if o.count is not None:
                        ins.then_inc(o.count[0], 16 if o.is_dma else 1)
                for sem, val in extra:
                    if waited.get(id(sem), 0) < val:
                        waited[id(sem)] = val
                        eng.wait_ge(sem, val)
            return body
        with nc.Block() as block:
            for e in self.ENG:
                getattr(block, self.ATTR[e])(emit(e))
        self.seg = {e: [] for e in self.ENG}
        self.segall = []
        if barrier or final:
            for r in self.resources:
                r.writers = []
                r.readers = []
            self.resources = []
            self.slot_of = {}
            self.free_slots = list(self.all_slots)
            self.last = {e: None for e in self.ENG}


class B:
    def __init__(self, dbg=False):
        self.dbg = dbg
        self.nc = bass.Bass("TRN2", target_bir_lowering=False)
        self.st = ExitStack()
        self.S = None
        self.ins = {}
        self.psum = None
        self.psi = 0

    def inp(self, name, shape, dt=F32):
        t = self.nc.dram_tensor(name, list(shape), dt, kind="ExternalInput").ap()
        self.ins[name] = t
        return t

    def scratch(self, name, shape, dt):
        kind = "ExternalOutput" if (self.dbg and name in self.dbg) else "Internal"
        return self.nc.dram_tensor(name, list(shape), dt, kind=kind).ap()

    def sb(self, stack, name, shape, dt):
        self.uid = getattr(self, "uid", 0) + 1
        return stack.enter_context(self.nc.sbuf_tensor("%s_%d" % (name, self.uid), list(shape), dt))

    def ps(self):
        i = self.psi
        self.psi = (self.psi + 1) % len(self.psum)
        return self.psum[i]

    def dma(self, q, out, in_, slot, reads=(), writes=(), pw=(), slow=False):
        if slow:
            return self.S.op(q, lambda e: e.dma_start(out=out, in_=in_, allow_slow_non_contiguous=True), reads=reads, writes=writes, pw=pw, dma=slot)
        return self.S.op(q, lambda e: e.dma_start(out=out, in_=in_), reads=reads, writes=writes, pw=pw, dma=slot)


def build(dbg=None, n_layers=DEPTH):
    b = B(dbg=dbg or ())
    b.n_layers = n_layers
    nc = b.nc
    st = b.st
    with st:
        _build(b)
    return b


def _build(b):
    nc, st = b.nc, b.st
    S = b.S = Sched(nc, st)
    inp = b.inp

    xT_in = inp("xT", [KC, 128, TT])
    cT = inp("cT", [128, KC, 2])
    ada_w = inp("ada_w", [DEPTH, D, 6 * D])
    ada_b = inp("ada_bT", [DEPTH, 128, 96, 2])
    nw1 = inp("nw1", [DEPTH, 128, KC, 2])
    nw2 = inp("nw2", [DEPTH, 128, KC, 2])
    fnw = inp("fnw", [128, KC])
    w_in = inp("w_in", [DEPTH, D, IN_W])
    rpbt = inp("rpbt", [DEPTH, 8, 128, 8, 512], BF16)
    maskt = inp("maskt", [128, 8, 8, 512], BF16)
    flags_in = inp("flags", [128, 2])
    gnw = inp("gnw", [DEPTH, 128, 4])
    wsT = inp("wsT", [DEPTH, 128, 4, 128])
    bsb = inp("bsb", [DEPTH, 128, 4, 128])
    w_f_out = inp("w_f_out", [DEPTH, 512, D])
    w_na_out = inp("w_na_out", [DEPTH, 1024, D])
    w_c_out = inp("w_c_out", [DEPTH, 512, D])
    w_o = inp("w_o", [DEPTH, D, D])
    ffn_up = inp("ffn_up", [DEPTH, D, 2 * DFF])
    cw = inp("cw", [DEPTH, 128, JF, 3])
    cb = inp("cb", [DEPTH, 128, JF])
    ffn_down = inp("ffn_down", [DEPTH, DFF, D])
    cdsd = inp("cdsd", [128, 256], BF16)
    cn = inp("cn", [SEQ, SEQ], BF16)
    sn = inp("sn", [SEQ, SEQ], BF16)
    cnc = inp("cnc", [NCTX, NCTX], BF16)
    snc = inp("snc", [NCTX, NCTX], BF16)
    ident_in = inp("ident", [128, 128], BF16)
    outT = nc.dram_tensor("outT", [KC, 128, 2048], F32, kind="ExternalOutput").ap()

    sc = b.scratch
    xT_mid = sc("xT_mid", [KC, 128, TT], F32)
    xT_nxt = sc("xT_nxt", [KC, 128, TT], F32)
    xT_fin = sc("xT_fin", [KC, 128, TT], F32)
    xcs = sc("xcs", [TT, 4, 256], BF16)
    qT = sc("qT", [8, 128, TT], BF16)
    kT = sc("kT", [8, 128, TT], BF16)
    Vd = sc("Vd", [TT, 1024], BF16)
    uT = sc("uT", [4, 128, TT], BF16)
    spT = sc("spT", [4, 128, TT], BF16)
    gT = sc("gT", [48, 128, TT], BF16)
    yT = sc("yT", [4, 128, TT], BF16)
    attT = sc("attT", [8, 128, TT], BF16)
    hmT = sc("hmT", [JF, 128, TT], BF16)

    sb = b.sb
    ident = sb(st, "ident_sb", [128, 128], BF16)
    ones_f = sb(st, "ones_f", [128, 128], F32)
    ones_b = sb(st, "ones_b", [128, 128], BF16)
    scT = sb(st, "scT", [128, KC, 2], F32)
    modT = sb(st, "modT", [128, 96, 2], F32)
    A1 = sb(st, "A1", [128, KC, 2], F32)
    A2 = sb(st, "A2", [128, KC, 2], F32)
    fnw_sb = sb(st, "fnw_sb", [128, KC], F32)
    ones16 = sb(st, "ones16", [128, KC], F32)
    zero16 = sb(st, "zero16", [128, KC], F32)
    flg = sb(st, "flg", [128, 2], F32)
    b.psum = [st.enter_context(nc.psum_tensor("psb%d" % i, [128, 512], F32)) for i in range(8)]
    r_psum = None

    def newres(n=1, name="r"):
        return [S.res(name) for _ in range(n)] if n > 1 else S.res(name)

    class Ring:
        def __init__(self, stack, name, shape, dt, n):
            self.t = [sb(stack, "%s%d" % (name, i), shape, dt) for i in range(n)]
            self.r = [None] * n
            self.i = 0
            self.n = n
            self.name = name

        def next(self):
            i = self.i
            self.i = (i + 1) % self.n
            if self.r[i] is None or self.r[i] not in S.resources:
                self.r[i] = S.res(self.name)
            return self.t[i], self.r[i]

    class PsRing:
        def __init__(self, banks):
            self.banks = banks
            self.r = [None] * len(banks)
            self.i = 0

        def next(self):
            i = self.i
            self.i = (i + 1) % len(self.banks)
            if self.r[i] is None or self.r[i] not in S.resources:
                self.r[i] = S.res("ps")
            return b.psum[self.banks[i]], self.r[i]

    PS = PsRing([0, 1, 2, 3])
    PSA = PsRing([4, 5, 6, 7])

    r_const = S.res("const")
    b.dma("sp", ident[:], ident_in, ident, writes=[r_const])
    S.op("dve", lambda e: e.memset(ones_f[:], 1.0), pw=[r_const])
    S.op("dve", lambda e: e.memset(ones_b[:], 1.0), pw=[r_const])
    S.op("dve", lambda e: e.memset(ones16[:], 1.0), pw=[r_const])
    S.op("dve", lambda e: e.memset(zero16[:], 0.0), pw=[r_const])
    b.dma("sp", scT[:], cT, scT, pw=[r_const])
    b.dma("sp", fnw_sb[:], fnw, fnw_sb, pw=[r_const])
    b.dma("sp", flg[:], flags_in, flg, pw=[r_const])
    S.barrier()
    r_c2 = S.res("c2")
    S.op("act", lambda e: e.activation(out=scT[:], in_=scT[:], func=AF.Silu), writes=[r_c2])
    S.barrier()

    def lat_tiles(lo, hi, step=512):
        return [(t, min(step, hi - t), 0) for t in range(lo, hi, step)]

    def norm_phase(pst, src, actA, r_act, tiles, Asc, Bsh, mul_only=False, dst=None):
        xr = Ring(pst, "n_x", [128, KC, 512], F32, 2)
        sq = Ring(pst, "n_sq", [128, 512], F32, 3)
        rs = Ring(pst, "n_rs", [128, 512], F32, 2)
        tm = Ring(pst, "n_tm", [128, 512], F32, 3)
        og = Ring(pst, "n_o", [128, KC, 512], F32, 1) if dst is not None else None
        for (c0, n, d0, sel) in tiles:
            xt, r_x = xr.next()
            b.dma("sp", xt[:, :, 0:n], src[:, :, c0:c0 + n].rearrange("k p t -> p k t"), xt, writes=[r_x], slow=(n == 1))
            pt, r_p = PS.next()
            for kc in range(KC):
                s_, r_s = sq.next()
                S.op("act", lambda e, s_=s_, xt=xt, kc=kc, n=n: e.activation(out=s_[:, 0:n], in_=xt[:, kc, 0:n], func=AF.Square),
                     reads=[r_x], writes=[r_s])
                S.op("pe", lambda e, pt=pt, s_=s_, kc=kc, n=n: e.matmul(pt[:, 0:n], lhsT=ones_f[:], rhs=s_[:, 0:n], start=(kc == 0), stop=(kc == KC - 1)),
                     reads=[r_s], writes=[r_p] if kc == 0 else (), pw=[r_p] if kc else ())
            rt, r_r = rs.next()
            S.op("dve", lambda e, rt=rt, pt=pt, n=n: e.tensor_scalar(out=rt[:, 0:n], in0=pt[:, 0:n], scalar1=1.0 / D, scalar2=EPS, op0=ALU.mult, op1=ALU.add),
                 reads=[r_p], writes=[r_r])
            S.op("act", lambda e, rt=rt, n=n: e.activation(out=rt[:, 0:n], in_=rt[:, 0:n], func=AF.Sqrt), reads=[r_r], writes=[r_r])
            S.op("dve", lambda e, rt=rt, n=n: e.reciprocal(out=rt[:, 0:n], in_=rt[:, 0:n]), reads=[r_r], writes=[r_r])
            if dst is not None:
                ot, r_o = og.next()
            for kc in range(KC):
                t_, r_t = tm.next()
                S.op("dve", lambda e, t_=t_, xt=xt, rt=rt, kc=kc, n=n: e.tensor_tensor(out=t_[:, 0:n], in0=xt[:, kc, 0:n], in1=rt[:, 0:n], op=ALU.mult),
                     reads=[r_x, r_r], writes=[r_t])
                if dst is None:
                    S.op("act", lambda e, t_=t_, kc=kc, n=n, d0=d0, sel=sel: e.activation(
                        out=actA[:, kc, d0:d0 + n], in_=t_[:, 0:n], func=AF.Identity,
                        bias=Bsh[:, kc, sel:sel + 1], scale=Asc[:, kc, sel:sel + 1]),
                        reads=[r_t], pw=[r_act])
                else:
                    S.op("act", lambda e, t_=t_, kc=kc, n=n, ot=ot: e.activation(
                        out=ot[:, kc, 0:n], in_=t_[:, 0:n], func=AF.Identity, scale=Asc[:, kc:kc + 1]),
                        reads=[r_t], pw=[r_o])
            if dst is not None:
                b.dma("sp", dst[:, :, d0:d0 + n].rearrange("k p t -> p k t"), ot[:, :, 0:n], ot, reads=[r_o])

    x_cur = xT_in
    for l in range(b.n_layers):
        last = (l == DEPTH - 1)
        with ExitStack() as pst:
            wr = Ring(pst, "ada_wb", [128, KC, 512], F32, 2)
            adab = sb(pst, "adab", [128, 96, 2], F32)
            n1 = sb(pst, "n1", [128, KC, 2], F32)
            n2 = sb(pst, "n2", [128, KC, 2], F32)
            r_ab = S.res("adab")
            b.dma("sp", adab[:], ada_b[l], adab, writes=[r_ab])
            b.dma("sp", n1[:], nw1[l], n1, pw=[r_ab])
            b.dma("sp", n2[:], nw2[l], n2, pw=[r_ab])
            pm, r_pm = PS.next()
            r_mod = S.res("mod")
            for blk in range(24):
                wt, r_w = wr.next()
                b.dma("sp", wt[:], ada_w[l][:, blk * 512:(blk + 1) * 512].rearrange("(k p) n -> p k n", p=128), wt, writes=[r_w])
                for fi in range(4):
                    fc = blk * 4 + fi
                    for kc in range(KC):
                        S.op("pe", lambda e, wt=wt, fi=fi, fc=fc, kc=kc: e.matmul(
                            pm[:, fc * 2:fc * 2 + 2], lhsT=wt[:, kc, fi * 128:(fi + 1) * 128], rhs=scT[:, kc, :],
                            start=(kc == 0), stop=(kc == KC - 1)), reads=[r_w], pw=[r_pm])
            S.op("dve", lambda e: e.tensor_tensor(out=modT[:].rearrange("p a b -> p (a b)"), in0=pm[:, 0:192],
                                                  in1=adab[:].rearrange("p a b -> p (a b)"), op=ALU.add),
                 reads=[r_pm, r_ab], writes=[r_mod])
            S.op("dve", lambda e: e.scalar_tensor_tensor(out=A1[:], in0=modT[:, 16:32, :], scalar=1.0, in1=n1[:], op0=ALU.add, op1=ALU.mult),
                 reads=[r_mod, r_ab], writes=[S.res()])
            S.op("dve", lambda e: e.scalar_tensor_tensor(out=A2[:], in0=modT[:, 64:80, :], scalar=1.0, in1=n2[:], op0=ALU.add, op1=ALU.mult),
                 reads=[r_mod, r_ab], writes=[S.res()])
            S.barrier()
        SH1 = modT[:, 0:16, :]
        G1 = modT[:, 32:48, :]
        SH2 = modT[:, 48:64, :]
        G2 = modT[:, 80:96, :]

        for grp in range(2):
            with ExitStack() as gst:
                actA = sb(gst, "actA", [128, KC, 2304], BF16)
                r_act = S.res("actA")
                if grp == 0:
                    tiles = [(t, 512, t, 0) for t in range(0, 2048, 512)] + [(SEQ, 256, 2048, 1)]
                else:
                    tiles = [(t, 512, t - 2048, 0) for t in range(2048, 4096, 512)]
                with ExitStack() as pst:
                    norm_phase(pst, x_cur, actA, r_act, tiles, A1, SH1)
                    S.barrier()
                r_act = S.res("actA")
                if grp == 0:
                    mt = [(t, 512, t, 0) for t in range(0, 2048, 512)] + [(2048, 256, SEQ, 1)]
                else:
                    mt = [(t, 512, t + 2048, 0) for t in range(0, 2048, 512)]
                with ExitStack() as pst:
                    wbr = Ring(pst, "p_wb", [128, KC, 256], BF16, 3)
                    stg = Ring(pst, "p_stg", [128, 512], BF16, 4)
                    fsb = Ring(pst, "p_fsb", [128, 512], BF16, 2)
                    xst = Ring(pst, "p_xst", [128, 2, 256], BF16, 3)
                    cd = sb(pst, "p_cd", [128, 256], BF16)
                    r_cd = S.res("cd")
                    b.dma("sp", cd[:], cdsd, cd, writes=[r_cd])
                    fcs = list(range(0, 20)) + list(range(28, 32)) + list(range(36, 84))
                    mt_halo = [(0, 128, 2048, 0), (1920, 128, 3968, 0)]
                    blocks = []
                    for fc in fcs:
                        if blocks and blocks[-1][0] // 2 == fc // 2 and len(blocks[-1]) < 2:
                            blocks[-1].append(fc)
                        else:
                            blocks.append([fc])

                    def load_blk(blk):
                        wt, r_w = wbr.next()
                        c0 = blk[0] * 128
                        nn = len(blk) * 128
                        b.dma("pool", wt[:, :, 0:nn], w_in[l][:, c0:c0 + nn].rearrange("(k p) n -> p k n", p=128), wt, writes=[r_w])
                        return wt, r_w
                    nxt = load_blk(blocks[0])
                    for bi, blk in enumerate(blocks):
                        wt, r_w = nxt
                        if bi + 1 < len(blocks):
                            nxt = load_blk(blocks[bi + 1])
                        for fi, fc in enumerate(blk):
                            fk = fc < 4 or 12 <= fc < 20
                            for (a0, n, g0, isctx) in (mt if (fk or not (last and grp == 1)) else mt_halo):
                                if last and isctx and not (12 <= fc < 20):
                                    continue
                                pt, r_p = PS.next()
                                for kc in range(KC):
                                    S.op("pe", lambda e, pt=pt, wt=wt, fi=fi, kc=kc, a0=a0, n=n: e.matmul(
                                        pt[:, 0:n], lhsT=wt[:, kc, fi * 128:(fi + 1) * 128], rhs=actA[:, kc, a0:a0 + n],
                                        start=(kc == 0), stop=(kc == KC - 1)),
                                        reads=[r_w, r_act], writes=[r_p] if kc == 0 else (), pw=[r_p] if kc else ())
                                if fc < 4:
                                    ft, r_f = fsb.next()
                                    S.op("act", lambda e, ft=ft, pt=pt, n=n: e.activation(out=ft[:, 0:n], in_=pt[:, 0:n], func=AF.Copy),
                                         reads=[r_p], writes=[r_f])
                                    nsub = n // 128
                                    for s0 in range(0, nsub, 2):
                                        k2 = min(2, nsub - s0)
                                        p2, r_p2 = PS.next()
                                        for a in range(k2):
                                            S.op("pe", lambda e, p2=p2, ft=ft, a=a, s0=s0: e.matmul(
                                                p2[:, a * 256:(a + 1) * 256], lhsT=ft[:, (s0 + a) * 128:(s0 + a + 1) * 128], rhs=cd[:], start=True, stop=True),
                                                reads=[r_f, r_cd], writes=[r_p2] if a == 0 else (), pw=[r_p2] if a else ())
                                        xs, r_xs = xst.next()
                                        S.op("dve", lambda e, xs=xs, p2=p2, k2=k2: e.tensor_copy(
                                            out=xs[:, 0:k2, :], in_=p2[:, 0:k2 * 256].rearrange("p (a c) -> p a c", a=k2)),
                                            reads=[r_p2], writes=[r_xs])
                                        t0 = g0 + s0 * 128
                                        b.dma("sp", xcs[t0:t0 + k2 * 128, fc, :].rearrange("(a p) c -> p a c", p=128), xs[:, 0:k2, :], xs, reads=[r_xs])
                                    continue
                                so, r_so = stg.next()
                                if fc < 12:
                                    S.op("act", lambda e, so=so, pt=pt, n=n: e.activation(out=so[:, 0:n], in_=pt[:, 0:n], func=AF.Copy, scale=128 ** -0.5),
                                         reads=[r_p], writes=[r_so])
                                    dstap = qT[fc - 4][:, g0:g0 + n]
                                elif fc < 20:
                                    S.op("dve", lambda e, so=so, pt=pt, n=n: e.tensor_copy(out=so[:, 0:n], in_=pt[:, 0:n]), reads=[r_p], writes=[r_so])
                                    dstap = kT[fc - 12][:, g0:g0 + n]
                                elif fc < 32:
                                    S.op("act", lambda e, so=so, pt=pt, n=n: e.activation(out=so[:, 0:n], in_=pt[:, 0:n], func=AF.Gelu), reads=[r_p], writes=[r_so])
                                    dstap = uT[fc - 28][:, g0:g0 + n]
                                else:
                                    S.op("act", lambda e, so=so, pt=pt, n=n: e.activation(out=so[:, 0:n], in_=pt[:, 0:n], func=AF.Sigmoid), reads=[r_p], writes=[r_so])
                                    dstap = gT[fc - 36][:, g0:g0 + n]
                                b.dma("sp", dstap, so[:, 0:n], so, reads=[r_so])
                    wvr = Ring(pst, "p_wv", [128, KC, 512], BF16, 2)
                    vst = Ring(pst, "p_vst", [128, 512], BF16, 3)
                    zg = Ring(pst, "p_zg", [128, 4, 128], F32, 2)
                    zc = Ring(pst, "p_zc", [128, 4, 128], F32, 2)
                    zq = Ring(pst, "p_zq", [128, 4, 128], F32, 2)
                    st4 = Ring(pst, "p_st4", [128, 8], F32, 4)
                    vh = Ring(pst, "p_vh", [128, 4, 128], BF16, 2)
                    spo = Ring(pst, "p_spo", [128, 4, 128], BF16, 2)
                    wst = sb(pst, "p_wst", [128, 4, 128], BF16)
                    bst = sb(pst, "p_bst", [128, 4, 128], F32)
                    gnt = sb(pst, "p_gnt", [128, 4], F32)
                    r_gc = S.res("gconst")
                    b.dma("pool", wst[:], wsT[l], wst, writes=[r_gc])
                    b.dma("sp", bst[:], bsb[l], bst, pw=[r_gc])
                    b.dma("sp", gnt[:], gnw[l], gnt, pw=[r_gc])
                    ntok = 2304 if grp == 0 else 2048
                    for sec, c0 in (("v0", 2560), ("v1", 3072), ("zv", 4096)):
                        wt, r_w = wvr.next()
                        b.dma("pool", wt[:], w_in[l][:, c0:c0 + 512].rearrange("(k p) n -> p k n", p=128), wt, writes=[r_w])
                        for tt in range(ntok // 128):
                            a0 = tt * 128
                            isctx = (grp == 0 and a0 >= 2048)
                            g0 = (SEQ + a0 - 2048) if isctx else (a0 + grp * 2048)
                            if sec == "zv" and last and (isctx or (grp == 1 and tt not in (0, 15))):
                                continue
                            pt, r_p = PS.next()
                            for kc in range(KC):
                                S.op("pe", lambda e, pt=pt, wt=wt, kc=kc, a0=a0: e.matmul(
                                    pt[:], lhsT=actA[:, kc, a0:a0 + 128], rhs=wt[:, kc, :], start=(kc == 0), stop=(kc == KC - 1)),
                                    reads=[r_w, r_act], writes=[r_p] if kc == 0 else (), pw=[r_p] if kc else ())
                            if sec != "zv":
                                so, r_so = vst.next()
                                S.op("dve" if tt % 2 else "act",
                                     (lambda e, so=so, pt=pt: e.tensor_copy(out=so[:], in_=pt[:])) if tt % 2 else
                                     (lambda e, so=so, pt=pt: e.activation(out=so[:], in_=pt[:], func=AF.Copy)),
                                     reads=[r_p], writes=[r_so])
                                vc0 = 0 if sec == "v0" else 512
                                b.dma("sp", Vd[g0:g0 + 128, vc0:vc0 + 512], so[:], so, reads=[r_so])
                                continue
                            z_, r_z = zg.next()
                            S.op("act", lambda e, z_=z_, pt=pt: e.activation(out=z_[:].rearrange("p g d -> p (g d)"), in_=pt[:], func=AF.Gelu),
                                 reads=[r_p], writes=[r_z])
                            s4, r_s4 = st4.next()
                            S.op("dve", lambda e, s4=s4, z_=z_: e.tensor_reduce(out=s4[:, 0:4], in_=z_[:], axis=AX.X, op=ALU.add),
                                 reads=[r_z], writes=[r_s4])
                            S.op("dve", lambda e, s4=s4: e.tensor_scalar(out=s4[:, 0:4], in0=s4[:, 0:4], scalar1=1.0 / 128, scalar2=0.0, op0=ALU.mult, op1=ALU.add),
                                 reads=[r_s4], writes=[r_s4])
                            c_, r_c = zc.next()
                            for g in range(4):
                                S.op("dve", lambda e, c_=c_, z_=z_, s4=s4, g=g: e.tensor_scalar(
                                    out=c_[:, g, :], in0=z_[:, g, :], scalar1=s4[:, g:g + 1], scalar2=0.0, op0=ALU.subtract, op1=ALU.add),
                                    reads=[r_z, r_s4], writes=[r_c] if g == 0 else (), pw=[r_c] if g else ())
                            q_, r_q = zq.next()
                            S.op("pool", lambda e, q_=q_, c_=c_: e.tensor_tensor(out=q_[:], in0=c_[:], in1=c_[:], op=ALU.mult), reads=[r_c], writes=[r_q])
                            S.op("dve", lambda e, s4=s4, q_=q_: e.tensor_reduce(out=s4[:, 4:8], in_=q_[:], axis=AX.X, op=ALU.add),
                                 reads=[r_q], writes=[r_s4])
                            S.op("dve", lambda e, s4=s4: e.tensor_scalar(out=s4[:, 4:8], in0=s4[:, 4:8], scalar1=1.0 / 128, scalar2=EPS, op0=ALU.mult, op1=ALU.add),
                                 reads=[r_s4], writes=[r_s4])
                            S.op("act", lambda e, s4=s4: e.activation(out=s4[:, 4:8], in_=s4[:, 4:8], func=AF.Sqrt), reads=[r_s4], writes=[r_s4])
                            S.op("dve", lambda e, s4=s4: e.reciprocal(out=s4[:, 4:8], in_=s4[:, 4:8]), reads=[r_s4], writes=[r_s4])
                            v_, r_v = vh.next()
                            for g in range(4):
                                S.op("dve", lambda e, v_=v_, c_=c_, s4=s4, g=g: e.tensor_scalar(
                                    out=v_[:, g, :], in0=c_[:, g, :], scalar1=s4[:, 4 + g:5 + g], scalar2=0.0, op0=ALU.mult, op1=ALU.add),
                                    reads=[r_c, r_s4], writes=[r_v] if g == 0 else (), pw=[r_v] if g else ())
                            p3, r_p3 = PS.next()
                            for g in range(4):
                                S.op("pe", lambda e, p3=p3, v_=v_, g=g: e.matmul(p3[:, g * 128:(g + 1) * 128], lhsT=v_[:, g, :], rhs=wst[:, g, :], start=True, stop=True),
                                     reads=[r_v, r_gc], writes=[r_p3] if g == 0 else (), pw=[r_p3] if g else ())
                            o_, r_o = spo.next()
                            for g in range(4):
                                S.op("dve", lambda e, o_=o_, p3=p3, g=g: e.scalar_tensor_tensor(
                                    out=o_[:, g, :], in0=p3[:, g * 128:(g + 1) * 128], scalar=gnt[:, g:g + 1], in1=bst[:, g, :], op0=ALU.mult, op1=ALU.add),
                                    reads=[r_p3, r_gc], writes=[r_o] if g == 0 else (), pw=[r_o] if g else ())
                            b.dma("sp", spT[:, :, g0:g0 + 128].rearrange("g p t -> p g t"), o_[:], o_, reads=[r_o])
                    S.barrier()

        with ExitStack() as pst:
            xa = sb(pst, "f_xa", [128, 32, 1024], BF16)
            r_xa = S.res("xa")
            for q4 in range(4):
                b.dma("sp", xa[:, q4 * 8:(q4 + 1) * 8, :],
                      xcs[q4 * 1024:(q4 + 1) * 1024].rearrange("(c p) g w -> p c (g w)", p=128), xa,
                      writes=[r_xa] if q4 == 0 else (), pw=[r_xa] if q4 else ())
            tcr = Ring(pst, "f_tc", [128, 8, 512], BF16, 3)
            tsr = Ring(pst, "f_ts", [128, 8, 512], BF16, 3)
            yst = Ring(pst, "f_y", [128, 512], BF16, 4)
            ftiles = [(t, 512) for t in range(0, SEQ if not last else 2048, 512)]
            if last:
                ftiles += [(2048, 128), (3968, 128)]
            for jt, (n0, nw) in enumerate(ftiles):
                banks = [(PSA if jt % 2 == 0 else PS).next() for _ in range(4)]
                for pc in range(4):
                    tc_, r_tc = tcr.next()
                    ts_, r_ts = tsr.next()
                    b.dma("sp", tc_[:, :, 0:nw], cn[pc * 1024:(pc + 1) * 1024, n0:n0 + nw].rearrange("(c p) n -> p c n", p=128), tc_, writes=[r_tc])
                    b.dma("sp", ts_[:, :, 0:nw], sn[pc * 1024:(pc + 1) * 1024, n0:n0 + nw].rearrange("(c p) n -> p c n", p=128), ts_, writes=[r_ts])
                    for g in range(4):
                        pt, r_p = banks[g]
                        for c8 in range(8):
                            ch = pc * 8 + c8
                            for cs, (tb, r_tb) in enumerate(((tc_, r_tc), (ts_, r_ts))):
                                first = (ch == 0 and cs == 0)
                                lastm = (ch == 31 and cs == 1)
                                S.op("pe", lambda e, pt=pt, ch=ch, g=g, cs=cs, tb=tb, c8=c8, first=first, lastm=lastm, nw=nw: e.matmul(
                                    pt[:, 0:nw], lhsT=xa[:, ch, g * 256 + cs * 128:g * 256 + cs * 128 + 128], rhs=tb[:, c8, 0:nw], start=first, stop=lastm),
                                    reads=[r_xa, r_tb], writes=[r_p] if first else (), pw=() if first else [r_p])
                for g in range(4):
                    pt, r_p = banks[g]
                    yo, r_y = yst.next()
                    S.op("act" if g % 2 else "dve",
                         (lambda e, yo=yo, pt=pt, nw=nw: e.activation(out=yo[:, 0:nw], in_=pt[:, 0:nw], func=AF.Copy)) if g % 2 else
                         (lambda e, yo=yo, pt=pt, nw=nw: e.tensor_copy(out=yo[:, 0:nw], in_=pt[:, 0:nw])), reads=[r_p], writes=[r_y])
                    b.dma("sp", yT[g][:, n0:n0 + nw], yo[:, 0:nw], yo, reads=[r_y])
            if not last:
                xc_ = sb(pst, "f_xc", [128, 2, 1024], BF16)
                tcc = sb(pst, "f_tcc", [128, 2, 256], BF16)
                tsc = sb(pst, "f_tsc", [128, 2, 256], BF16)
                r_xc = S.res("xc")
                b.dma("sp", xc_[:], xcs[SEQ:TT].rearrange("(c p) g w -> p c (g w)", p=128), xc_, writes=[r_xc])
                b.dma("sp", tcc[:], cnc.rearrange("(c p) n -> p c n", p=128), tcc, pw=[r_xc])
                b.dma("sp", tsc[:], snc.rearrange("(c p) n -> p c n", p=128), tsc, pw=[r_xc])
                for g in range(4):
                    pt, r_p = PS.next()
                    i = 0
                    for ch in range(2):
                        for cs, tb in enumerate((tcc, tsc)):
                            S.op("pe", lambda e, pt=pt, ch=ch, g=g, cs=cs, tb=tb, i=i: e.matmul(
                                pt[:, 0:256], lhsT=xc_[:, ch, g * 256 + cs * 128:g * 256 + cs * 128 + 128], rhs=tb[:, ch, :], start=(i == 0), stop=(i == 3)),
                                reads=[r_xc], writes=[r_p] if i == 0 else (), pw=() if i == 0 else [r_p])
                            i += 1
                    yo, r_y = yst.next()
                    S.op("dve", lambda e, yo=yo, pt=pt: e.tensor_copy(out=yo[:, 0:256], in_=pt[:, 0:256]), reads=[r_p], writes=[r_y])
                    b.dma("sp", yT[g][:, SEQ:TT], yo[:, 0:256], yo, reads=[r_y])
            S.barrier()

        with ExitStack() as pst:
            vall = sb(pst, "a_v", [128, 34, 1024], BF16)
            mkr = Ring(pst, "a_mk", [128, 8, 512], BF16, 3)
            r_v = S.res("vall")
            for q4 in range(4):
                b.dma("sp", vall[:, q4 * 8:(q4 + 1) * 8, :], Vd[q4 * 1024:(q4 + 1) * 1024].rearrange("(c p) d -> p c d", p=128), vall,
                      writes=[r_v] if q4 == 0 else (), pw=[r_v] if q4 else ())
            b.dma("sp", vall[:, 32:34, :], Vd[SEQ:TT].rearrange("(c p) d -> p c d", p=128), vall, pw=[r_v])
            qr_ = Ring(pst, "a_q", [128, TT], BF16, 2)
            kr_ = Ring(pst, "a_k", [128, TT], BF16, 2)
            rp_ = Ring(pst, "a_rp", [128, 8, 512], BF16, 2)
            pTr = Ring(pst, "a_p", [128, 512], BF16, 4)
            rdr = Ring(pst, "a_rd", [128, 512], F32, 2)
            aor = Ring(pst, "a_o", [128, 512], BF16, 3)

            rmr = Ring(pst, "a_rm", [128, 8, 512], BF16, 2)

            def load_head(h):
                q_, r_q = qr_.next(); k_, r_k = kr_.next(); rp, r_rp = rp_.next()
                b.dma("sp", q_[:], qT[h], q_, writes=[r_q])
                b.dma("sp", k_[:], kT[h], k_, writes=[r_k])
                b.dma("sp", rp[:], rpbt[l, h], rp, writes=[r_rp])
                return (q_, r_q, k_, r_k, rp, r_rp)
            qblocks = [(qb * 512, 512, qb, 0) for qb in range(4 if last else 8)]
            if not last:
                qblocks.append((SEQ, 256, -1, 0))
            else:
                qblocks += [(2048, 128, 4, 0), (3968, 128, 7, 384)]
            items = [(h,) + blk for h in range(8) for blk in qblocks]
            heads = {0: load_head(0)}
            masks = {}
            rms = {}

            def load_mask(ii):
                if ii >= len(items) or items[ii][3] < 0:
                    return
                mk, r_mk = mkr.next()
                b.dma("sp", mk[:], maskt[:, items[ii][3]], mk, writes=[r_mk])
                masks[ii] = (mk, r_mk)

            def prep_rm(ii):
                if ii >= len(items) or items[ii][3] < 0:
                    return
                h_, q0_, nq_, qb_, qoff_ = items[ii]
                mk, r_mk = masks.pop(ii)
                rp, r_rp = heads[h_][4], heads[h_][5]
                rm, r_rm = rmr.next()
                S.op("pool", lambda e, rm=rm, rp=rp, mk=mk, nq_=nq_, qoff_=qoff_: e.tensor_tensor(
                    out=rm[:, :, 0:nq_], in0=rp[:, :, qoff_:qoff_ + nq_], in1=mk[:, :, qoff_:qoff_ + nq_], op=ALU.add),
                    reads=[r_rp, r_mk], writes=[r_rm])
                rms[ii] = (rm, r_rm)
            load_mask(0)
            load_mask(1)
            prep_rm(0)
            pending = []
            for ii, (h, q0, nq, qb, qoff) in enumerate(items):
                if ii % len(qblocks) == 0 and h + 1 < 8:
                    heads[h + 1] = load_head(h + 1)
                q_, r_q, k_, r_k, rp, r_rp = heads[h]
                load_mask(ii + 2)
                prep_rm(ii + 1)
                rm, r_rm = rms.pop(ii, (None, None))
                chunks = [(SEQ, 32, -1), (SEQ + 128, 33, -1)]
                if qb >= 0:
                    for j in range(8):
                        kc_idx = (4 * qb - 2 + j) % 32
                        chunks.append((kc_idx * 128, kc_idx, j))
                acc, r_acc = PSA.next()
                den, r_den = PSA.next()

                def s_mm(ci, k_=k_, q_=q_, rm=rm, r_rm=r_rm, q0=q0, nq=nq, r_q=r_q, r_k=r_k, chunks=chunks):
                    kcol, vidx, j = chunks[ci]
                    sp_, r_sp = PS.next()
                    S.op("pe", lambda e, sp_=sp_, kcol=kcol, j=j: e.matmul(sp_[:, 0:nq], lhsT=k_[:, kcol:kcol + 128], rhs=q_[:, q0:q0 + nq], start=True, stop=(j < 0)),
                         reads=[r_q, r_k], writes=[r_sp])
                    if j >= 0:
                        S.op("pe", lambda e, sp_=sp_, j=j: e.matmul(sp_[:, 0:nq], lhsT=ident[:], rhs=rm[:, j, 0:nq], start=False, stop=True), reads=[r_rm], pw=[r_sp])
                    return sp_, r_sp
                cur = s_mm(0)
                for ci in range(len(chunks)):
                    sp_, r_sp = cur
                    if ci + 1 < len(chunks):
                        cur = s_mm(ci + 1)
                    if ci == 0 and pending:
                        pending.pop()()
                    kcol, vidx, j = chunks[ci]
                    p_, r_pp = pTr.next()
                    S.op("act", lambda e, p_=p_, sp_=sp_, nq=nq: e.activation(out=p_[:, 0:nq], in_=sp_[:, 0:nq], func=AF.Exp), reads=[r_sp], writes=[r_pp])
                    first = (ci == 0); lastc = (ci == len(chunks) - 1)
                    S.op("pe", lambda e, p_=p_, vidx=vidx, first=first, lastc=lastc, acc=acc, h=h, nq=nq: e.matmul(
                        acc[:, 0:nq], lhsT=vall[:, vidx, h * 128:(h + 1) * 128], rhs=p_[:, 0:nq], start=first, stop=lastc),
                        reads=[r_pp, r_v], writes=[r_acc] if first else (), pw=() if first else [r_acc])
                    S.op("pe", lambda e, p_=p_, first=first, lastc=lastc, den=den, nq=nq: e.matmul(
                        den[:, 0:nq], lhsT=ones_b[:], rhs=p_[:, 0:nq], start=first, stop=lastc),
                        reads=[r_pp], writes=[r_den] if first else (), pw=() if first else [r_den])

                def fin(den=den, r_den=r_den, acc=acc, r_acc=r_acc, nq=nq, q0=q0, h=h):
                    rd, r_rd = rdr.next()
                    S.op("dve", lambda e, rd=rd: e.reciprocal(out=rd[:, 0:nq], in_=den[:, 0:nq]), reads=[r_den], writes=[r_rd])
                    ao, r_ao = aor.next()
                    S.op("dve", lambda e, ao=ao, rd=rd: e.tensor_tensor(out=ao[:, 0:nq], in0=acc[:, 0:nq], in1=rd[:, 0:nq], op=ALU.mult),
                         reads=[r_acc, r_rd], writes=[r_ao])
                    b.dma("sp", attT[h][:, q0:q0 + nq], ao[:, 0:nq], ao, reads=[r_ao])
                pending.append(fin)
            while pending:
                pending.pop()()
            S.barrier()

        mgroups = [[(0, 384, 0), (384, 384, 0), (768, 384, 0)],
                   [(1152, 448, 0), (1600, 448, 0)],
                   [(2048, 384, 0), (2432, 384, 0), (2816, 384, 0)],
                   [(3200, 448, 0), (3648, 448, 0)]]
        if not last:
            mgroups[1].append((SEQ, 256, 1))
        else:
            mgroups = mgroups[:2] + [[(2048, 128, 0), (3968, 128, 0)]]
        for mg in mgroups:
            with ExitStack() as pst:
                cat = sb(pst, "m_cat", [128, KC, 1152], BF16)
                mT = sb(pst, "m_mT", [128, KC, 1152], BF16)
                ur = Ring(pst, "m_u", [128, 4, 512], BF16, 2)
                sr = Ring(pst, "m_s", [128, 4, 512], BF16, 2)
                r_cat = S.res("cat")
                loc = []
                a0 = 0
                for (g0, n, sel) in mg:
                    loc.append((a0, n, g0, sel))
                    b.dma("sp", cat[:, 0:4, a0:a0 + n], yT[:, :, g0:g0 + n].rearrange("g p t -> p g t"), cat, pw=[r_cat])
                    b.dma("sp", cat[:, 4:12, a0:a0 + n], attT[:, :, g0:g0 + n].rearrange("g p t -> p g t"), cat, pw=[r_cat])
                    u_, r_u = ur.next(); s_, r_s = sr.next()
                    b.dma("sp", u_[:, :, 0:n], uT[:, :, g0:g0 + n].rearrange("g p t -> p g t"), u_, writes=[r_u])
                    b.dma("sp", s_[:, :, 0:n], spT[:, :, g0:g0 + n].rearrange("g p t -> p g t"), s_, writes=[r_s])
                    S.op("pool", lambda e, u_=u_, s_=s_, a0=a0, n=n: e.tensor_tensor(out=cat[:, 12:16, a0:a0 + n], in0=u_[:, :, 0:n], in1=s_[:, :, 0:n], op=ALU.mult),
                         reads=[r_u, r_s], pw=[r_cat])
                    a0 += n
                wbr = Ring(pst, "m_wb", [128, KC, 256], BF16, 3)
                gtr = Ring(pst, "m_gt", [128, 3, 512], BF16, 3)
                t1r = Ring(pst, "m_t1", [128, 512], F32, 3)
                t2r = Ring(pst, "m_t2", [128, 512], F32, 3)
                xor_ = Ring(pst, "m_xo", [128, 512], F32, 3)
                xnr = Ring(pst, "m_xn", [128, 512], F32, 3)
                r_mT = S.res("mT")

                def load_w1(fb):
                    wt, r_w = wbr.next()
                    c0 = fb * 256
                    b.dma("pool", wt[:, 0:4, :], w_f_out[l][:, c0:c0 + 256].rearrange("(k p) n -> p k n", p=128), wt, writes=[r_w])
                    b.dma("pool", wt[:, 4:12, :], w_na_out[l][:, c0:c0 + 256].rearrange("(k p) n -> p k n", p=128), wt, pw=[r_w])
                    b.dma("pool", wt[:, 12:16, :], w_c_out[l][:, c0:c0 + 256].rearrange("(k p) n -> p k n", p=128), wt, pw=[r_w])
                    return wt, r_w
                nxt = load_w1(0)
                for fb in range(8):
                    wt, r_w = nxt
                    if fb + 1 < 8:
                        nxt = load_w1(fb + 1)
                    for fi in range(2):
                        fc = fb * 2 + fi
                        for (a0, n, g0, sel) in loc:
                            gt, r_g = gtr.next()
                            b.dma("sp", gt[:, :, 0:n], gT.rearrange("(s f) p t -> f p s t", s=3)[fc][:, :, g0:g0 + n], gt, writes=[r_g])
                            b.mu = getattr(b, "mu", 0) + 1
                            ring_ = PS if b.mu % 2 else PSA
                            pf, r_pf = ring_.next(); pn, r_pn = ring_.next(); pc_, r_pc = ring_.next()
                            for (pt, r_p, k0, k1) in ((pf, r_pf, 0, 4), (pn, r_pn, 4, 12), (pc_, r_pc, 12, 16)):
                                for kc in range(k0, k1):
                                    S.op("pe", lambda e, pt=pt, wt=wt, kc=kc, fi=fi, a0=a0, n=n, k0=k0, k1=k1: e.matmul(
                                        pt[:, 0:n], lhsT=wt[:, kc, fi * 128:(fi + 1) * 128], rhs=cat[:, kc, a0:a0 + n], start=(kc == k0), stop=(kc == k1 - 1)),
                                        reads=[r_w, r_cat], writes=[r_p] if kc == k0 else (), pw=() if kc == k0 else [r_p])
                            t1, r_t1 = t1r.next(); t2, r_t2 = t2r.next()
                            S.op("dve", lambda e, t1=t1, pf=pf, gt=gt, n=n: e.tensor_tensor(out=t1[:, 0:n], in0=pf[:, 0:n], in1=gt[:, 0, 0:n], op=ALU.mult),
                                 reads=[r_pf, r_g], writes=[r_t1])
                            S.op("dve", lambda e, t2=t2, pn=pn, gt=gt, n=n: e.tensor_tensor(out=t2[:, 0:n], in0=pn[:, 0:n], in1=gt[:, 1, 0:n], op=ALU.mult),
                                 reads=[r_pn, r_g], writes=[r_t2])
                            S.op("pool", lambda e, t1=t1, t2=t2, n=n: e.tensor_tensor(out=t1[:, 0:n], in0=t1[:, 0:n], in1=t2[:, 0:n], op=ALU.add),
                                 reads=[r_t2, r_t1], writes=[r_t1])
                            t3, r_t3 = t2r.next()
                            S.op("dve", lambda e, t3=t3, pc_=pc_, gt=gt, n=n: e.tensor_tensor(out=t3[:, 0:n], in0=pc_[:, 0:n], in1=gt[:, 2, 0:n], op=ALU.mult),
                                 reads=[r_pc, r_g], writes=[r_t3])
                            S.op("pool", lambda e, t1=t1, t3=t3, fc=fc, a0=a0, n=n: e.tensor_tensor(out=mT[:, fc, a0:a0 + n], in0=t1[:, 0:n], in1=t3[:, 0:n], op=ALU.add),
                                 reads=[r_t1, r_t3], pw=[r_mT])

                def load_w2(fb):
                    wt, r_w = wbr.next()
                    b.dma("pool", wt[:], w_o[l][:, fb * 256:(fb + 1) * 256].rearrange("(k p) n -> p k n", p=128), wt, writes=[r_w])
                    return wt, r_w
                nxt = load_w2(0)
                for fb in range(8):
                    wt, r_w = nxt
                    if fb + 1 < 8:
                        nxt = load_w2(fb + 1)
                    for fi in range(2):
                        fc = fb * 2 + fi
                        for (a0, n, g0, sel) in loc:
                            xo, r_xo = xor_.next()
                            b.dma("sp", xo[:, 0:n], x_cur[fc][:, g0:g0 + n], xo, writes=[r_xo])
                            pt, r_p = PS.next()
                            for kc in range(KC):
                                S.op("pe", lambda e, pt=pt, wt=wt, kc=kc, fi=fi, a0=a0, n=n: e.matmul(
                                    pt[:, 0:n], lhsT=wt[:, kc, fi * 128:(fi + 1) * 128], rhs=mT[:, kc, a0:a0 + n], start=(kc == 0), stop=(kc == KC - 1)),
                                    reads=[r_w, r_mT], writes=[r_p] if kc == 0 else (), pw=[r_p] if kc else ())
                            xn, r_xn = xnr.next()
                            S.op("dve", lambda e, xn=xn, pt=pt, xo=xo, fc=fc, sel=sel, n=n: e.scalar_tensor_tensor(
                                out=xn[:, 0:n], in0=pt[:, 0:n], scalar=G1[:, fc, sel:sel + 1], in1=xo[:, 0:n], op0=ALU.mult, op1=ALU.add),
                                reads=[r_p, r_xo], writes=[r_xn])
                            b.dma("sp", xT_mid[fc][:, g0:g0 + n], xn[:, 0:n], xn, reads=[r_xn])
                S.barrier()

        for grp in range(1 if last else 2):
            srcL, srcR = (SEQ - 1, 2048) if grp == 0 else (2047, 0)
            fL, fR = (0, 1) if grp == 0 else (1, 0)
            ntiles = [(srcL, 1, 0, 0)] + [(grp * 2048 + t, 512, 1 + t, 0) for t in range(0, 2048, 512)] + [(srcR, 1, 2049, 0)]
            if grp == 0 and not last:
                ntiles.append((SEQ, 256, 2050, 1))
            ncols = 2306 if (grp == 0 and not last) else 2050
            with ExitStack() as gst:
                actA = sb(gst, "actB", [128, KC, 2306], BF16)
                r_act = S.res("actB")
                with ExitStack() as pst:
                    norm_phase(pst, xT_mid, actA, r_act, ntiles, A2, SH2)
                    S.barrier()
                r_act = S.res("actB")
                ntl = (ncols + 511) // 512
                bnd = [(ncols * i) // ntl for i in range(ntl + 1)]
                mtiles = [(bnd[i], bnd[i + 1] - bnd[i]) for i in range(ntl)]
                with ExitStack() as pst:
                    war = Ring(pst, "u_wa", [128, KC, 128], BF16, 3)
                    wgr = Ring(pst, "u_wg", [128, KC, 128], BF16, 3)
                    arr = Ring(pst, "u_ar", [128, 2308], F32, 2)
                    grr = Ring(pst, "u_gr", [128, 2308], F32, 2)
                    yr = Ring(pst, "u_y", [128, 2308], F32, 2)
                    hr = Ring(pst, "u_h", [128, 2304], BF16, 2)
                    cwt = sb(pst, "u_cw", [128, JF, 3], F32)
                    cbt = sb(pst, "u_cb", [128, JF], F32)
                    r_cc = S.res("cc")
                    b.dma("sp", cwt[:], cw[l], cwt, writes=[r_cc])
                    b.dma("sp", cbt[:], cb[l], cbt, pw=[r_cc])
                    for _i in range(arr.n):
                        ar_t, r_art = arr.next()
                        S.op("pool", lambda e, ar_t=ar_t: e.memset(ar_t[:], 0.0), writes=[r_art])

                    def load_w(j):
                        wa, r_wa = war.next(); wg, r_wg = wgr.next()
                        b.dma("pool", wa[:], ffn_up[l][:, j * 128:(j + 1) * 128].rearrange("(k p) n -> p k n", p=128), wa, writes=[r_wa])
                        b.dma("pool", wg[:], ffn_up[l][:, DFF + j * 128:DFF + (j + 1) * 128].rearrange("(k p) n -> p k n", p=128), wg, writes=[r_wg])
                        return wa, r_wa, wg, r_wg
                    nxt = load_w(0)
                    for j in range(JF):
                        wa, r_wa, wg, r_wg = nxt
                        if j + 1 < JF:
                            nxt = load_w(j + 1)
                        ar, r_ar = arr.next()
                        gr, r_gr = grr.next()
                        for (t0, n) in mtiles:
                            for which, (wt, r_w, dst, r_d) in enumerate(((wa, r_wa, ar, r_ar), (wg, r_wg, gr, r_gr))):
                                pt, r_p = PS.next()
                                for kc in range(KC):
                                    S.op("pe", lambda e, pt=pt, wt=wt, kc=kc, t0=t0, n=n: e.matmul(
                                        pt[:, 0:n], lhsT=wt[:, kc, :], rhs=actA[:, kc, t0:t0 + n], start=(kc == 0), stop=(kc == KC - 1)),
                                        reads=[r_w, r_act], writes=[r_p] if kc == 0 else (), pw=[r_p] if kc else ())
                                segs = []
                                lo, hi = t0, t0 + n
                                m_hi = min(hi, 2050)
                                if lo < m_hi:
                                    segs.append((lo - t0, m_hi - lo, lo))
                                if hi > 2050:
                                    c_lo = max(lo, 2050)
                                    segs.append((c_lo - t0, hi - c_lo, c_lo + 1))
                                for (p0, sn_, d0) in segs:
                                    if which == 0:
                                        S.op("act", lambda e, pt=pt, dst=dst, p0=p0, sn_=sn_, d0=d0: e.activation(out=dst[:, d0:d0 + sn_], in_=pt[:, p0:p0 + sn_], func=AF.Copy),
                                             reads=[r_p], pw=[r_d])
                                    else:
                                        S.op("dve", lambda e, pt=pt, dst=dst, p0=p0, sn_=sn_, d0=d0: e.tensor_copy(out=dst[:, d0:d0 + sn_], in_=pt[:, p0:p0 + sn_]),
                                             reads=[r_p], pw=[r_d])
                        S.op("dve", lambda e, ar=ar, fL=fL: e.tensor_scalar(out=ar[:, 0:1], in0=ar[:, 0:1], scalar1=flg[:, fL:fL + 1], scalar2=0.0, op0=ALU.mult, op1=ALU.add),
                             reads=[r_ar], writes=[r_ar])
                        S.op("dve", lambda e, ar=ar, fR=fR: e.tensor_scalar(out=ar[:, 2049:2050], in0=ar[:, 2049:2050], scalar1=flg[:, fR:fR + 1], scalar2=0.0, op0=ALU.mult, op1=ALU.add),
                             reads=[r_ar], writes=[r_ar])
                        hi_c = 2307 if (grp == 0 and not last) else 2049
                        y_, r_y = yr.next()
                        W = hi_c - 1
                        S.op("dve", lambda e, y_=y_, ar=ar, j=j, W=W: e.tensor_scalar(out=y_[:, 1:1 + W], in0=ar[:, 1:1 + W], scalar1=cwt[:, j, 1:2], scalar2=cbt[:, j:j + 1], op0=ALU.mult, op1=ALU.add),
                             reads=[r_ar, r_cc], writes=[r_y])
                        S.op("dve", lambda e, y_=y_, ar=ar, j=j, W=W: e.scalar_tensor_tensor(out=y_[:, 1:1 + W], in0=ar[:, 0:W], scalar=cwt[:, j, 0:1], in1=y_[:, 1:1 + W], op0=ALU.mult, op1=ALU.add),
                             reads=[r_ar, r_y], writes=[r_y])
                        S.op("dve", lambda e, y_=y_, ar=ar, j=j, W=W: e.scalar_tensor_tensor(out=y_[:, 1:1 + W], in0=ar[:, 2:2 + W], scalar=cwt[:, j, 2:3], in1=y_[:, 1:1 + W], op0=ALU.mult, op1=ALU.add),
                             reads=[r_ar, r_y], writes=[r_y])
                        S.op("act", lambda e, y_=y_, W=W: e.activation(out=y_[:, 1:1 + W], in_=y_[:, 1:1 + W], func=AF.Silu), reads=[r_y], writes=[r_y])
                        h_, r_h = hr.next()
                        S.op("pool", lambda e, h_=h_, y_=y_, gr=gr: e.tensor_tensor(out=h_[:, 0:2048], in0=y_[:, 1:2049], in1=gr[:, 1:2049], op=ALU.mult),
                             reads=[r_y, r_gr], writes=[r_h])
                        b.dma("sp", hmT[j][:, grp * 2048:(grp + 1) * 2048], h_[:, 0:2048], h_, reads=[r_h])
                        if grp == 0 and not last:
                            S.op("dve", lambda e, h_=h_, y_=y_, gr=gr: e.tensor_tensor(out=h_[:, 2048:2304], in0=y_[:, 2051:2307], in1=gr[:, 2051:2307], op=ALU.mult),
                                 reads=[r_y, r_gr], pw=[r_h])
                            b.dma("sp", hmT[j][:, SEQ:TT], h_[:, 2048:2304], h_, reads=[r_h])
                    S.barrier()

        x_next = xT_nxt if not last else xT_fin
        dgroups = [[(g * 1024, 512, 0), (g * 1024 + 512, 512, 0)] for g in range(4)]
        if not last:
            dgroups.append([(SEQ, 256, 1)])
        else:
            dgroups = dgroups[:2]
        for dg in dgroups:
            with ExitStack() as pst:
                hm = sb(pst, "d_hm", [128, JF, 1024], BF16)
                r_hm = S.res("hm")
                loc = []
                a0 = 0
                for (g0, n, sel) in dg:
                    loc.append((a0, n, g0, sel))
                    for q4 in range(4):
                        b.dma("sp", hm[:, q4 * 11:(q4 + 1) * 11, a0:a0 + n], hmT[q4 * 11:(q4 + 1) * 11, :, g0:g0 + n].rearrange("j p t -> p j t"), hm, pw=[r_hm])
                    a0 += n
                wdr = Ring(pst, "d_w", [128, JF, 256], BF16, 2)
                xor_ = Ring(pst, "d_xo", [128, 512], F32, 3)
                xnr = Ring(pst, "d_xn", [128, 512], F32, 3)

                def load_wd(fb):
                    wt, r_w = wdr.next()
                    for q4 in range(4):
                        b.dma("pool", wt[:, q4 * 11:(q4 + 1) * 11, :],
                              ffn_down[l][q4 * 11 * 128:(q4 + 1) * 11 * 128, fb * 256:(fb + 1) * 256].rearrange("(k p) n -> p k n", p=128), wt,
                              writes=[r_w] if q4 == 0 else (), pw=[r_w] if q4 else ())
                    return wt, r_w
                nxt = load_wd(0)
                for fb in range(8):
                    wt, r_w = nxt
                    if fb + 1 < 8:
                        nxt = load_wd(fb + 1)
                    for fi in range(2):
                        fc = fb * 2 + fi
                        for (a0, n, g0, sel) in loc:
                            xo, r_xo = xor_.next()
                            b.dma("sp", xo[:, 0:n], xT_mid[fc][:, g0:g0 + n], xo, writes=[r_xo])
                            pt, r_p = PS.next()
                            for kc in range(JF):
                                S.op("pe", lambda e, pt=pt, wt=wt, kc=kc, fi=fi, a0=a0, n=n: e.matmul(
                                    pt[:, 0:n], lhsT=wt[:, kc, fi * 128:(fi + 1) * 128], rhs=hm[:, kc, a0:a0 + n], start=(kc == 0), stop=(kc == JF - 1)),
                                    reads=[r_w, r_hm], writes=[r_p] if kc == 0 else (), pw=[r_p] if kc else ())
                            xn, r_xn = xnr.next()
                            S.op("dve", lambda e, xn=xn, pt=pt, xo=xo, fc=fc, sel=sel, n=n: e.scalar_tensor_tensor(
                                out=xn[:, 0:n], in0=pt[:, 0:n], scalar=G2[:, fc, sel:sel + 1], in1=xo[:, 0:n], op0=ALU.mult, op1=ALU.add),
                                reads=[r_p, r_xo], writes=[r_xn])
                            b.dma("sp", x_next[fc][:, g0:g0 + n], xn[:, 0:n], xn, reads=[r_xn])
                S.barrier()
        x_cur = x_next

    with ExitStack() as pst:
        tiles = [(t, 512, t, 0) for t in range(0, 2048, 512)]
        norm_phase(pst, x_cur, None, None, tiles, fnw_sb, None, dst=outT)
        S.flush(final=True)


_CONST = {}


def _consts():
    if _CONST:
        return _CONST
    bf = ml_dtypes.bfloat16
    c = np.arange(128)
    ang = 2 * np.pi * np.outer(c, c) / 128.0
    _CONST["cdsd"] = np.concatenate([np.cos(ang), np.sin(ang)], 1).astype(np.float32) / np.sqrt(128.0)
    _CONST["cdsd"] = _CONST["cdsd"].astype(bf)
    n = np.arange(SEQ, dtype=np.int64)
    m = (np.outer(n, n) % SEQ).astype(np.float64) * (2 * np.pi / SEQ)
    _CONST["cn"] = (np.cos(m) / np.sqrt(SEQ)).astype(np.float32).astype(bf)
    _CONST["sn"] = (-np.sin(m) / np.sqrt(SEQ)).astype(np.float32).astype(bf)
    n = np.arange(NCTX, dtype=np.int64)
    m = (np.outer(n, n) % NCTX).astype(np.float64) * (2 * np.pi / NCTX)
    _CONST["cnc"] = (np.cos(m) / np.sqrt(NCTX)).astype(np.float32).astype(bf)
    _CONST["snc"] = (-np.sin(m) / np.sqrt(NCTX)).astype(np.float32).astype(bf)
    _CONST["ident"] = np.eye(128, dtype=np.float32).astype(bf)
    krr = np.arange(2)[:, None, None, None, None, None]
    kc = np.arange(64)[None, :, None, None, None, None]
    qb = np.arange(8)[None, None, :, None, None, None]
    j = np.arange(8)[None, None, None, :, None, None]
    qr = np.arange(8)[None, None, None, None, :, None]
    qc = np.arange(64)[None, None, None, None, None, :]
    R = 8 * qb + qr
    Kr = 8 * qb - 4 + 2 * j + krr
    rs = np.clip(R - 4, 0, 56)
    cs = np.clip(qc - 8, 0, 48)
    valid = (Kr >= 0) & (Kr <= 63) & (Kr >= rs) & (Kr <= rs + 7) & (kc >= cs) & (kc < cs + 16)
    mfull = np.where(valid, 0.0, NEG).astype(np.float32).reshape(128, 8, 8, 512).astype(bf)
    _CONST["maskt"] = [np.ascontiguousarray(np.roll(mfull, -4 * hf, axis=1)) for hf in range(2)]
    _CONST["cn_h"] = [_CONST["cn"], np.ascontiguousarray(np.roll(_CONST["cn"], (-2048, -2048), axis=(0, 1)))]
    _CONST["sn_h"] = [_CONST["sn"], np.ascontiguousarray(np.roll(_CONST["sn"], (-2048, -2048), axis=(0, 1)))]
    dr = np.clip(2 * j + krr - qr + 3, 0, 14)
    dc = np.clip(kc - qc + 15, 0, 30)
    dr, dc = np.broadcast_arrays(dr[:, :, 0], dc[:, :, 0])
    _CONST["dr"] = dr.reshape(128, 8, 512)
    _CONST["dc"] = dc.reshape(128, 8, 512)
    return _CONST


_NC_CACHE = {}


def _fm(v, rep=None):
    a = np.asarray(v, np.float32)
    a = a.reshape(a.shape[:-1] + (KC, 128)).swapaxes(-1, -2)
    if rep:
        a = np.repeat(a[..., None], rep, axis=-1)
    return np.ascontiguousarray(a)


def kernel(x, c, ctx, c_ctx, ada_w, ada_b, norm1_w, norm2_w, w_in, na_rpb, gmlp_norm_w, gmlp_ws, gmlp_bs,
           w_f_out, w_na_out, w_c_out, w_o, ffn_up, ffn_conv_w, ffn_conv_b, ffn_down, final_norm_w, _dbg=None, _nl=DEPTH):
    bf = ml_dtypes.bfloat16
    K = _consts()
    f32 = lambda a: np.ascontiguousarray(np.asarray(a, np.float32))
    x = f32(x); ctx = f32(ctx); c = f32(c); c_ctx = f32(c_ctx)
    na_rpb = f32(na_rpb)
    shared = {
        "ada_w": f32(ada_w),
        "ada_bT": np.ascontiguousarray(np.repeat(f32(ada_b).reshape(DEPTH, 96, 128).swapaxes(1, 2)[..., None], 2, axis=-1)),
        "nw1": _fm(norm1_w, 2), "nw2": _fm(norm2_w, 2), "fnw": _fm(final_norm_w),
        "w_in": f32(w_in),
        "rpbt": np.ascontiguousarray(na_rpb[:, :, K["dr"], K["dc"]]).astype(bf),
        "gnw": np.ascontiguousarray(f32(gmlp_norm_w).reshape(DEPTH, 4, 128).swapaxes(1, 2)),
        "wsT": np.ascontiguousarray(f32(gmlp_ws).transpose(0, 3, 1, 2)),
        "bsb": np.ascontiguousarray(np.broadcast_to(f32(gmlp_bs)[:, None, :, :], (DEPTH, 128, 4, 128))),
        "w_f_out": f32(w_f_out), "w_na_out": f32(w_na_out), "w_c_out": f32(w_c_out), "w_o": f32(w_o),
        "ffn_up": f32(ffn_up),
        "cw": np.ascontiguousarray(f32(ffn_conv_w).reshape(DEPTH, 3, JF, 128).transpose(0, 3, 2, 1)),
        "cb": np.ascontiguousarray(f32(ffn_conv_b).reshape(DEPTH, JF, 128).swapaxes(1, 2)),
        "ffn_down": f32(ffn_down),
        "cdsd": K["cdsd"], "cnc": K["cnc"], "snc": K["snc"], "ident": K["ident"],
    }
    key = (tuple(_dbg) if _dbg else (), _nl)
    if key not in _NC_CACHE:
        _NC_CACHE[key] = build(dbg=_dbg, n_layers=_nl)
    bb = _NC_CACHE[key]
    in_maps = []
    for core in range(8):
        bi, hf = core // 2, core % 2
        xl = np.roll(x[bi], -2048 * hf, axis=0)
        xt = np.concatenate([xl.T, ctx[bi].T], axis=1).reshape(KC, 128, TT)
        cT = np.stack([c[bi].reshape(KC, 128).T, c_ctx.reshape(KC, 128).T], axis=-1)
        m = dict(shared)
        m["xT"] = np.ascontiguousarray(xt)
        m["cT"] = np.ascontiguousarray(cT)
        m["cn"] = K["cn_h"][hf]; m["sn"] = K["sn_h"][hf]; m["maskt"] = K["maskt"][hf]
        fl = np.zeros((128, 2), np.float32); fl[:, 0] = hf; fl[:, 1] = 1 - hf
        m["flags"] = fl
        in_maps.append(m)
    res = run_bass_kernel_spmd(bb.nc, in_maps, core_ids=list(range(8)))
    out = np.empty((4, SEQ, D), np.float32)
    for core in range(8):
        bi, hf = core // 2, core % 2
        out[bi, hf * 2048:(hf + 1) * 2048] = res.results[core]["outT"].reshape(D, 2048).T
    if _dbg:
        return out, res
    return out
```

```python
import numpy as np
import ml_dtypes
from contextlib import ExitStack
import concourse.bass as bass
import concourse.mybir as mybir
from concourse.bass_utils import run_bass_kernel_spmd

F32 = mybir.dt.float32
BF16 = mybir.dt.bfloat16
AF = mybir.ActivationFunctionType
ALU = mybir.AluOpType
AX = mybir.AxisListType

D = 2048
KC = 16
SEQ = 4096
NCTX = 256
TT = SEQ + NCTX
IN_W = 10752
DFF = 5632
JF = 44
EPS = 1e-6
DEPTH = 2
NEG = -30000.0


class Res:
    __slots__ = ("name", "writers", "readers")

    def __init__(self, name):
        self.name = name
        self.writers = []
        self.readers = []


class Op:
    __slots__ = ("eng", "fn", "deps", "is_dma", "slot", "needed", "count")


class Sched:
    ENG = ("pe", "act", "dve", "pool", "sp")
    ATTR = {"pe": "tensor", "act": "scalar", "dve": "vector", "pool": "gpsimd", "sp": "sync"}

    def __init__(self, nc, stack):
        self.nc = nc
        self.stack = stack
        self.seg = {e: [] for e in self.ENG}
        self.segall = []
        self.resources = []
        self.esem = {e: stack.enter_context(nc.semaphore("s_" + e)) for e in self.ENG}
        self.ecount = {e: 0 for e in self.ENG}
        self.waited = {e: {} for e in self.ENG}
        self.slot_of = {}
        self.free_slots = []
        self.all_slots = []
        self.pending = {e: [] for e in self.ENG}
        self.last = {e: None for e in self.ENG}
        self.n_ops = 0

    def res(self, name="r"):
        r = Res(name)
        self.resources.append(r)
        return r

    def op(self, eng, fn, reads=(), writes=(), pw=(), dma=None):
        o = Op()
        o.eng = eng; o.fn = fn; o.is_dma = dma is not None; o.slot = dma
        o.needed = False; o.count = None
        deps = list(self.pending[eng])
        self.pending[eng] = []
        for r in reads:
            deps.extend(r.writers)
        for w in writes:
            deps.extend(w.writers)
            deps.extend(w.readers)
        for w in pw:
            deps.extend(w.readers)
            if w.writers:
                deps.append(w.writers[0])
        o.deps = deps
        for r in reads:
            r.readers.append(o)
        for w in writes:
            w.writers = [o]
            w.readers = []
        for w in pw:
            w.writers.append(o)
        self.segall.append(o)
        self.seg[eng].append(o)
        if not o.is_dma:
            self.last[eng] = o
        self.n_ops += 1
        return o

    def _slot(self, key):
        s = self.slot_of.get(id(key))
        if s is None:
            if self.free_slots:
                s = self.free_slots.pop()
            else:
                s = [self.stack.enter_context(self.nc.semaphore("d%d" % len(self.all_slots))), 0, None]
                self.all_slots.append(s)
            self.slot_of[id(key)] = s
        return s

    def barrier(self):
        self.flush(barrier=True)

    def flush(self, barrier=False, final=False):
        nc = self.nc
        bar = []
        if barrier or final:
            for e in self.ENG:
                if self.last[e] is not None:
                    self.last[e].needed = True
        for o in self.segall:
            for d in o.deps:
                if d.is_dma:
                    continue
                if d.eng == o.eng and o.eng == "pe" and not o.is_dma:
                    continue
                d.needed = True
        for o in self.segall:
            if o.count is not None:
                continue
            if o.is_dma:
                s = self._slot(o.slot)
                s[1] += 16
                s[2] = o
                o.count = (s[0], s[1])
            elif o.needed:
                self.ecount[o.eng] += 1
                o.count = (self.esem[o.eng], self.ecount[o.eng])
        segs = self.seg
        extra = []
        if barrier or final:
            for e in self.ENG:
                if self.last[e] is not None and self.last[e].count is not None:
                    extra.append(self.last[e].count)
            for s in self.all_slots:
                if s[1] > 0:
                    extra.append((s[0], s[1]))

        def emit(ename):
            def body(eng):
                waited = self.waited[ename]
                for o in segs[ename]:
                    for d in o.deps:
                        c = d.count
                        if c is None:
                            continue
                        if d.eng == ename and ename == "pe" and not d.is_dma and not o.is_dma:
                            continue
                        if waited.get(id(c[0]), 0) >= c[1]:
                            continue
                        waited[id(c[0])] = c[1]
                        eng.wait_ge(c[0], c[1])
                    ins = o.fn(eng)
                    if o.count is not None:
                        ins.then_inc(o.count[0], 16 if o.is_dma else 1)
                for sem, val in extra:
                    if waited.get(id(sem), 0) < val:
                        waited[id(sem)] = val
                        eng.wait_ge(sem, val)
            return body
        with nc.Block() as block:
            for e in self.ENG:
                getattr(block, self.ATTR[e])(emit(e))
        self.seg = {e: [] for e in self.ENG}
        self.segall = []
        if barrier or final:
            for r in self.resources:
                r.writers = []
                r.readers = []
            self.resources = []
            self.slot_of = {}
            self.free_slots = list(self.all_slots)
            self.last = {e: None for e in self.ENG}


class B:
    def __init__(self, dbg=False):
        self.dbg = dbg
        self.nc = bass.Bass("TRN2", target_bir_lowering=False)
        self.st = ExitStack()
        self.S = None
        self.ins = {}
        self.psum = None
        self.psi = 0

    def inp(self, name, shape, dt=F32):
        t = self.nc.dram_tensor(name, list(shape), dt, kind="ExternalInput").ap()
        self.ins[name] = t
        return t

    def scratch(self, name, shape, dt):
        kind = "ExternalOutput" if (self.dbg and name in self.dbg) else "Internal"
        return self.nc.dram_tensor(name, list(shape), dt, kind=kind).ap()

    def sb(self, stack, name, shape, dt):
        self.uid = getattr(self, "uid", 0) + 1
        return stack.enter_context(self.nc.sbuf_tensor("%s_%d" % (name, self.uid), list(shape), dt))

    def ps(self):
        i = self.psi
        self.psi = (self.psi + 1) % len(self.psum)
        return self.psum[i]

    def dma(self, q, out, in_, slot, reads=(), writes=(), pw=(), slow=False):
        if slow:
            return self.S.op(q, lambda e: e.dma_start(out=out, in_=in_, allow_slow_non_contiguous=True), reads=reads, writes=writes, pw=pw, dma=slot)
        return self.S.op(q, lambda e: e.dma_start(out=out, in_=in_), reads=reads, writes=writes, pw=pw, dma=slot)


def build(dbg=None, n_layers=DEPTH):
    b = B(dbg=dbg or ())
    b.n_layers = n_layers
    nc = b.nc
    st = b.st
    with st:
        _build(b)
    return b


def _build(b):
    nc, st = b.nc, b.st
    S = b.S = Sched(nc, st)
    inp = b.inp

    xT_in = inp("xT", [KC, 128, TT])
    cT = inp("cT", [128, KC, 2])
    ada_w = inp("ada_w", [DEPTH, D, 6 * D])
    ada_b = inp("ada_bT", [DEPTH, 128, 96, 2])
    nw1 = inp("nw1", [DEPTH, 128, KC, 2])
    nw2 = inp("nw2", [DEPTH, 128, KC, 2])
    fnw = inp("fnw", [128, KC])
    w_in = inp("w_in", [DEPTH, D, IN_W])
    rpbt = inp("rpbt", [DEPTH, 8, 128, 8, 512], BF16)
    maskt = inp("maskt", [128, 8, 8, 512], BF16)
    flags_in = inp("flags", [128, 2])
    gnw = inp("gnw", [DEPTH, 128, 4])
    wsT = inp("wsT", [DEPTH, 128, 4, 128])
    bsb = inp("bsb", [DEPTH, 128, 4, 128])
    w_f_out = inp("w_f_out", [DEPTH, 512, D])
    w_na_out = inp("w_na_out", [DEPTH, 1024, D])
    w_c_out = inp("w_c_out", [DEPTH, 512, D])
    w_o = inp("w_o", [DEPTH, D, D])
    ffn_up = inp("ffn_up", [DEPTH, D, 2 * DFF])
    cw = inp("cw", [DEPTH, 128, JF, 3])
    cb = inp("cb", [DEPTH, 128, JF])
    ffn_down = inp("ffn_down", [DEPTH, DFF, D])
    cdsd = inp("cdsd", [128, 256], BF16)
    cn = inp("cn", [SEQ, SEQ], BF16)
    sn = inp("sn", [SEQ, SEQ], BF16)
    cnc = inp("cnc", [NCTX, NCTX], BF16)
    snc = inp("snc", [NCTX, NCTX], BF16)
    ident_in = inp("ident", [128, 128], BF16)
    outT = nc.dram_tensor("outT", [KC, 128, 2048], F32, kind="ExternalOutput").ap()

    sc = b.scratch
    xT_mid = sc("xT_mid", [KC, 128, TT], F32)
    xT_nxt = sc("xT_nxt", [KC, 128, TT], F32)
    xT_fin = sc("xT_fin", [KC, 128, TT], F32)
    xcs = sc("xcs", [TT, 4, 256], BF16)
    qT = sc("qT", [8, 128, TT], BF16)
    kT = sc("kT", [8, 128, TT], BF16)
    Vd = sc("Vd", [TT, 1024], BF16)
    uT = sc("uT", [4, 128, TT], BF16)
    spT = sc("spT", [4, 128, TT], BF16)
    gT = sc("gT", [48, 128, TT], BF16)
    yT = sc("yT", [4, 128, TT], BF16)
    attT = sc("attT", [8, 128, TT], BF16)
    hmT = sc("hmT", [JF, 128, TT], BF16)

    sb = b.sb
    ident = sb(st, "ident_sb", [128, 128], BF16)
    ones_f = sb(st, "ones_f", [128, 128], F32)
    ones_b = sb(st, "ones_b", [128, 128], BF16)
    scT = sb(st, "scT", [128, KC, 2], F32)
    modT = sb(st, "modT", [128, 96, 2], F32)
    A1 = sb(st, "A1", [128, KC, 2], F32)
    A2 = sb(st, "A2", [128, KC, 2], F32)
    fnw_sb = sb(st, "fnw_sb", [128, KC], F32)
    ones16 = sb(st, "ones16", [128, KC], F32)
    zero16 = sb(st, "zero16", [128, KC], F32)
    flg = sb(st, "flg", [128, 2], F32)
    b.psum = [st.enter_context(nc.psum_tensor("psb%d" % i, [128, 512], F32)) for i in range(8)]
    r_psum = None

    def newres(n=1, name="r"):
        return [S.res(name) for _ in range(n)] if n > 1 else S.res(name)

    class Ring:
        def __init__(self, stack, name, shape, dt, n):
            self.t = [sb(stack, "%s%d" % (name, i), shape, dt) for i in range(n)]
            self.r = [None] * n
            self.i = 0
            self.n = n
            self.name = name

        def next(self):
            i = self.i
            self.i = (i + 1) % self.n
            if self.r[i] is None or self.r[i] not in S.resources:
                self.r[i] = S.res(self.name)
            return self.t[i], self.r[i]

    class PsRing:
        def __init__(self, banks):
            self.banks = banks
            self.r = [None] * len(banks)
            self.i = 0

        def next(self):
            i = self.i
            self.i = (i + 1) % len(self.banks)
            if self.r[i] is None or self.r[i] not in S.resources:
                self.r[i] = S.res("ps")
            return b.psum[self.banks[i]], self.r[i]

    PS = PsRing([0, 1, 2, 3])
    PSA = PsRing([4, 5, 6, 7])

    r_const = S.res("const")
    b.dma("sp", ident[:], ident_in, ident, writes=[r_const])
    S.op("dve", lambda e: e.memset(ones_f[:], 1.0), pw=[r_const])
    S.op("dve", lambda e: e.memset(ones_b[:], 1.0), pw=[r_const])
    S.op("dve", lambda e: e.memset(ones16[:], 1.0), pw=[r_const])
    S.op("dve", lambda e: e.memset(zero16[:], 0.0), pw=[r_const])
    b.dma("sp", scT[:], cT, scT, pw=[r_const])
    b.dma("sp", fnw_sb[:], fnw, fnw_sb, pw=[r_const])
    b.dma("sp", flg[:], flags_in, flg, pw=[r_const])
    S.barrier()
    r_c2 = S.res("c2")
    S.op("act", lambda e: e.activation(out=scT[:], in_=scT[:], func=AF.Silu), writes=[r_c2])
    S.barrier()

    def lat_tiles(lo, hi, step=512):
        return [(t, min(step, hi - t), 0) for t in range(lo, hi, step)]

    def norm_phase(pst, src, actA, r_act, tiles, Asc, Bsh, mul_only=False, dst=None):
        xr = Ring(pst, "n_x", [128, KC, 512], F32, 2)
        sq = Ring(pst, "n_sq", [128, 512], F32, 3)
        rs = Ring(pst, "n_rs", [128, 512], F32, 2)
        tm = Ring(pst, "n_tm", [128, 512], F32, 3)
        og = Ring(pst, "n_o", [128, KC, 512], F32, 1) if dst is not None else None
        for (c0, n, d0, sel) in tiles:
            xt, r_x = xr.next()
            b.dma("sp", xt[:, :, 0:n], src[:, :, c0:c0 + n].rearrange("k p t -> p k t"), xt, writes=[r_x], slow=(n == 1))
            pt, r_p = PS.next()
            for kc in range(KC):
                s_, r_s = sq.next()
                S.op("act", lambda e, s_=s_, xt=xt, kc=kc, n=n: e.activation(out=s_[:, 0:n], in_=xt[:, kc, 0:n], func=AF.Square),
                     reads=[r_x], writes=[r_s])
                S.op("pe", lambda e, pt=pt, s_=s_, kc=kc, n=n: e.matmul(pt[:, 0:n], lhsT=ones_f[:], rhs=s_[:, 0:n], start=(kc == 0), stop=(kc == KC - 1)),
                     reads=[r_s], writes=[r_p] if kc == 0 else (), pw=[r_p] if kc else ())
            rt, r_r = rs.next()
            S.op("dve", lambda e, rt=rt, pt=pt, n=n: e.tensor_scalar(out=rt[:, 0:n], in0=pt[:, 0:n], scalar1=1.0 / D, scalar2=EPS, op0=ALU.mult, op1=ALU.add),
                 reads=[r_p], writes=[r_r])
            S.op("act", lambda e, rt=rt, n=n: e.activation(out=rt[:, 0:n], in_=rt[:, 0:n], func=AF.Sqrt), reads=[r_r], writes=[r_r])
            S.op("dve", lambda e, rt=rt, n=n: e.reciprocal(out=rt[:, 0:n], in_=rt[:, 0:n]), reads=[r_r], writes=[r_r])
            if dst is not None:
                ot, r_o = og.next()
            for kc in range(KC):
                t_, r_t = tm.next()
                S.op("dve", lambda e, t_=t_, xt=xt, rt=rt, kc=kc, n=n: e.tensor_tensor(out=t_[:, 0:n], in0=xt[:, kc, 0:n], in1=rt[:, 0:n], op=ALU.mult),
                     reads=[r_x, r_r], writes=[r_t])
                if dst is None:
                    S.op("act", lambda e, t_=t_, kc=kc, n=n, d0=d0, sel=sel: e.activation(
                        out=actA[:, kc, d0:d0 + n], in_=t_[:, 0:n], func=AF.Identity,
                        bias=Bsh[:, kc, sel:sel + 1], scale=Asc[:, kc, sel:sel + 1]),
                        reads=[r_t], pw=[r_act])
                else:
                    S.op("act", lambda e, t_=t_, kc=kc, n=n, ot=ot: e.activation(
                        out=ot[:, kc, 0:n], in_=t_[:, 0:n], func=AF.Identity, scale=Asc[:, kc:kc + 1]),
                        reads=[r_t], pw=[r_o])
            if dst is not None:
                b.dma("sp", dst[:, :, d0:d0 + n].rearrange("k p t -> p k t"), ot[:, :, 0:n], ot, reads=[r_o])

    x_cur = xT_in
    for l in range(b.n_layers):
        last = (l == DEPTH - 1)
        with ExitStack() as pst:
            wr = Ring(pst, "ada_wb", [128, KC, 512], F32, 2)
            adab = sb(pst, "adab", [128, 96, 2], F32)
            n1 = sb(pst, "n1", [128, KC, 2], F32)
            n2 = sb(pst, "n2", [128, KC, 2], F32)
            r_ab = S.res("adab")
            b.dma("sp", adab[:], ada_b[l], adab, writes=[r_ab])
            b.dma("sp", n1[:], nw1[l], n1, pw=[r_ab])
            b.dma("sp", n2[:], nw2[l], n2, pw=[r_ab])
            pm, r_pm = PS.next()
            r_mod = S.res("mod")
            for blk in range(24):
                wt, r_w = wr.next()
                b.dma("sp", wt[:], ada_w[l][:, blk * 512:(blk + 1) * 512].rearrange("(k p) n -> p k n", p=128), wt, writes=[r_w])
                for fi in range(4):
                    fc = blk * 4 + fi
                    for kc in range(KC):
                        S.op("pe", lambda e, wt=wt, fi=fi, fc=fc, kc=kc: e.matmul(
                            pm[:, fc * 2:fc * 2 + 2], lhsT=wt[:, kc, fi * 128:(fi + 1) * 128], rhs=scT[:, kc, :],
                            start=(kc == 0), stop=(kc == KC - 1)), reads=[r_w], pw=[r_pm])
            S.op("dve", lambda e: e.tensor_tensor(out=modT[:].rearrange("p a b -> p (a b)"), in0=pm[:, 0:192],
                                                  in1=adab[:].rearrange("p a b -> p (a b)"), op=ALU.add),
                 reads=[r_pm, r_ab], writes=[r_mod])
            S.op("dve", lambda e: e.scalar_tensor_tensor(out=A1[:], in0=modT[:, 16:32, :], scalar=1.0, in1=n1[:], op0=ALU.add, op1=ALU.mult),
                 reads=[r_mod, r_ab], writes=[S.res()])
            S.op("dve", lambda e: e.scalar_tensor_tensor(out=A2[:], in0=modT[:, 64:80, :], scalar=1.0, in1=n2[:], op0=ALU.add, op1=ALU.mult),
                 reads=[r_mod, r_ab], writes=[S.res()])
            S.barrier()
        SH1 = modT[:, 0:16, :]
        G1 = modT[:, 32:48, :]
        SH2 = modT[:, 48:64, :]
        G2 = modT[:, 80:96, :]

        for grp in range(2):
            with ExitStack() as gst:
                actA = sb(gst, "actA", [128, KC, 2304], BF16)
                r_act = S.res("actA")
                if grp == 0:
                    tiles = [(t, 512, t, 0) for t in range(0, 2048, 512)] + [(SEQ, 256, 2048, 1)]
                else:
                    tiles = [(t, 512, t - 2048, 0) for t in range(2048, 4096, 512)]
                with ExitStack() as pst:
                    norm_phase(pst, x_cur, actA, r_act, tiles, A1, SH1)
                    S.barrier()
                r_act = S.res("actA")
                if grp == 0:
                    mt = [(t, 512, t, 0) for t in range(0, 2048, 512)] + [(2048, 256, SEQ, 1)]
                else:
                    mt = [(t, 512, t + 2048, 0) for t in range(0, 2048, 512)]
                with ExitStack() as pst:
                    wbr = Ring(pst, "p_wb", [128, KC, 256], BF16, 3)
                    stg = Ring(pst, "p_stg", [128, 512], BF16, 4)
                    fsb = Ring(pst, "p_fsb", [128, 512], BF16, 2)
                    xst = Ring(pst, "p_xst", [128, 2, 256], BF16, 3)
                    cd = sb(pst, "p_cd", [128, 256], BF16)
                    r_cd = S.res("cd")
                    b.dma("sp", cd[:], cdsd, cd, writes=[r_cd])
                    fcs = list(range(0, 20)) + list(range(28, 32)) + list(range(36, 84))
                    mt_halo = [(0, 128, 2048, 0), (1920, 128, 3968, 0)]
                    blocks = []
                    for fc in fcs:
                        if blocks and blocks[-1][0] // 2 == fc // 2 and len(blocks[-1]) < 2:
                            blocks[-1].append(fc)
                        else:
                            blocks.append([fc])

                    def load_blk(blk):
                        wt, r_w = wbr.next()
                        c0 = blk[0] * 128
                        nn = len(blk) * 128
                        b.dma("pool", wt[:, :, 0:nn], w_in[l][:, c0:c0 + nn].rearrange("(k p) n -> p k n", p=128), wt, writes=[r_w])
                        return wt, r_w
                    nxt = load_blk(blocks[0])
                    for bi, blk in enumerate(blocks):
                        wt, r_w = nxt
                        if bi + 1 < len(blocks):
                            nxt = load_blk(blocks[bi + 1])
                        for fi, fc in enumerate(blk):
                            fk = fc < 4 or 12 <= fc < 20
                            for (a0, n, g0, isctx) in (mt if (fk or not (last and grp == 1)) else mt_halo):
                                if last and isctx and not (12 <= fc < 20):
                                    continue
                                pt, r_p = PS.next()
                                for kc in range(KC):
                                    S.op("pe", lambda e, pt=pt, wt=wt, fi=fi, kc=kc, a0=a0, n=n: e.matmul(
                                        pt[:, 0:n], lhsT=wt[:, kc, fi * 128:(fi + 1) * 128], rhs=actA[:, kc, a0:a0 + n],
                                        start=(kc == 0), stop=(kc == KC - 1)),
                                        reads=[r_w, r_act], writes=[r_p] if kc == 0 else (), pw=[r_p] if kc else ())
                                if fc < 4:
                                    ft, r_f = fsb.next()
                                    S.op("act", lambda e, ft=ft, pt=pt, n=n: e.activation(out=ft[:, 0:n], in_=pt[:, 0:n], func=AF.Copy),
                                         reads=[r_p], writes=[r_f])
                                    nsub = n // 128
                                    for s0 in range(0, nsub, 2):
                                        k2 = min(2, nsub - s0)
                                        p2, r_p2 = PS.next()
                                        for a in range(k2):
                                            S.op("pe", lambda e, p2=p2, ft=ft, a=a, s0=s0: e.matmul(
                                                p2[:, a * 256:(a + 1) * 256], lhsT=ft[:, (s0 + a) * 128:(s0 + a + 1) * 128], rhs=cd[:], start=True, stop=True),
                                                reads=[r_f, r_cd], writes=[r_p2] if a == 0 else (), pw=[r_p2] if a else ())
                                        xs, r_xs = xst.next()
                                        S.op("dve", lambda e, xs=xs, p2=p2, k2=k2: e.tensor_copy(
                                            out=xs[:, 0:k2, :], in_=p2[:, 0:k2 * 256].rearrange("p (a c) -> p a c", a=k2)),
                                            reads=[r_p2], writes=[r_xs])
                                        t0 = g0 + s0 * 128
                                        b.dma("sp", xcs[t0:t0 + k2 * 128, fc, :].rearrange("(a p) c -> p a c", p=128), xs[:, 0:k2, :], xs, reads=[r_xs])
                                    continue
                                so, r_so = stg.next()
                                if fc < 12:
                                    S.op("act", lambda e, so=so, pt=pt, n=n: e.activation(out=so[:, 0:n], in_=pt[:, 0:n], func=AF.Copy, scale=128 ** -0.5),
                                         reads=[r_p], writes=[r_so])
                                    dstap = qT[fc - 4][:, g0:g0 + n]
                                elif fc < 20:
                                    S.op("dve", lambda e, so=so, pt=pt, n=n: e.tensor_copy(out=so[:, 0:n], in_=pt[:, 0:n]), reads=[r_p], writes=[r_so])
                                    dstap = kT[fc - 12][:, g0:g0 + n]
                                elif fc < 32:
                                    S.op("act", lambda e, so=so, pt=pt, n=n: e.activation(out=so[:, 0:n], in_=pt[:, 0:n], func=AF.Gelu), reads=[r_p], writes=[r_so])
                                    dstap = uT[fc - 28][:, g0:g0 + n]
                                else:
                                    S.op("act", lambda e, so=so, pt=pt, n=n: e.activation(out=so[:, 0:n], in_=pt[:, 0:n], func=AF.Sigmoid), reads=[r_p], writes=[r_so])
                                    dstap = gT[fc - 36][:, g0:g0 + n]
                                b.dma("sp", dstap, so[:, 0:n], so, reads=[r_so])
                    wvr = Ring(pst, "p_wv", [128, KC, 512], BF16, 2)
                    vst = Ring(pst, "p_vst", [128, 512], BF16, 3)
                    zg = Ring(pst, "p_zg", [128, 4, 128], F32, 2)
                    zc = Ring(pst, "p_zc", [128, 4, 128], F32, 2)
                    zq = Ring(pst, "p_zq", [128, 4, 128], F32, 2)
                    st4 = Ring(pst, "p_st4", [128, 8], F32, 4)
                    vh = Ring(pst, "p_vh", [128, 4, 128], BF16, 2)
                    spo = Ring(pst, "p_spo", [128, 4, 128], BF16, 2)
                    wst = sb(pst, "p_wst", [128, 4, 128], BF16)
                    bst = sb(pst, "p_bst", [128, 4, 128], F32)
                    gnt = sb(pst, "p_gnt", [128, 4], F32)
                    r_gc = S.res("gconst")
                    b.dma("pool", wst[:], wsT[l], wst, writes=[r_gc])
                    b.dma("sp", bst[:], bsb[l], bst, pw=[r_gc])
                    b.dma("sp", gnt[:], gnw[l], gnt, pw=[r_gc])
                    ntok = 2304 if grp == 0 else 2048
                    for sec, c0 in (("v0", 2560), ("v1", 3072), ("zv", 4096)):
                        wt, r_w = wvr.next()
                        b.dma("pool", wt[:], w_in[l][:, c0:c0 + 512].rearrange("(k p) n -> p k n", p=128), wt, writes=[r_w])
                        for tt in range(ntok // 128):
                            a0 = tt * 128
                            isctx = (grp == 0 and a0 >= 2048)
                            g0 = (SEQ + a0 - 2048) if isctx else (a0 + grp * 2048)
                            if sec == "zv" and last and (isctx or (grp == 1 and tt not in (0, 15))):
                                continue
                            pt, r_p = PS.next()
                            for kc in range(KC):
                                S.op("pe", lambda e, pt=pt, wt=wt, kc=kc, a0=a0: e.matmul(
                                    pt[:], lhsT=actA[:, kc, a0:a0 + 128], rhs=wt[:, kc, :], start=(kc == 0), stop=(kc == KC - 1)),
                                    reads=[r_w, r_act], writes=[r_p] if kc == 0 else (), pw=[r_p] if kc else ())
                            if sec != "zv":
                                so, r_so = vst.next()
                                S.op("dve" if tt % 2 else "act",
                                     (lambda e, so=so, pt=pt: e.tensor_copy(out=so[:], in_=pt[:])) if tt % 2 else
                                     (lambda e, so=so, pt=pt: e.activation(out=so[:], in_=pt[:], func=AF.Copy)),
                                     reads=[r_p], writes=[r_so])
                                vc0 = 0 if sec == "v0" else 512
                                b.dma("sp", Vd[g0:g0 + 128, vc0:vc0 + 512], so[:], so, reads=[r_so])
                                continue
                            z_, r_z = zg.next()
                            S.op("act", lambda e, z_=z_, pt=pt: e.activation(out=z_[:].rearrange("p g d -> p (g d)"), in_=pt[:], func=AF.Gelu),
                                 reads=[r_p], writes=[r_z])
                            s4, r_s4 = st4.next()
                            S.op("dve", lambda e, s4=s4, z_=z_: e.tensor_reduce(out=s4[:, 0:4], in_=z_[:], axis=AX.X, op=ALU.add),
                                 reads=[r_z], writes=[r_s4])
                            S.op("dve", lambda e, s4=s4: e.tensor_scalar(out=s4[:, 0:4], in0=s4[:, 0:4], scalar1=1.0 / 128, scalar2=0.0, op0=ALU.mult, op1=ALU.add),
                                 reads=[r_s4], writes=[r_s4])
                            c_, r_c = zc.next()
                            for g in range(4):
                                S.op("dve", lambda e, c_=c_, z_=z_, s4=s4, g=g: e.tensor_scalar(
                                    out=c_[:, g, :], in0=z_[:, g, :], scalar1=s4[:, g:g + 1], scalar2=0.0, op0=ALU.subtract, op1=ALU.add),
                                    reads=[r_z, r_s4], writes=[r_c] if g == 0 else (), pw=[r_c] if g else ())
                            q_, r_q = zq.next()
                            S.op("pool", lambda e, q_=q_, c_=c_: e.tensor_tensor(out=q_[:], in0=c_[:], in1=c_[:], op=ALU.mult), reads=[r_c], writes=[r_q])
                            S.op("dve", lambda e, s4=s4, q_=q_: e.tensor_reduce(out=s4[:, 4:8], in_=q_[:], axis=AX.X, op=ALU.add),
                                 reads=[r_q], writes=[r_s4])
                            S.op("dve", lambda e, s4=s4: e.tensor_scalar(out=s4[:, 4:8], in0=s4[:, 4:8], scalar1=1.0 / 128, scalar2=EPS, op0=ALU.mult, op1=ALU.add),
                                 reads=[r_s4], writes=[r_s4])
                            S.op("act", lambda e, s4=s4: e.activation(out=s4[:, 4:8], in_=s4[:, 4:8], func=AF.Sqrt), reads=[r_s4], writes=[r_s4])
                            S.op("dve", lambda e, s4=s4: e.reciprocal(out=s4[:, 4:8], in_=s4[:, 4:8]), reads=[r_s4], writes=[r_s4])
                            v_, r_v = vh.next()
                            for g in range(4):
                                S.op("dve", lambda e, v_=v_, c_=c_, s4=s4, g=g: e.tensor_scalar(
                                    out=v_[:, g, :], in0=c_[:, g, :], scalar1=s4[:, 4 + g:5 + g], scalar2=0.0, op0=ALU.mult, op1=ALU.add),
                                    reads=[r_c, r_s4], writes=[r_v] if g == 0 else (), pw=[r_v] if g else ())
                            p3, r_p3 = PS.next()
                            for g in range(4):
                                S.op("pe", lambda e, p3=p3, v_=v_, g=g: e.matmul(p3[:, g * 128:(g + 1) * 128], lhsT=v_[:, g, :], rhs=wst[:, g, :], start=True, stop=True),
                                     reads=[r_v, r_gc], writes=[r_p3] if g == 0 else (), pw=[r_p3] if g else ())
                            o_, r_o = spo.next()
                            for g in range(4):
                                S.op("dve", lambda e, o_=o_, p3=p3, g=g: e.scalar_tensor_tensor(
                                    out=o_[:, g, :], in0=p3[:, g * 128:(g + 1) * 128], scalar=gnt[:, g:g + 1], in1=bst[:, g, :], op0=ALU.mult, op1=ALU.add),
                                    reads=[r_p3, r_gc], writes=[r_o] if g == 0 else (), pw=[r_o] if g else ())
                            b.dma("sp", spT[:, :, g0:g0 + 128].rearrange("g p t -> p g t"), o_[:], o_, reads=[r_o])
                    S.barrier()

        with ExitStack() as pst:
            xa = sb(pst, "f_xa", [128, 32, 1024], BF16)
            r_xa = S.res("xa")
            for q4 in range(4):
                b.dma("sp", xa[:, q4 * 8:(q4 + 1) * 8, :],
                      xcs[q4 * 1024:(q4 + 1) * 1024].rearrange("(c p) g w -> p c (g w)", p=128), xa,
                      writes=[r_xa] if q4 == 0 else (), pw=[r_xa] if q4 else ())
            tcr = Ring(pst, "f_tc", [128, 8, 512], BF16, 3)
            tsr = Ring(pst, "f_ts", [128, 8, 512], BF16, 3)
            yst = Ring(pst, "f_y", [128, 512], BF16, 4)
            ftiles = [(t, 512) for t in range(0, SEQ if not last else 2048, 512)]
            if last:
                ftiles += [(2048, 128), (3968, 128)]
            for jt, (n0, nw) in enumerate(ftiles):
                banks = [(PSA if jt % 2 == 0 else PS).next() for _ in range(4)]
                for pc in range(4):
                    tc_, r_tc = tcr.next()
                    ts_, r_ts = tsr.next()
                    b.dma("sp", tc_[:, :, 0:nw], cn[pc * 1024:(pc + 1) * 1024, n0:n0 + nw].rearrange("(c p) n -> p c n", p=128), tc_, writes=[r_tc])
                    b.dma("sp", ts_[:, :, 0:nw], sn[pc * 1024:(pc + 1) * 1024, n0:n0 + nw].rearrange("(c p) n -> p c n", p=128), ts_, writes=[r_ts])
                    for g in range(4):
                        pt, r_p = banks[g]
                        for c8 in range(8):
                            ch = pc * 8 + c8
                            for cs, (tb, r_tb) in enumerate(((tc_, r_tc), (ts_, r_ts))):
                                first = (ch == 0 and cs == 0)
                                lastm = (ch == 31 and cs == 1)
                                S.op("pe", lambda e, pt=pt, ch=ch, g=g, cs=cs, tb=tb, c8=c8, first=first, lastm=lastm, nw=nw: e.matmul(
                                    pt[:, 0:nw], lhsT=xa[:, ch, g * 256 + cs * 128:g * 256 + cs * 128 + 128], rhs=tb[:, c8, 0:nw], start=first, stop=lastm),
                                    reads=[r_xa, r_tb], writes=[r_p] if first else (), pw=() if first else [r_p])
                for g in range(4):
                    pt, r_p = banks[g]
                    yo, r_y = yst.next()
                    S.op("act" if g % 2 else "dve",
                         (lambda e, yo=yo, pt=pt, nw=nw: e.activation(out=yo[:, 0:nw], in_=pt[:, 0:nw], func=AF.Copy)) if g % 2 else
                         (lambda e, yo=yo, pt=pt, nw=nw: e.tensor_copy(out=yo[:, 0:nw], in_=pt[:, 0:nw])), reads=[r_p], writes=[r_y])
                    b.dma("sp", yT[g][:, n0:n0 + nw], yo[:, 0:nw], yo, reads=[r_y])
            if not last:
                xc_ = sb(pst, "f_xc", [128, 2, 1024], BF16)
                tcc = sb(pst, "f_tcc", [128, 2, 256], BF16)
                tsc = sb(pst, "f_tsc", [128, 2, 256], BF16)
                r_xc = S.res("xc")
                b.dma("sp", xc_[:], xcs[SEQ:TT].rearrange("(c p) g w -> p c (g w)", p=128), xc_, writes=[r_xc])
                b.dma("sp", tcc[:], cnc.rearrange("(c p) n -> p c n", p=128), tcc, pw=[r_xc])
                b.dma("sp", tsc[:], snc.rearrange("(c p) n -> p c n", p=128), tsc, pw=[r_xc])
                for g in range(4):
                    pt, r_p = PS.next()
                    i = 0
                    for ch in range(2):
                        for cs, tb in enumerate((tcc, tsc)):
                            S.op("pe", lambda e, pt=pt, ch=ch, g=g, cs=cs, tb=tb, i=i: e.matmul(
                                pt[:, 0:256], lhsT=xc_[:, ch, g * 256 + cs * 128:g * 256 + cs * 128 + 128], rhs=tb[:, ch, :], start=(i == 0), stop=(i == 3)),
                                reads=[r_xc], writes=[r_p] if i == 0 else (), pw=() if i == 0 else [r_p])
                            i += 1
                    yo, r_y = yst.next()
                    S.op("dve", lambda e, yo=yo, pt=pt: e.tensor_copy(out=yo[:, 0:256], in_=pt[:, 0:256]), reads=[r_p], writes=[r_y])
                    b.dma("sp", yT[g][:, SEQ:TT], yo[:, 0:256], yo, reads=[r_y])
            S.barrier()

        with ExitStack() as pst:
            vall = sb(pst, "a_v", [128, 34, 1024], BF16)
            mkr = Ring(pst, "a_mk", [128, 8, 512], BF16, 3)
            r_v = S.res("vall")
            for q4 in range(4):
                b.dma("sp", vall[:, q4 * 8:(q4 + 1) * 8, :], Vd[q4 * 1024:(q4 + 1) * 1024].rearrange("(c p) d -> p c d", p=128), vall,
                      writes=[r_v] if q4 == 0 else (), pw=[r_v] if q4 else ())
            b.dma("sp", vall[:, 32:34, :], Vd[SEQ:TT].rearrange("(c p) d -> p c d", p=128), vall, pw=[r_v])
            qr_ = Ring(pst, "a_q", [128, TT], BF16, 2)
            kr_ = Ring(pst, "a_k", [128, TT], BF16, 2)
            rp_ = Ring(pst, "a_rp", [128, 8, 512], BF16, 2)
            pTr = Ring(pst, "a_p", [128, 512], BF16, 4)
            rdr = Ring(pst, "a_rd", [128, 512], F32, 2)
            aor = Ring(pst, "a_o", [128, 512], BF16, 3)

            rmr = Ring(pst, "a_rm", [128, 8, 512], BF16, 2)

            def load_head(h):
                q_, r_q = qr_.next(); k_, r_k = kr_.next(); rp, r_rp = rp_.next()
                b.dma("sp", q_[:], qT[h], q_, writes=[r_q])
                b.dma("sp", k_[:], kT[h], k_, writes=[r_k])
                b.dma("sp", rp[:], rpbt[l, h], rp, writes=[r_rp])
                return (q_, r_q, k_, r_k, rp, r_rp)
            qblocks = [(qb * 512, 512, qb, 0) for qb in range(4 if last else 8)]
            if not last:
                qblocks.append((SEQ, 256, -1, 0))
            else:
                qblocks += [(2048, 128, 4, 0), (3968, 128, 7, 384)]
            items = [(h,) + blk for h in range(8) for blk in qblocks]
            heads = {0: load_head(0)}
            masks = {}
            rms = {}

            def load_mask(ii):
                if ii >= len(items) or items[ii][3] < 0:
                    return
                mk, r_mk = mkr.next()
                b.dma("sp", mk[:], maskt[:, items[ii][3]], mk, writes=[r_mk])
                masks[ii] = (mk, r_mk)

            def prep_rm(ii):
                if ii >= len(items) or items[ii][3] < 0:
                    return
                h_, q0_, nq_, qb_, qoff_ = items[ii]
                mk, r_mk = masks.pop(ii)
                rp, r_rp = heads[h_][4], heads[h_][5]
                rm, r_rm = rmr.next()
                S.op("pool", lambda e, rm=rm, rp=rp, mk=mk, nq_=nq_, qoff_=qoff_: e.tensor_tensor(
                    out=rm[:, :, 0:nq_], in0=rp[:, :, qoff_:qoff_ + nq_], in1=mk[:, :, qoff_:qoff_ + nq_], op=ALU.add),
                    reads=[r_rp, r_mk], writes=[r_rm])
                rms[ii] = (rm, r_rm)
            load_mask(0)
            load_mask(1)
            prep_rm(0)
            pending = []
            for ii, (h, q0, nq, qb, qoff) in enumerate(items):
                if ii % len(qblocks) == 0 and h + 1 < 8:
                    heads[h + 1] = load_head(h + 1)
                q_, r_q, k_, r_k, rp, r_rp = heads[h]
                load_mask(ii + 2)
                prep_rm(ii + 1)
                rm, r_rm = rms.pop(ii, (None, None))
                chunks = [(SEQ, 32, -1), (SEQ + 128, 33, -1)]
                if qb >= 0:
                    for j in range(8):
                        kc_idx = (4 * qb - 2 + j) % 32
                        chunks.append((kc_idx * 128, kc_idx, j))
                acc, r_acc = PSA.next()
                den, r_den = PSA.next()

                def s_mm(ci, k_=k_, q_=q_, rm=rm, r_rm=r_rm, q0=q0, nq=nq, r_q=r_q, r_k=r_k, chunks=chunks):
                    kcol, vidx, j = chunks[ci]
                    sp_, r_sp = PS.next()
                    S.op("pe", lambda e, sp_=sp_, kcol=kcol, j=j: e.matmul(sp_[:, 0:nq], lhsT=k_[:, kcol:kcol + 128], rhs=q_[:, q0:q0 + nq], start=True, stop=(j < 0)),
                         reads=[r_q, r_k], writes=[r_sp])
                    if j >= 0:
                        S.op("pe", lambda e, sp_=sp_, j=j: e.matmul(sp_[:, 0:nq], lhsT=ident[:], rhs=rm[:, j, 0:nq], start=False, stop=True), reads=[r_rm], pw=[r_sp])
                    return sp_, r_sp
                cur = s_mm(0)
                for ci in range(len(chunks)):
                    sp_, r_sp = cur
                    if ci + 1 < len(chunks):
                        cur = s_mm(ci + 1)
                    if ci == 0 and pending:
                        pending.pop()()
                    kcol, vidx, j = chunks[ci]
                    p_, r_pp = pTr.next()
                    S.op("act", lambda e, p_=p_, sp_=sp_, nq=nq: e.activation(out=p_[:, 0:nq], in_=sp_[:, 0:nq], func=AF.Exp), reads=[r_sp], writes=[r_pp])
                    first = (ci == 0); lastc = (ci == len(chunks) - 1)
                    S.op("pe", lambda e, p_=p_, vidx=vidx, first=first, lastc=lastc, acc=acc, h=h, nq=nq: e.matmul(
                        acc[:, 0:nq], lhsT=vall[:, vidx, h * 128:(h + 1) * 128], rhs=p_[:, 0:nq], start=first, stop=lastc),
                        reads=[r_pp, r_v], writes=[r_acc] if first else (), pw=() if first else [r_acc])
                    S.op("pe", lambda e, p_=p_, first=first, lastc=lastc, den=den, nq=nq: e.matmul(
                        den[:, 0:nq], lhsT=ones_b[:], rhs=p_[:, 0:nq], start=first, stop=lastc),
                        reads=[r_pp], writes=[r_den] if first else (), pw=() if first else [r_den])

                def fin(den=den, r_den=r_den, acc=acc, r_acc=r_acc, nq=nq, q0=q0, h=h):
                    rd, r_rd = rdr.next()
                    S.op("dve", lambda e, rd=rd: e.reciprocal(out=rd[:, 0:nq], in_=den[:, 0:nq]), reads=[r_den], writes=[r_rd])
                    ao, r_ao = aor.next()
                    S.op("dve", lambda e, ao=ao, rd=rd: e.tensor_tensor(out=ao[:, 0:nq], in0=acc[:, 0:nq], in1=rd[:, 0:nq], op=ALU.mult),
                         reads=[r_acc, r_rd], writes=[r_ao])
                    b.dma("sp", attT[h][:, q0:q0 + nq], ao[:, 0:nq], ao, reads=[r_ao])
                pending.append(fin)
            while pending:
                pending.pop()()
            S.barrier()

        mgroups = [[(0, 384, 0), (384, 384, 0), (768, 384, 0)],
                   [(1152, 448, 0), (1600, 448, 0)],
                   [(2048, 384, 0), (2432, 384, 0), (2816, 384, 0)],
                   [(3200, 448, 0), (3648, 448, 0)]]
        if not last:
            mgroups[1].append((SEQ, 256, 1))
        else:
            mgroups = mgroups[:2] + [[(2048, 128, 0), (3968, 128, 0)]]
        for mg in mgroups:
            with ExitStack() as pst:
                cat = sb(pst, "m_cat", [128, KC, 1152], BF16)
                mT = sb(pst, "m_mT", [128, KC, 1152], BF16)
                ur = Ring(pst, "m_u", [128, 4, 512], BF16, 2)
                sr = Ring(pst, "m_s", [128, 4, 512], BF16, 2)
                r_cat = S.res("cat")
                loc = []
                a0 = 0
                for (g0, n, sel) in mg:
                    loc.append((a0, n, g0, sel))
                    b.dma("sp", cat[:, 0:4, a0:a0 + n], yT[:, :, g0:g0 + n].rearrange("g p t -> p g t"), cat, pw=[r_cat])
                    b.dma("sp", cat[:, 4:12, a0:a0 + n], attT[:, :, g0:g0 + n].rearrange("g p t -> p g t"), cat, pw=[r_cat])
                    u_, r_u = ur.next(); s_, r_s = sr.next()
                    b.dma("sp", u_[:, :, 0:n], uT[:, :, g0:g0 + n].rearrange("g p t -> p g t"), u_, writes=[r_u])
                    b.dma("sp", s_[:, :, 0:n], spT[:, :, g0:g0 + n].rearrange("g p t -> p g t"), s_, writes=[r_s])
                    S.op("pool", lambda e, u_=u_, s_=s_, a0=a0, n=n: e.tensor_tensor(out=cat[:, 12:16, a0:a0 + n], in0=u_[:, :, 0:n], in1=s_[:, :, 0:n], op=ALU.mult),
                         reads=[r_u, r_s], pw=[r_cat])
                    a0 += n
                wbr = Ring(pst, "m_wb", [128, KC, 256], BF16, 3)
                gtr = Ring(pst, "m_gt", [128, 3, 512], BF16, 3)
                t1r = Ring(pst, "m_t1", [128, 512], F32, 3)
                t2r = Ring(pst, "m_t2", [128, 512], F32, 3)
                xor_ = Ring(pst, "m_xo", [128, 512], F32, 3)
                xnr = Ring(pst, "m_xn", [128, 512], F32, 3)
                r_mT = S.res("mT")

                def load_w1(fb):
                    wt, r_w = wbr.next()
                    c0 = fb * 256
                    b.dma("pool", wt[:, 0:4, :], w_f_out[l][:, c0:c0 + 256].rearrange("(k p) n -> p k n", p=128), wt, writes=[r_w])
                    b.dma("pool", wt[:, 4:12, :], w_na_out[l][:, c0:c0 + 256].rearrange("(k p) n -> p k n", p=128), wt, pw=[r_w])
                    b.dma("pool", wt[:, 12:16, :], w_c_out[l][:, c0:c0 + 256].rearrange("(k p) n -> p k n", p=128), wt, pw=[r_w])
                    return wt, r_w
                nxt = load_w1(0)
                for fb in range(8):
                    wt, r_w = nxt
                    if fb + 1 < 8:
                        nxt = load_w1(fb + 1)
                    for fi in range(2):
                        fc = fb * 2 + fi
                        for (a0, n, g0, sel) in loc:
                            gt, r_g = gtr.next()
                            b.dma("sp", gt[:, :, 0:n], gT.rearrange("(s f) p t -> f p s t", s=3)[fc][:, :, g0:g0 + n], gt, writes=[r_g])
                            b.mu = getattr(b, "mu", 0) + 1
                            ring_ = PS if b.mu % 2 else PSA
                            pf, r_pf = ring_.next(); pn, r_pn = ring_.next(); pc_, r_pc = ring_.next()
                            for (pt, r_p, k0, k1) in ((pf, r_pf, 0, 4), (pn, r_pn, 4, 12), (pc_, r_pc, 12, 16)):
                                for kc in range(k0, k1):
                                    S.op("pe", lambda e, pt=pt, wt=wt, kc=kc, fi=fi, a0=a0, n=n, k0=k0, k1=k1: e.matmul(
                                        pt[:, 0:n], lhsT=wt[:, kc, fi * 128:(fi + 1) * 128], rhs=cat[:, kc, a0:a0 + n], start=(kc == k0), stop=(kc == k1 - 1)),
                                        reads=[r_w, r_cat], writes=[r_p] if kc == k0 else (), pw=() if kc == k0 else [r_p])
                            t1, r_t1 = t1r.next(); t2, r_t2 = t2r.next()
                            S.op("dve", lambda e, t1=t1, pf=pf, gt=gt, n=n: e.tensor_tensor(out=t1[:, 0:n], in0=pf[:, 0:n], in1=gt[:, 0, 0:n], op=ALU.mult),
                                 reads=[r_pf, r_g], writes=[r_t1])
                            S.op("dve", lambda e, t2=t2, pn=pn, gt=gt, n=n: e.tensor_tensor(out=t2[:, 0:n], in0=pn[:, 0:n], in1=gt[:, 1, 0:n], op=ALU.mult),
                                 reads=[r_pn, r_g], writes=[r_t2])
                            S.op("pool", lambda e, t1=t1, t2=t2, n=n: e.tensor_tensor(out=t1[:, 0:n], in0=t1[:, 0:n], in1=t2[:, 0:n], op=ALU.add),
                                 reads=[r_t2, r_t1], writes=[r_t1])
                            t3, r_t3 = t2r.next()
                            S.op("dve", lambda e, t3=t3, pc_=pc_, gt=gt, n=n: e.tensor_tensor(out=t3[:, 0:n], in0=pc_[:, 0:n], in1=gt[:, 2, 0:n], op=ALU.mult),
                                 reads=[r_pc, r_g], writes=[r_t3])
                            S.op("pool", lambda e, t1=t1, t3=t3, fc=fc, a0=a0, n=n: e.tensor_tensor(out=mT[:, fc, a0:a0 + n], in0=t1[:, 0:n], in1=t3[:, 0:n], op=ALU.add),
                                 reads=[r_t1, r_t3], pw=[r_mT])

                def load_w2(fb):
                    wt, r_w = wbr.next()
                    b.dma("pool", wt[:], w_o[l][:, fb * 256:(fb + 1) * 256].rearrange("(k p) n -> p k n", p=128), wt, writes=[r_w])
                    return wt, r_w
                nxt = load_w2(0)
                for fb in range(8):
                    wt, r_w = nxt
                    if fb + 1 < 8:
                        nxt = load_w2(fb + 1)
                    for fi in range(2):
                        fc = fb * 2 + fi
                        for (a0, n, g0, sel) in loc:
                            xo, r_xo = xor_.next()
                            b.dma("act", xo[:, 0:n], x_cur[fc][:, g0:g0 + n], xo, writes=[r_xo])
                            pt, r_p = PS.next()
                            for kc in range(KC):
                                S.op("pe", lambda e, pt=pt, wt=wt, kc=kc, fi=fi, a0=a0, n=n: e.matmul(
                                    pt[:, 0:n], lhsT=wt[:, kc, fi * 128:(fi + 1) * 128], rhs=mT[:, kc, a0:a0 + n], start=(kc == 0), stop=(kc == KC - 1)),
                                    reads=[r_w, r_mT], writes=[r_p] if kc == 0 else (), pw=[r_p] if kc else ())
                            xn, r_xn = xnr.next()
                            S.op("dve", lambda e, xn=xn, pt=pt, xo=xo, fc=fc, sel=sel, n=n: e.scalar_tensor_tensor(
                                out=xn[:, 0:n], in0=pt[:, 0:n], scalar=G1[:, fc, sel:sel + 1], in1=xo[:, 0:n], op0=ALU.mult, op1=ALU.add),
                                reads=[r_p, r_xo], writes=[r_xn])
                            b.dma("sp", xT_mid[fc][:, g0:g0 + n], xn[:, 0:n], xn, reads=[r_xn])
                S.barrier()

        for grp in range(1 if last else 2):
            srcL, srcR = (SEQ - 1, 2048) if grp == 0 else (2047, 0)
            fL, fR = (0, 1) if grp == 0 else (1, 0)
            ntiles = [(srcL, 1, 0, 0)] + [(grp * 2048 + t, 512, 1 + t, 0) for t in range(0, 2048, 512)] + [(srcR, 1, 2049, 0)]
            if grp == 0 and not last:
                ntiles.append((SEQ, 256, 2050, 1))
            ncols = 2306 if (grp == 0 and not last) else 2050
            with ExitStack() as gst:
                actA = sb(gst, "actB", [128, KC, 2306], BF16)
                r_act = S.res("actB")
                with ExitStack() as pst:
                    norm_phase(pst, xT_mid, actA, r_act, ntiles, A2, SH2)
                    S.barrier()
                r_act = S.res("actB")
                ntl = (ncols + 511) // 512
                bnd = [(ncols * i) // ntl for i in range(ntl + 1)]
                mtiles = [(bnd[i], bnd[i + 1] - bnd[i]) for i in range(ntl)]
                with ExitStack() as pst:
                    war = Ring(pst, "u_wa", [128, KC, 128], BF16, 3)
                    wgr = Ring(pst, "u_wg", [128, KC, 128], BF16, 3)
                    arr = Ring(pst, "u_ar", [128, 2308], F32, 2)
                    grr = Ring(pst, "u_gr", [128, 2308], F32, 2)
                    yr = Ring(pst, "u_y", [128, 2308], F32, 2)
                    hr = Ring(pst, "u_h", [128, 2304], BF16, 2)
                    cwt = sb(pst, "u_cw", [128, JF, 3], F32)
                    cbt = sb(pst, "u_cb", [128, JF], F32)
                    r_cc = S.res("cc")
                    b.dma("sp", cwt[:], cw[l], cwt, writes=[r_cc])
                    b.dma("sp", cbt[:], cb[l], cbt, pw=[r_cc])
                    for _i in range(arr.n):
                        ar_t, r_art = arr.next()
                        S.op("pool", lambda e, ar_t=ar_t: e.memset(ar_t[:], 0.0), writes=[r_art])

                    def load_w(j):
                        wa, r_wa = war.next(); wg, r_wg = wgr.next()
                        b.dma("pool", wa[:], ffn_up[l][:, j * 128:(j + 1) * 128].rearrange("(k p) n -> p k n", p=128), wa, writes=[r_wa])
                        b.dma("pool", wg[:], ffn_up[l][:, DFF + j * 128:DFF + (j + 1) * 128].rearrange("(k p) n -> p k n", p=128), wg, writes=[r_wg])
                        return wa, r_wa, wg, r_wg
                    nxt = load_w(0)
                    for j in range(JF):
                        wa, r_wa, wg, r_wg = nxt
                        if j + 1 < JF:
                            nxt = load_w(j + 1)
                        ar, r_ar = arr.next()
                        gr, r_gr = grr.next()
                        for (t0, n) in mtiles:
                            for which, (wt, r_w, dst, r_d) in enumerate(((wa, r_wa, ar, r_ar), (wg, r_wg, gr, r_gr))):
                                pt, r_p = PS.next()
                                for kc in range(KC):
                                    S.op("pe", lambda e, pt=pt, wt=wt, kc=kc, t0=t0, n=n: e.matmul(
                                        pt[:, 0:n], lhsT=wt[:, kc, :], rhs=actA[:, kc, t0:t0 + n], start=(kc == 0), stop=(kc == KC - 1)),
                                        reads=[r_w, r_act], writes=[r_p] if kc == 0 else (), pw=[r_p] if kc else ())
                                segs = []
                                lo, hi = t0, t0 + n
                                m_hi = min(hi, 2050)
                                if lo < m_hi:
                                    segs.append((lo - t0, m_hi - lo, lo))
                                if hi > 2050:
                                    c_lo = max(lo, 2050)
                                    segs.append((c_lo - t0, hi - c_lo, c_lo + 1))
                                for (p0, sn_, d0) in segs:
                                    if which == 0:
                                        S.op("act", lambda e, pt=pt, dst=dst, p0=p0, sn_=sn_, d0=d0: e.activation(out=dst[:, d0:d0 + sn_], in_=pt[:, p0:p0 + sn_], func=AF.Copy),
                                             reads=[r_p], pw=[r_d])
                                    else:
                                        S.op("dve", lambda e, pt=pt, dst=dst, p0=p0, sn_=sn_, d0=d0: e.tensor_copy(out=dst[:, d0:d0 + sn_], in_=pt[:, p0:p0 + sn_]),
                                             reads=[r_p], pw=[r_d])
                        S.op("dve", lambda e, ar=ar, fL=fL: e.tensor_scalar(out=ar[:, 0:1], in0=ar[:, 0:1], scalar1=flg[:, fL:fL + 1], scalar2=0.0, op0=ALU.mult, op1=ALU.add),
                             reads=[r_ar], writes=[r_ar])
                        S.op("dve", lambda e, ar=ar, fR=fR: e.tensor_scalar(out=ar[:, 2049:2050], in0=ar[:, 2049:2050], scalar1=flg[:, fR:fR + 1], scalar2=0.0, op0=ALU.mult, op1=ALU.add),
                             reads=[r_ar], writes=[r_ar])
                        hi_c = 2307 if (grp == 0 and not last) else 2049
                        y_, r_y = yr.next()
                        W = hi_c - 1
                        S.op("dve", lambda e, y_=y_, ar=ar, j=j, W=W: e.tensor_scalar(out=y_[:, 1:1 + W], in0=ar[:, 1:1 + W], scalar1=cwt[:, j, 1:2], scalar2=cbt[:, j:j + 1], op0=ALU.mult, op1=ALU.add),
                             reads=[r_ar, r_cc], writes=[r_y])
                        S.op("dve", lambda e, y_=y_, ar=ar, j=j, W=W: e.scalar_tensor_tensor(out=y_[:, 1:1 + W], in0=ar[:, 0:W], scalar=cwt[:, j, 0:1], in1=y_[:, 1:1 + W], op0=ALU.mult, op1=ALU.add),
                             reads=[r_ar, r_y], writes=[r_y])
                        S.op("dve", lambda e, y_=y_, ar=ar, j=j, W=W: e.scalar_tensor_tensor(out=y_[:, 1:1 + W], in0=ar[:, 2:2 + W], scalar=cwt[:, j, 2:3], in1=y_[:, 1:1 + W], op0=ALU.mult, op1=ALU.add),
                             reads=[r_ar, r_y], writes=[r_y])
                        S.op("act", lambda e, y_=y_, W=W: e.activation(out=y_[:, 1:1 + W], in_=y_[:, 1:1 + W], func=AF.Silu), reads=[r_y], writes=[r_y])
                        h_, r_h = hr.next()
                        S.op("pool", lambda e, h_=h_, y_=y_, gr=gr: e.tensor_tensor(out=h_[:, 0:2048], in0=y_[:, 1:2049], in1=gr[:, 1:2049], op=ALU.mult),
                             reads=[r_y, r_gr], writes=[r_h])
                        b.dma("sp", hmT[j][:, grp * 2048:(grp + 1) * 2048], h_[:, 0:2048], h_, reads=[r_h])
                        if grp == 0 and not last:
                            S.op("dve", lambda e, h_=h_, y_=y_, gr=gr: e.tensor_tensor(out=h_[:, 2048:2304], in0=y_[:, 2051:2307], in1=gr[:, 2051:2307], op=ALU.mult),
                                 reads=[r_y, r_gr], pw=[r_h])
                            b.dma("sp", hmT[j][:, SEQ:TT], h_[:, 2048:2304], h_, reads=[r_h])
                    S.barrier()

        x_next = xT_nxt if not last else xT_fin
        dgroups = [[(g * 1024, 512, 0), (g * 1024 + 512, 512, 0)] for g in range(4)]
        if not last:
            dgroups.append([(SEQ, 256, 1)])
        else:
            dgroups = dgroups[:2]
        for dg in dgroups:
            with ExitStack() as pst:
                hm = sb(pst, "d_hm", [128, JF, 1024], BF16)
                r_hm = S.res("hm")
                loc = []
                a0 = 0
                for (g0, n, sel) in dg:
                    loc.append((a0, n, g0, sel))
                    for q4 in range(4):
                        b.dma("sp", hm[:, q4 * 11:(q4 + 1) * 11, a0:a0 + n], hmT[q4 * 11:(q4 + 1) * 11, :, g0:g0 + n].rearrange("j p t -> p j t"), hm, pw=[r_hm])
                    a0 += n
                wdr = Ring(pst, "d_w", [128, JF, 256], BF16, 2)
                xor_ = Ring(pst, "d_xo", [128, 512], F32, 3)
                xnr = Ring(pst, "d_xn", [128, 512], F32, 3)

                def load_wd(fb):
                    wt, r_w = wdr.next()
                    for q4 in range(4):
                        b.dma("pool", wt[:, q4 * 11:(q4 + 1) * 11, :],
                              ffn_down[l][q4 * 11 * 128:(q4 + 1) * 11 * 128, fb * 256:(fb + 1) * 256].rearrange("(k p) n -> p k n", p=128), wt,
                              writes=[r_w] if q4 == 0 else (), pw=[r_w] if q4 else ())
                    return wt, r_w
                nxt = load_wd(0)
                for fb in range(8):
                    wt, r_w = nxt
                    if fb + 1 < 8:
                        nxt = load_wd(fb + 1)
                    for fi in range(2):
                        fc = fb * 2 + fi
                        for (a0, n, g0, sel) in loc:
                            xo, r_xo = xor_.next()
                            b.dma("act", xo[:, 0:n], xT_mid[fc][:, g0:g0 + n], xo, writes=[r_xo])
                            pt, r_p = PS.next()
                            for kc in range(JF):
                                S.op("pe", lambda e, pt=pt, wt=wt, kc=kc, fi=fi, a0=a0, n=n: e.matmul(
                                    pt[:, 0:n], lhsT=wt[:, kc, fi * 128:(fi + 1) * 128], rhs=hm[:, kc, a0:a0 + n], start=(kc == 0), stop=(kc == JF - 1)),
                                    reads=[r_w, r_hm], writes=[r_p] if kc == 0 else (), pw=[r_p] if kc else ())
                            xn, r_xn = xnr.next()
                            S.op("dve", lambda e, xn=xn, pt=pt, xo=xo, fc=fc, sel=sel, n=n: e.scalar_tensor_tensor(
                                out=xn[:, 0:n], in0=pt[:, 0:n], scalar=G2[:, fc, sel:sel + 1], in1=xo[:, 0:n], op0=ALU.mult, op1=ALU.add),
                                reads=[r_p, r_xo], writes=[r_xn])
                            b.dma("sp", x_next[fc][:, g0:g0 + n], xn[:, 0:n], xn, reads=[r_xn])
                S.barrier()
        x_cur = x_next

    with ExitStack() as pst:
        tiles = [(t, 512, t, 0) for t in range(0, 2048, 512)]
        norm_phase(pst, x_cur, None, None, tiles, fnw_sb, None, dst=outT)
        S.flush(final=True)


_CONST = {}


def _consts():
    if _CONST:
        return _CONST
    bf = ml_dtypes.bfloat16
    c = np.arange(128)
    ang = 2 * np.pi * np.outer(c, c) / 128.0
    _CONST["cdsd"] = np.concatenate([np.cos(ang), np.sin(ang)], 1).astype(np.float32) / np.sqrt(128.0)
    _CONST["cdsd"] = _CONST["cdsd"].astype(bf)
    n = np.arange(SEQ, dtype=np.int64)
    m = (np.outer(n, n) % SEQ).astype(np.float64) * (2 * np.pi / SEQ)
    _CONST["cn"] = (np.cos(m) / np.sqrt(SEQ)).astype(np.float32).astype(bf)
    _CONST["sn"] = (-np.sin(m) / np.sqrt(SEQ)).astype(np.float32).astype(bf)
    n = np.arange(NCTX, dtype=np.int64)
    m = (np.outer(n, n) % NCTX).astype(np.float64) * (2 * np.pi / NCTX)
    _CONST["cnc"] = (np.cos(m) / np.sqrt(NCTX)).astype(np.float32).astype(bf)
    _CONST["snc"] = (-np.sin(m) / np.sqrt(NCTX)).astype(np.float32).astype(bf)
    _CONST["ident"] = np.eye(128, dtype=np.float32).astype(bf)
    krr = np.arange(2)[:, None, None, None, None, None]
    kc = np.arange(64)[None, :, None, None, None, None]
    qb = np.arange(8)[None, None, :, None, None, None]
    j = np.arange(8)[None, None, None, :, None, None]
    qr = np.arange(8)[None, None, None, None, :, None]
    qc = np.arange(64)[None, None, None, None, None, :]
    R = 8 * qb + qr
    Kr = 8 * qb - 4 + 2 * j + krr
    rs = np.clip(R - 4, 0, 56)
    cs = np.clip(qc - 8, 0, 48)
    valid = (Kr >= 0) & (Kr <= 63) & (Kr >= rs) & (Kr <= rs + 7) & (kc >= cs) & (kc < cs + 16)
    mfull = np.where(valid, 0.0, NEG).astype(np.float32).reshape(128, 8, 8, 512).astype(bf)
    _CONST["maskt"] = [np.ascontiguousarray(np.roll(mfull, -4 * hf, axis=1)) for hf in range(2)]
    _CONST["cn_h"] = [_CONST["cn"], np.ascontiguousarray(np.roll(_CONST["cn"], (-2048, -2048), axis=(0, 1)))]
    _CONST["sn_h"] = [_CONST["sn"], np.ascontiguousarray(np.roll(_CONST["sn"], (-2048, -2048), axis=(0, 1)))]
    dr = np.clip(2 * j + krr - qr + 3, 0, 14)
    dc = np.clip(kc - qc + 15, 0, 30)
    dr, dc = np.broadcast_arrays(dr[:, :, 0], dc[:, :, 0])
    _CONST["dr"] = dr.reshape(128, 8, 512)
    _CONST["dc"] = dc.reshape(128, 8, 512)
    return _CONST


_NC_CACHE = {}


def _fm(v, rep=None):
    a = np.asarray(v, np.float32)
    a = a.reshape(a.shape[:-1] + (KC, 128)).swapaxes(-1, -2)
    if rep:
        a = np.repeat(a[..., None], rep, axis=-1)
    return np.ascontiguousarray(a)


def kernel(x, c, ctx, c_ctx, ada_w, ada_b, norm1_w, norm2_w, w_in, na_rpb, gmlp_norm_w, gmlp_ws, gmlp_bs,
           w_f_out, w_na_out, w_c_out, w_o, ffn_up, ffn_conv_w, ffn_conv_b, ffn_down, final_norm_w, _dbg=None, _nl=DEPTH):
    bf = ml_dtypes.bfloat16
    K = _consts()
    f32 = lambda a: np.ascontiguousarray(np.asarray(a, np.float32))
    x = f32(x); ctx = f32(ctx); c = f32(c); c_ctx = f32(c_ctx)
    na_rpb = f32(na_rpb)
    shared = {
        "ada_w": f32(ada_w),
        "ada_bT": np.ascontiguousarray(np.repeat(f32(ada_b).reshape(DEPTH, 96, 128).swapaxes(1, 2)[..., None], 2, axis=-1)),
        "nw1": _fm(norm1_w, 2), "nw2": _fm(norm2_w, 2), "fnw": _fm(final_norm_w),
        "w_in": f32(w_in),
        "rpbt": np.ascontiguousarray(na_rpb[:, :, K["dr"], K["dc"]]).astype(bf),
        "gnw": np.ascontiguousarray(f32(gmlp_norm_w).reshape(DEPTH, 4, 128).swapaxes(1, 2)),
        "wsT": np.ascontiguousarray(f32(gmlp_ws).transpose(0, 3, 1, 2)),
        "bsb": np.ascontiguousarray(np.broadcast_to(f32(gmlp_bs)[:, None, :, :], (DEPTH, 128, 4, 128))),
        "w_f_out": f32(w_f_out), "w_na_out": f32(w_na_out), "w_c_out": f32(w_c_out), "w_o": f32(w_o),
        "ffn_up": f32(ffn_up),
        "cw": np.ascontiguousarray(f32(ffn_conv_w).reshape(DEPTH, 3, JF, 128).transpose(0, 3, 2, 1)),
        "cb": np.ascontiguousarray(f32(ffn_conv_b).reshape(DEPTH, JF, 128).swapaxes(1, 2)),
        "ffn_down": f32(ffn_down),
        "cdsd": K["cdsd"], "cnc": K["cnc"], "snc": K["snc"], "ident": K["ident"],
    }
    key = (tuple(_dbg) if _dbg else (), _nl)
    if key not in _NC_CACHE:
        _NC_CACHE[key] = build(dbg=_dbg, n_layers=_nl)
    bb = _NC_CACHE[key]
    in_maps = []
    for core in range(8):
        bi, hf = core // 2, core % 2
        xl = np.roll(x[bi], -2048 * hf, axis=0)
        xt = np.concatenate([xl.T, ctx[bi].T], axis=1).reshape(KC, 128, TT)
        cT = np.stack([c[bi].reshape(KC, 128).T, c_ctx.reshape(KC, 128).T], axis=-1)
        m = dict(shared)
        m["xT"] = np.ascontiguousarray(xt)
        m["cT"] = np.ascontiguousarray(cT)
        m["cn"] = K["cn_h"][hf]; m["sn"] = K["sn_h"][hf]; m["maskt"] = K["maskt"][hf]
        fl = np.zeros((128, 2), np.float32); fl[:, 0] = hf; fl[:, 1] = 1 - hf
        m["flags"] = fl
        in_maps.append(m)
    res = run_bass_kernel_spmd(bb.nc, in_maps, core_ids=list(range(8)))
    out = np.empty((4, SEQ, D), np.float32)
    for core in range(8):
        bi, hf = core // 2, core % 2
        out[bi, hf * 2048:(hf + 1) * 2048] = res.results[core]["outT"].reshape(D, 2048).T
    if _dbg:
        return out, res
    return out
```
